# Optimizing a Trainium2 kernel written in Bass

```python
import jax, jax.numpy as jnp
from jax import lax
import numpy as np

D_MODEL = 2048
BATCH = 4
SEQ = 2048
DEPTH = 1
DEC_BATCH = 1
DEC_SEQ = 8192
PAST_LEN = 128

N_META = 16
GLA_HEADS = 4
GLA_DQK = D_MODEL // 2
GLA_DV_TOT = D_MODEL
GLA_DK = GLA_DQK // GLA_HEADS
GLA_DV = GLA_DV_TOT // GLA_HEADS
GLA_GATE_RANK = 16
GLA_TAU = 16.0
GLA_CHUNK = 64
CHUNK_PAD = GLA_CHUNK - N_META
DECAY_BIAS_MEAN = 2.0
CONV_CH = D_MODEL
CONV_W = 31
D_FF = 4 * D_MODEL
EPS = 1e-6

IN_SIZES = (GLA_DQK, GLA_DQK, GLA_DV_TOT, GLA_DV_TOT, GLA_GATE_RANK, GLA_GATE_RANK, 2 * CONV_CH, 2 * D_MODEL)
N_IN = int(sum(IN_SIZES))
IN_SPLITS = tuple(int(s) for s in np.cumsum(IN_SIZES)[:-1])

kernel_name = "hybrid_gla_conformer_encoder"


def _rmsnorm(x, g):
    xf = x.astype(jnp.float32)
    y = xf * lax.rsqrt(jnp.mean(xf * xf, axis=-1, keepdims=True) + EPS)
    return (y * g.astype(jnp.float32)).astype(x.dtype)


def _gla_chunked(q, k, v, log_a, inclusive):
    bsz, nh = q.shape[0], q.shape[1]
    b = jnp.cumsum(log_a, axis=3)
    b_last = b[:, :, :, -1:, :]
    q_d = q * jnp.exp(b)
    k_d = k * jnp.exp(-b)
    k_end = k * jnp.exp(b_last - b)
    scores = jnp.einsum('bhnik,bhnjk->bhnij', q_d, k_d)
    mask = jnp.tril(jnp.ones((GLA_CHUNK, GLA_CHUNK), dtype=bool), 0 if inclusive else -1)
    scores = jnp.where(mask, scores, 0.0)
    o_intra = jnp.einsum('bhnij,bhnjv->bhniv', scores, v)
    chunk_decay = jnp.exp(b_last[:, :, :, 0, :])

    def step(state, inp):
        q_c, k_c, v_c, dec_c = inp
        o_c = jnp.einsum('bhik,bhkv->bhiv', q_c, state)
        state = dec_c[..., None] * state + jnp.einsum('bhjk,bhjv->bhkv', k_c, v_c)
        return state, o_c

    init = jnp.zeros((bsz, nh, GLA_DK, GLA_DV), jnp.float32)
    xs = (jnp.moveaxis(q_d, 2, 0), jnp.moveaxis(k_end, 2, 0), jnp.moveaxis(v, 2, 0), jnp.moveaxis(chunk_decay, 2, 0))
    _, o_inter = lax.scan(step, init, xs)
    return o_intra + jnp.moveaxis(o_inter, 0, 2)


def _gla_branch(u_q, u_k, u_v, u_r, z_f, z_b, w_a2_f, b_a_f, w_a2_b, b_a_b, g_gla, w_gla_o):
    bsz, seq_len, _ = u_q.shape
    lp = seq_len + CHUNK_PAD
    n_chunks = lp // GLA_CHUNK
    f32 = jnp.float32
    la_f = jax.nn.log_sigmoid((z_f @ w_a2_f + b_a_f).astype(f32)) / GLA_TAU
    la_b = jax.nn.log_sigmoid((z_b @ w_a2_b + b_a_b).astype(f32)) / GLA_TAU
    pad = lambda t: jnp.pad(t, ((0, 0), (CHUNK_PAD, 0), (0, 0)))
    q = pad(u_q.astype(f32) * (GLA_DK ** -0.5))
    k = pad(u_k.astype(f32))
    v = pad(u_v.astype(f32))
    la_f = pad(la_f)
    la_b = pad(la_b)

    def to_chunks(t, d):
        return t.reshape(bsz, n_chunks, GLA_CHUNK, GLA_HEADS, d).transpose(0, 3, 1, 2, 4)

    def from_chunks(t):
        return t.transpose(0, 2, 3, 1, 4).reshape(bsz, lp, GLA_HEADS, GLA_DV)

    flip = lambda t: jnp.flip(t, axis=1)
    o_f = from_chunks(_gla_chunked(to_chunks(q, GLA_DK), to_chunks(k, GLA_DK), to_chunks(v, GLA_DV),
                                   to_chunks(la_f, GLA_DK), True))
    o_b = flip(from_chunks(_gla_chunked(to_chunks(flip(q), GLA_DK), to_chunks(flip(k), GLA_DK),
                                        to_chunks(flip(v), GLA_DV), to_chunks(flip(la_b), GLA_DK), False)))
    o = (o_f + o_b)[:, CHUNK_PAD:]
    o = o * lax.rsqrt(jnp.mean(o * o, axis=-1, keepdims=True) + EPS)
    o = o.reshape(bsz, seq_len, GLA_DV_TOT) * g_gla.astype(f32)
    o = o.astype(u_r.dtype) * jax.nn.silu(u_r)
    return o @ w_gla_o


def _conv_branch(p, w_dw, b_dw, ln_g, ln_b, w_conv_o, b_conv_o):
    a, gt = jnp.split(p, 2, axis=-1)
    g = a * jax.nn.sigmoid(gt)
    y = lax.conv_general_dilated(g, w_dw[:, None, :].astype(g.dtype), window_strides=(1,),
                                 padding=((CONV_W // 2, CONV_W // 2),),
                                 dimension_numbers=('NWC', 'WIO', 'NWC'),
                                 feature_group_count=CONV_CH) + b_dw
    yf = y.astype(jnp.float32)
    mu = jnp.mean(yf, axis=-1, keepdims=True)
    var = jnp.mean(jnp.square(yf - mu), axis=-1, keepdims=True)
    yf = (yf - mu) * lax.rsqrt(var + EPS) * ln_g.astype(jnp.float32) + ln_b.astype(jnp.float32)
    y = jax.nn.silu(yf).astype(p.dtype)
    return y @ w_conv_o + b_conv_o


def _layer(h, g_pre_mix, w_in, w_a2_f, b_a_f, w_a2_b, b_a_b, g_gla, w_gla_o,
           w_dw, b_dw, ln_g, ln_b, w_conv_o, b_conv_o, w_out, g_post_mix,
           g_pre_mlp, w_up, w_down, g_post_mlp):
    u = _rmsnorm(h, g_pre_mix)
    proj = u @ w_in
    u_q, u_k, u_v, u_r, z_f, z_b, p_glu, gate_logits = jnp.split(proj, IN_SPLITS, axis=-1)
    y_a = _gla_branch(u_q, u_k, u_v, u_r, z_f, z_b, w_a2_f, b_a_f, w_a2_b, b_a_b, g_gla, w_gla_o)
    y_b = _conv_branch(p_glu, w_dw, b_dw, ln_g, ln_b, w_conv_o, b_conv_o)
    gate_a, gate_b = jnp.split(jax.nn.sigmoid(gate_logits), 2, axis=-1)
    mix = (gate_a * y_a + gate_b * y_b) @ w_out
    h = h + _rmsnorm(mix, g_post_mix)
    u = _rmsnorm(h, g_pre_mlp)
    f = jnp.square(jax.nn.relu(u @ w_up)) @ w_down
    return h + _rmsnorm(f, g_post_mlp)


def _trunk(x, meta_tokens, layer_params):
    bsz = x.shape[0]
    meta = jnp.broadcast_to(meta_tokens[None].astype(x.dtype), (bsz, N_META, D_MODEL))
    h = jnp.concatenate([meta, x], axis=1)
    for l in range(DEPTH):
        h = _layer(h, *[p[l] for p in layer_params])
    return h[:, N_META:]


def setup_inputs(seed: int = 0) -> dict:
    key = jax.random.key(seed)
    ks = jax.random.split(key, 24)
    nrm = lambda k, shape, scale: jax.random.normal(k, shape, jnp.float32) * scale
    gain = lambda k, shape: 1.0 + nrm(k, shape, 0.02)
    D, L = D_MODEL, DEPTH
    return {
        "x_prompt": nrm(ks[0], (BATCH, SEQ, D), 1.0),
        "x_sample": nrm(ks[1], (DEC_BATCH, DEC_SEQ, D), 1.0),
        "meta_tokens": nrm(ks[2], (N_META, D), 1.0),
        "g_pre_mix": gain(ks[3], (L, D)),
        "w_in": nrm(ks[4], (L, D, N_IN), D ** -0.5),
        "w_a2_f": nrm(ks[5], (L, GLA_GATE_RANK, GLA_DQK), GLA_GATE_RANK ** -0.5),
        "b_a_f": DECAY_BIAS_MEAN + nrm(ks[6], (L, GLA_DQK), 0.5),
        "w_a2_b": nrm(ks[7], (L, GLA_GATE_RANK, GLA_DQK), GLA_GATE_RANK ** -0.5),
        "b_a_b": DECAY_BIAS_MEAN + nrm(ks[8], (L, GLA_DQK), 0.5),
        "g_gla": gain(ks[9], (L, GLA_DV_TOT)),
        "w_gla_o": nrm(ks[10], (L, GLA_DV_TOT, D), GLA_DV_TOT ** -0.5),
        "w_dw": nrm(ks[11], (L, CONV_W, CONV_CH), CONV_W ** -0.5),
        "b_dw": nrm(ks[12], (L, CONV_CH), 0.02),
        "ln_g": gain(ks[13], (L, CONV_CH)),
        "ln_b": nrm(ks[14], (L, CONV_CH), 0.02),
        "w_conv_o": nrm(ks[15], (L, CONV_CH, D), CONV_CH ** -0.5),
        "b_conv_o": nrm(ks[16], (L, D), 0.02),
        "w_out": nrm(ks[17], (L, D, D), D ** -0.5),
        "g_post_mix": gain(ks[18], (L, D)),
        "g_pre_mlp": gain(ks[19], (L, D)),
        "w_up": nrm(ks[20], (L, D, D_FF), D ** -0.5),
        "w_down": nrm(ks[21], (L, D_FF, D), D_FF ** -0.5),
        "g_post_mlp": gain(ks[22], (L, D)),
    }


def reference(x_prompt, x_sample, meta_tokens, g_pre_mix, w_in, w_a2_f, b_a_f, w_a2_b, b_a_b,
              g_gla, w_gla_o, w_dw, b_dw, ln_g, ln_b, w_conv_o, b_conv_o, w_out, g_post_mix,
              g_pre_mlp, w_up, w_down, g_post_mlp):
    layer_params = (g_pre_mix, w_in, w_a2_f, b_a_f, w_a2_b, b_a_b, g_gla, w_gla_o,
                    w_dw, b_dw, ln_g, ln_b, w_conv_o, b_conv_o, w_out, g_post_mix,
                    g_pre_mlp, w_up, w_down, g_post_mlp)
    y_prompt = _trunk(x_prompt, meta_tokens, layer_params)
    y_sample = _trunk(x_sample, meta_tokens, layer_params)
    return (y_prompt, y_sample)
```

```python
import numpy as np
import ml_dtypes
from contextlib import ExitStack
import concourse.bass as bass
import concourse.mybir as mybir
from concourse.bass_utils import run_bass_kernel_spmd

F32 = mybir.dt.float32
BF16 = mybir.dt.bfloat16
AF = mybir.ActivationFunctionType
ALU = mybir.AluOpType

D = 2048
KT = 16
NB = 17
ROWS = 112 + 16 + 2048 + 16
NTOK = 2048
NIN = 14368
C_Q, C_K, C_V, C_R, C_ZF, C_ZB, C_A, C_GT, C_GA, C_GB = 0, 1024, 2048, 4096, 6144, 6160, 6176, 8224, 10272, 12320
EPS = 1e-6
NOTH = 128 + 3 * 2048
TRACE = False
SKIP_WDMA = False


class Res:
    __slots__ = ("name", "w", "r")

    def __init__(self, name=""):
        self.name = name
        self.w = None
        self.r = []


class Prog:
    SAME_ENGINE_SYNC = True

    def __init__(self, nc, n_dma_sems=10):
        self.nc = nc
        self.eng = {"pe": nc.tensor, "act": nc.scalar, "dve": nc.vector, "pool": nc.gpsimd, "sp": nc.sync}
        self.sem = {k: nc.alloc_semaphore("s_" + k) for k in self.eng}
        self.cnt = {k: 0 for k in self.eng}
        self.waited = {k: {} for k in self.eng}
        self.dsem, self.dcnt, self.dnext = {}, {}, {}
        for q in ("sp", "act", "pool"):
            self.dsem[q] = [nc.alloc_semaphore(f"d_{q}{i}") for i in range(n_dma_sems)]
            self.dcnt[q] = [0] * n_dma_sems
            self.dnext[q] = 0

    def _wait(self, e, ev):
        if ev is None:
            return
        sem, val, src = ev
        if src == e and not (self.SAME_ENGINE_SYNC and e != "pe"):
            return
        key = id(sem)
        if self.waited[e].get(key, 0) >= val:
            return
        self.waited[e][key] = val
        self.eng[e].wait_ge(sem, val)

    def _deps(self, e, reads, writes):
        for r in reads:
            self._wait(e, r.w)
        for w in writes:
            if w.w is not None and w.w[2] != e:
                self._wait(e, w.w)
            for ev in w.r:
                if ev[2] != e:
                    self._wait(e, ev)

    def _commit(self, ev, reads, writes):
        for r in reads:
            r.r.append(ev)
            if len(r.r) > 40:
                r.r = r.r[-40:]
        for w in writes:
            w.w = ev
            w.r = []

    @staticmethod
    def _flat(xs):
        out = []
        for x in xs:
            if isinstance(x, (list, tuple)):
                out.extend(x)
            else:
                out.append(x)
        return out

    def op(self, e, fn, reads=(), writes=()):
        reads, writes = self._flat(reads), self._flat(writes)
        self._deps(e, reads, writes)
        inst = fn(self.eng[e])
        self.cnt[e] += 1
        inst.then_inc(self.sem[e], 1)
        ev = (self.sem[e], self.cnt[e], e)
        self._commit(ev, reads, writes)
        return ev

    def dma(self, q, out, in_, reads=(), writes=(), **kw):
        reads, writes = self._flat(reads), self._flat(writes)
        i = self.dnext[q]
        self.dnext[q] = (i + 1) % len(self.dsem[q])
        sem = self.dsem[q][i]
        if self.dcnt[q][i] > 0:
            self._wait(q, (sem, self.dcnt[q][i], "dma"))
        self._deps(q, reads, writes)
        inst = self.eng[q].dma_start(out=out, in_=in_, **kw)
        self.dcnt[q][i] += 16
        inst.then_inc(sem, 16)
        ev = (sem, self.dcnt[q][i], "dma")
        self._commit(ev, reads, writes)
        return ev

    def barrier(self):
        evs = [(self.sem[k], self.cnt[k], k) for k in self.eng if self.cnt[k] > 0]
        for q in self.dsem:
            for i, sem in enumerate(self.dsem[q]):
                if self.dcnt[q][i] > 0:
                    evs.append((sem, self.dcnt[q][i], "dma"))
        for e in self.eng:
            for ev in evs:
                if ev[2] != e:
                    self._wait(e, ev)

    def finish(self, e, resources):
        for r in resources:
            self._wait(e, r.w)
            for ev in r.r:
                self._wait(e, ev)


class T:
    def __init__(self, t, name):
        self.t = t
        self.r = Res(name)

    def __getitem__(self, k):
        return self.t[k]


def build_program():
    nc = bass.Bass("TRN2", target_bir_lowering=False)
    P = Prog(nc)

    def din(name, shape, dt=F32):
        return nc.dram_tensor(name, list(shape), dt, kind="ExternalInput").ap()

    xin = din("xin", [ROWS, D])
    w_in_l = din("w_in_l", [28, 128, 16, 512])
    wz_l = din("wz_l", [128, 16, 32])
    w_a2_f = din("w_a2_f", [16, 1024]); b_a_f = din("b_a_f", [1, 1024])
    w_a2_b = din("w_a2_b", [16, 1024]); b_a_b = din("b_a_b", [1, 1024])
    w_gla_o_l = din("w_gla_o_l", [4, 128, 16, 512]); w_conv_o_l = din("w_conv_o_l", [4, 128, 16, 512]); w_out_l = din("w_out_l", [4, 128, 16, 512])
    w_up_l = din("w_up_l", [16, 128, 16, 512]); w_down_l = din("w_down_l", [16, 128, 16, 512])
    cols_d = din("cols", [128, 8, 16])
    wdw_d = din("wdw", [128, 16, 31])
    gbc_d = din("gbc", [2, 128, D])
    flags_d = din("flags", [128, 32])
    xoth = din("xoth", [NOTH, D])
    ident_d = din("ident", [128, 128], BF16)
    masks_d = din("masks", [128, 6, 128])
    yout = nc.dram_tensor("yout", [NTOK, D], F32, kind="ExternalOutput").ap()

    s_qdb = nc.dram_tensor("s_qdb", [NB, 128, 1024], BF16).ap()
    s_kdb = nc.dram_tensor("s_kdb", [NB, 128, 1024], BF16).ap()
    s_v = nc.dram_tensor("s_v", [NB, 128, D], BF16).ap()
    s_o = nc.dram_tensor("s_o", [NB, 128, D], F32).ap()
    s_onT = nc.dram_tensor("s_onT", [D, NB * 128], BF16).ap()
    r_scr = {k: Res(k) for k in ["qdb", "qcf", "kdb", "v", "o", "onT", "cc_in", "cc_out", "yout"]}

    def sb(name, shape, dt=F32):
        return T(nc.alloc_sbuf_tensor("t_" + name, list(shape), dt), name)

    psts = [T(nc.alloc_psum_tensor(f"pst{i}", [128, 8, 128], BF16), f"pst{i}") for i in range(2)]
    banks = [T(nc.alloc_psum_tensor(f"pb{i}", [128, 512], F32), f"pb{i}") for i in range(5)]
    pmisc = T(nc.alloc_psum_tensor("pmisc", [128, 512], F32), "pmisc")
    bstate = {"i": 0}

    def nbank():
        b = banks[bstate["i"] % 5]
        bstate["i"] += 1
        return b

    cols = sb("cols", [128, 8, 16])
    flags = sb("flags", [128, 32])
    ident = sb("ident", [128, 128], BF16)
    masks = sb("masks", [128, 6, 128])
    P.dma("sp", cols[:], cols_d, writes=[cols.r])
    P.dma("sp", flags[:], flags_d, writes=[flags.r])
    P.dma("sp", ident[:], ident_d, writes=[ident.r])
    P.dma("sp", masks[:], masks_d, writes=[masks.r])
    CG_PREMIX, CG_PREMLP, C_BDW, C_LNG, C_LNB, C_GGLA, C_BCO = 0, 1, 2, 3, 4, 5, 6
    M_TRII, M_MASKB, M_TRII_S, M_TRIE_S, M_SUF_S, M_ONES_S = 0, 1, 2, 3, 4, 5
    ones = sb("ones", [128, 128])
    P.op("pool", lambda e: e.memset(ones[:], 1.0), writes=[ones.r])
    gcol0 = sb("gcol0", [128, 16])
    P.op("dve", lambda e: e.tensor_scalar(out=gcol0[:], in0=cols[:, CG_PREMIX, :], scalar1=flags[:, 12:13], scalar2=None, op0=ALU.mult),
         reads=[cols.r, flags.r], writes=[gcol0.r])

    wbuf = []
    wstate = {"i": 0}

    def alloc_wbuf(alloc):
        wbuf.clear()
        for i in range(2):
            wb_ = alloc(f"wbuf{i}_{wstate['i']}", [128, 16, 512], BF16)
            wb_.rq = [Res("wq") for _ in range(4)]
            wbuf.append(wb_)

    def load_w(src_ap):
        wb = wbuf[wstate["i"] % 2]
        wstate["i"] += 1
        src = src_ap
        if SKIP_WDMA and wstate["i"] > 2:
            return wb
        P.dma("pool", wb[:], src, writes=[wb.rq])
        return wb

    xt = sb("xt", [128, D])
    junk = sb("junk", [128, D], BF16)
    ub = sb("ub", [128, D], BF16)
    ssq = sb("ssq", [128, 4])
    rstd = sb("rstd", [128, 4])

    def rms_rows(src, n, width, out_bf, nh=1):
        for h in range(nh):
            P.op("act", lambda e, h=h: e.activation(out=junk[0:n, h * width:(h + 1) * width], in_=src[0:n, h * width:(h + 1) * width],
                                                   func=AF.Square, accum_out=ssq[0:n, h:h + 1]),
                 reads=[src.r], writes=[junk.r, ssq.r])
        P.op("act", lambda e: e.activation(out=rstd[0:n, 0:nh], in_=ssq[0:n, 0:nh], func=AF.Sqrt, scale=1.0 / width, bias=EPS),
             reads=[ssq.r], writes=[rstd.r])
        P.op("dve", lambda e: e.reciprocal(out=rstd[0:n, 0:nh], in_=rstd[0:n, 0:nh]), reads=[rstd.r], writes=[rstd.r])
        if out_bf is not None:
            for h in range(nh):
                P.op("dve", lambda e, h=h: e.tensor_scalar(out=out_bf[0:n, h * width:(h + 1) * width], in0=src[0:n, h * width:(h + 1) * width],
                                                          scalar1=rstd[0:n, h:h + 1], scalar2=None, op0=ALU.mult),
                     reads=[src.r, rstd.r], writes=[out_bf.r])

    def transpose_rows(src_bf, n, dstT, col0, scale_cols):
        for g in range(2):
            pst = psts[g]
            for j in range(8):
                kt = g * 8 + j
                P.op("pe", lambda e, kt=kt, j=j: e.transpose(out=pst[:, j, 0:n], in_=src_bf[0:n, kt * 128:(kt + 1) * 128], identity=ident[0:n, 0:n]),
                     reads=[src_bf.r, ident.r], writes=[pst.r])
            for j in range(8):
                kt = g * 8 + j
                if scale_cols is None:
                    P.op("act", lambda e, kt=kt, j=j: e.activation(out=dstT[:, kt, col0:col0 + n], in_=pst[:, j, 0:n], func=AF.Copy),
                         reads=[pst.r], writes=[dstT.r])
                else:
                    sc, scr = scale_cols
                    P.op("act", lambda e, kt=kt, j=j: e.activation(out=dstT[:, kt, col0:col0 + n], in_=pst[:, j, 0:n], func=AF.Copy, scale=sc[:, kt:kt + 1]),
                         reads=[pst.r, scr], writes=[dstT.r])

    esS = ExitStack()
    Sf = T(esS.enter_context(nc.sbuf_tensor("t_Sf", [128, 8, 512], F32)), "Sf")
    Sba = T(esS.enter_context(nc.sbuf_tensor("t_Sba", [128, 8, 512], F32)), "Sba")
    Sf.rk = [Res("Sfk") for _ in range(8)]
    Sba.rk = [Res("Sbak") for _ in range(8)]
    P.op("pool", lambda e: e.memset(Sf[:], 0.0), writes=[Sf.rk])
    P.op("pool", lambda e: e.memset(Sba[:], 0.0), writes=[Sba.rk])

    def decay_z(S, b_flags):
        for d in range(2):
            pb = nbank()
            for kt in range(KT):
                P.op("pe", lambda e, kt=kt: e.matmul(out=pb[0:16, 0:128], lhsT=S.wz[:, kt, d * 16:(d + 1) * 16], rhs=S.uT[:, kt, :], start=(kt == 0), stop=(kt == KT - 1)),
                     reads=[S.wz.r, S.uT.r], writes=[pb.r])
            P.op("dve", lambda e: e.tensor_copy(out=S.zT[d][:], in_=pb[0:16, 0:128]), reads=[pb.r], writes=[S.zT[d].r])
            for cc in range(2):
                pb2 = nbank()
                P.op("pe", lambda e: e.matmul(out=pb2[:], lhsT=S.zT[d][:], rhs=S.wa2[d][:, cc * 512:(cc + 1) * 512], start=True, stop=False),
                     reads=[S.zT[d].r, S.wa2[d].r], writes=[pb2.r])
                P.op("pe", lambda e: e.matmul(out=pb2[:], lhsT=S.ones1[:], rhs=S.ba[d][:, cc * 512:(cc + 1) * 512], start=False, stop=True),
                     reads=[S.ones1.r, S.ba[d].r], writes=[pb2.r])
                P.op("act", lambda e: e.activation(out=S.spl[d][:, cc * 512:(cc + 1) * 512], in_=pb2[:], func=AF.Exp, scale=-1.0),
                     reads=[pb2.r], writes=[S.spl[d].r])
            P.op("act", lambda e: e.activation(out=S.spl[d][:], in_=S.spl[d][:], func=AF.Ln, bias=1.0, scale=1.0), reads=[S.spl[d].r], writes=[S.spl[d].r])
            if b_flags is not None:
                fc = b_flags[d]
                P.op("dve", lambda e: e.tensor_scalar(out=S.spl[d][:], in0=S.spl[d][:], scalar1=flags[:, fc:fc + 1], scalar2=None, op0=ALU.mult),
                     reads=[S.spl[d].r, flags.r], writes=[S.spl[d].r])
    def decay_tot(S):
        for d in range(2):
            for kt in range(8):
                P.op("pe", lambda e, kt=kt: e.matmul(out=pmisc[:, 256 + d * 8 + kt:256 + d * 8 + kt + 1], lhsT=S.spl[d][:, kt * 128:(kt + 1) * 128],
                                                     rhs=masks[:, M_ONES_S, 0:1], start=True, stop=True),
                     reads=[S.spl[d].r, masks.r], writes=[pmisc.r])
        P.op("act", lambda e: e.activation(out=S.Dcols[:], in_=pmisc[:, 256:272], func=AF.Exp), reads=[pmisc.r], writes=[S.Dcols.r])

    def tm_decays(S):
        for d, mi in ((0, M_SUF_S), (1, M_TRIE_S)):
            for cc in range(2):
                pb = nbank()
                P.op("pe", lambda e: e.matmul(out=pb[:], lhsT=masks[:, mi, :], rhs=S.spl[d][:, cc * 512:(cc + 1) * 512], start=True, stop=True),
                     reads=[S.spl[d].r, masks.r], writes=[pb.r])
                P.op("act", lambda e: e.activation(out=S.Gb[d][:, cc * 512:(cc + 1) * 512], in_=pb[:], func=AF.Exp), reads=[pb.r], writes=[S.Gb[d].r])

    class NS:
        pass

    nc.push_named_scope('phase0')
    es0 = ExitStack()

    def sb0(name, shape, dt=F32):
        return T(es0.enter_context(nc.sbuf_tensor("p0_" + name, list(shape), dt)), name)

    S0 = NS()
    S0.uT = sb0("uT", [128, 16, 128], BF16)
    S0.wz = sb0("wz", [128, 16, 32], BF16)
    for qd in range(4):
        P.dma("pool", S0.wz[:, qd * 4:(qd + 1) * 4, :], wz_l[:, qd * 4:(qd + 1) * 4, :], writes=[S0.wz.r])
    S0.wa2 = [sb0("wa2f", [16, 1024]), sb0("wa2b", [16, 1024])]
    S0.ba = [sb0("baf", [1, 1024]), sb0("bab", [1, 1024])]
    P.dma("sp", S0.wa2[0][:], w_a2_f, writes=[S0.wa2[0].r]); P.dma("sp", S0.wa2[1][:], w_a2_b, writes=[S0.wa2[1].r])
    P.dma("sp", S0.ba[0][:], b_a_f, writes=[S0.ba[0].r]); P.dma("sp", S0.ba[1][:], b_a_b, writes=[S0.ba[1].r])
    S0.ones1 = sb0("ones1", [1, 128])
    P.op("pool", lambda e: e.memset(S0.ones1[:], 1.0), writes=[S0.ones1.r])
    S0.zT = [sb0("zfT", [16, 128]), sb0("zbT", [16, 128])]
    S0.spl = [sb0("spf", [128, 1024]), sb0("spb", [128, 1024])]
    S0.Gb = [sb0("G0", [128, 1024]), sb0("G1", [128, 1024])]
    S0.Dcols = sb0("Dcols", [128, 16])
    wkv = sb0("wkv", [128, 16, 3072], BF16)
    wkv_r = [Res("wkv") for _ in range(6)]
    for c in range(6):
        srcw = w_in_l[2 + c]
        for qd in range(4):
            P.dma("pool", wkv[:, qd * 4:(qd + 1) * 4, c * 512:(c + 1) * 512], srcw[:, qd * 4:(qd + 1) * 4, :], writes=[wkv_r[c]])
    ktm0 = sb0("ktm", [128, 1024]); vtm0 = sb0("vtm", [128, D], BF16)
    kendf0 = sb0("kendf", [128, 1024], BF16); kdbtm0 = sb0("kdbtm", [128, 1024], BF16)
    Pacc = sb0("Pacc", [128, 16]); expP = sb0("expP", [128, 16])
    P.op("pool", lambda e: e.memset(Pacc[:], 0.0), writes=[Pacc.r])
    ub2 = sb0("ub2", [128, D], BF16)
    uT2 = sb0("uT2", [128, 16, 128], BF16)
    ubs = [ub, ub2]
    uTs = [S0.uT, uT2]

    def p0_front(pb_i):
        P.dma("sp", xt[:], xoth[pb_i * 128:(pb_i + 1) * 128, :], writes=[xt.r])
        rms_rows(xt, 128, D, ubs[pb_i % 2])
        transpose_rows(ubs[pb_i % 2], 128, uTs[pb_i % 2], 0, (cols[:, CG_PREMIX, :], cols.r))

    NP0 = NOTH // 128
    p0_front(0)
    for pb_i in range(NP0):
        if pb_i == 0:
            bf = (13, 14)
        else:
            j = (pb_i - 1) // 16
            bf = (16 + 2 * j, 17 + 2 * j)
        S0.uT = uTs[pb_i % 2]
        decay_z(S0, bf)
        for c in range(6):
            pb = nbank()
            for kt in range(KT):
                P.op("pe", lambda e, kt=kt: e.matmul(out=pb[:], lhsT=S0.uT[:, kt, :], rhs=wkv[:, kt, c * 512:(c + 1) * 512], start=(kt == 0), stop=(kt == KT - 1)),
                     reads=[wkv_r[c], S0.uT.r], writes=[pb.r])
            if c < 2:
                P.op("dve", lambda e: e.tensor_copy(out=ktm0[:, c * 512:(c + 1) * 512], in_=pb[:]), reads=[pb.r], writes=[ktm0.r])
            else:
                P.op("act", lambda e: e.activation(out=vtm0[:, (c - 2) * 512:(c - 1) * 512], in_=pb[:], func=AF.Copy), reads=[pb.r], writes=[vtm0.r])
        if pb_i + 1 < NP0:
            p0_front(pb_i + 1)
        decay_tot(S0)
        P.op("act", lambda e: e.activation(out=expP[:], in_=Pacc[:], func=AF.Exp), reads=[Pacc.r], writes=[expP.r])
        tm_decays(S0)
        P.op("dve", lambda e: e.scalar_tensor_tensor(out=kendf0[:], in0=ktm0[:], scalar=flags[:, bf[0]:bf[0] + 1], in1=S0.Gb[0][:], op0=ALU.mult, op1=ALU.mult),
             reads=[ktm0.r, flags.r, S0.Gb[0].r], writes=[kendf0.r])
        P.op("dve", lambda e: e.scalar_tensor_tensor(out=kdbtm0[:], in0=ktm0[:], scalar=flags[:, bf[1]:bf[1] + 1], in1=S0.Gb[1][:], op0=ALU.mult, op1=ALU.mult),
             reads=[ktm0.r, flags.r, S0.Gb[1].r], writes=[kdbtm0.r])
        for kt in range(8):
            h = kt // 2
            pu = nbank()
            P.op("pe", lambda e: e.matmul(out=pu[:], lhsT=kendf0[:, kt * 128:(kt + 1) * 128], rhs=vtm0[:, h * 512:(h + 1) * 512], start=True, stop=True),
                 reads=[kendf0.r, vtm0.r], writes=[pu.r])
            P.op("dve", lambda e: e.scalar_tensor_tensor(out=Sf[:, kt, :], in0=Sf[:, kt, :], scalar=S0.Dcols[:, kt:kt + 1], in1=pu[:], op0=ALU.mult, op1=ALU.add),
                 reads=[Sf.rk[kt], S0.Dcols.r, pu.r], writes=[Sf.rk[kt]])
            pu2 = nbank()
            P.op("pe", lambda e: e.matmul(out=pu2[:], lhsT=kdbtm0[:, kt * 128:(kt + 1) * 128], rhs=vtm0[:, h * 512:(h + 1) * 512], start=True, stop=True),
                 reads=[kdbtm0.r, vtm0.r], writes=[pu2.r])
            P.op("dve", lambda e: e.scalar_tensor_tensor(out=Sba[:, kt, :], in0=pu2[:], scalar=expP[:, 8 + kt:9 + kt], in1=Sba[:, kt, :], op0=ALU.mult, op1=ALU.add),
                 reads=[Sba.rk[kt], expP.r, pu2.r], writes=[Sba.rk[kt]])
        P.op("dve", lambda e: e.tensor_tensor(out=Pacc[:], in0=Pacc[:], in1=pmisc[:, 256:272], op=ALU.add), reads=[Pacc.r, pmisc.r], writes=[Pacc.r])
    S0.uT = uTs[0]
    for b in range(NB):
        P.dma("sp", xt[:], xin[b * 128:(b + 1) * 128, :], writes=[xt.r])
        rms_rows(xt, 128, D, ub)
        transpose_rows(ub, 128, S0.uT, 0, ((gcol0, gcol0.r) if b == 0 else (cols[:, CG_PREMIX, :], cols.r)))
        for c in range(2, 6):
            pb = nbank()
            for kt in range(KT):
                P.op("pe", lambda e, kt=kt: e.matmul(out=pb[:], lhsT=S0.uT[:, kt, :], rhs=wkv[:, kt, c * 512:(c + 1) * 512], start=(kt == 0), stop=(kt == KT - 1)),
                     reads=[wkv_r[c], S0.uT.r], writes=[pb.r])
            P.op("act", lambda e: e.activation(out=vtm0[:, (c - 2) * 512:(c - 1) * 512], in_=pb[:], func=AF.Copy), reads=[pb.r], writes=[vtm0.r])
        P.dma("sp", s_v[b], vtm0[:], reads=[vtm0.r], writes=[r_scr["v"]])
    P.barrier()
    es0.close()
    nc.pop_named_scope('phase0')
    nc.push_named_scope('sweep1a')

    es = ExitStack()

    def sb1(name, shape, dt=F32):
        return T(es.enter_context(nc.sbuf_tensor("t_" + name, list(shape), dt)), name)

    alloc_wbuf(sb1)
    S1 = NS()
    uT = S1.uT = sb1("uT", [128, 16, 128], BF16)
    S1.wz = sb1("wz", [128, 16, 32], BF16)
    for qd in range(4):
        P.dma("pool", S1.wz[:, qd * 4:(qd + 1) * 4, :], wz_l[:, qd * 4:(qd + 1) * 4, :], writes=[S1.wz.r])
    S1.wa2 = [sb1("wa2f", [16, 1024]), sb1("wa2b", [16, 1024])]
    S1.ba = [sb1("baf", [1, 1024]), sb1("bab", [1, 1024])]
    P.dma("sp", S1.wa2[0][:], w_a2_f, writes=[S1.wa2[0].r]); P.dma("sp", S1.wa2[1][:], w_a2_b, writes=[S1.wa2[1].r])
    P.dma("sp", S1.ba[0][:], b_a_f, writes=[S1.ba[0].r]); P.dma("sp", S1.ba[1][:], b_a_b, writes=[S1.ba[1].r])
    S1.ones1 = sb1("ones1", [1, 128])
    P.op("pool", lambda e: e.memset(S1.ones1[:], 1.0), writes=[S1.ones1.r])
    qT = sb1("qT", [128, 8, 128]); kT = sb1("kT", [128, 8, 128])
    ktm = sb1("ktm", [128, 1024]); vtm = sb1("vtm", [128, D], BF16)
    S1.zT = [sb1("zfT", [16, 128]), sb1("zbT", [16, 128])]
    spl = S1.spl = [sb1("spf", [128, 1024]), sb1("spb", [128, 1024])]
    Eb = [sb1("E0", [128, 8, 128]), sb1("E1", [128, 8, 128])]
    Gb = S1.Gb = [sb1("G0", [128, 1024]), sb1("G1", [128, 1024])]
    qdf = sb1("qdf", [128, 8, 128], BF16); kdf = sb1("kdf", [128, 8, 128], BF16)
    qdb = sb1("qdb", [128, 8, 128], BF16); kdb = sb1("kdb", [128, 8, 128], BF16)
    kendf = sb1("kendf", [128, 1024], BF16); kdbtm = sb1("kdbtm", [128, 1024], BF16)
    st1 = sb1("st1", [128, 128]); st2 = sb1("st2", [128, 128]); ST = sb1("ST", [128, 128], BF16)
    Sfb = sb1("Sfb", [128, 8, 512], BF16)
    opart = sb1("opart", [128, D])
    Dcols = S1.Dcols = sb1("Dcols", [128, 16])
    Dball = sb1("Dball", [128, NB, 8])
    Sfb.rk = [Res("Sfbk") for _ in range(8)]
    P.op("act", lambda e: e.activation(out=Sfb[:], in_=Sf[:], func=AF.Copy), reads=[Sf.rk], writes=[Sfb.rk])

    for b in range(NB):
        P.dma("sp", xt[:], xin[b * 128:(b + 1) * 128, :], writes=[xt.r])
        rms_rows(xt, 128, D, ub)
        transpose_rows(ub, 128, uT, 0, ((gcol0, gcol0.r) if b == 0 else (cols[:, CG_PREMIX, :], cols.r)))
        decay_z(S1, (12, 12) if b == 0 else None)
        P.dma("sp", vtm[:], s_v[b], reads=[r_scr["v"]], writes=[vtm.r])
        for c in range(4):
            wb = load_w(w_in_l[c])
            if c < 4:
                pb = nbank()
                for n in range(4):
                    for kt in range(KT):
                        P.op("pe", lambda e, n=n, kt=kt: e.matmul(out=pb[:, n * 128:(n + 1) * 128], lhsT=wb[:, kt, n * 128:(n + 1) * 128], rhs=uT[:, kt, :],
                                                                  start=(kt == 0), stop=(kt == KT - 1)),
                             reads=[wb.rq[kt // 4], uT.r], writes=[pb.r])
                dst = qT if c < 2 else kT
                cc = c % 2
                P.op("act", lambda e: e.activation(out=dst[:, cc * 4:(cc + 1) * 4, :], in_=pb[:].rearrange("p (a b) -> p a b", a=4), func=AF.Copy,
                                                   scale=(1.0 / 16 if c < 2 else 1.0)),
                     reads=[pb.r], writes=[dst.r])
            if c >= 2:
                pb = nbank()
                for kt in range(KT):
                    P.op("pe", lambda e, kt=kt: e.matmul(out=pb[:], lhsT=uT[:, kt, :], rhs=wb[:, kt, :], start=(kt == 0), stop=(kt == KT - 1)),
                         reads=[wb.rq[kt // 4], uT.r], writes=[pb.r])
                if c < 4:
                    P.op("dve", lambda e: e.tensor_copy(out=ktm[:, (c - 2) * 512:(c - 1) * 512], in_=pb[:]), reads=[pb.r], writes=[ktm.r])
                else:
                    P.op("dve", lambda e: e.tensor_copy(out=vtm[:, (c - 4) * 512:(c - 3) * 512], in_=pb[:]), reads=[pb.r], writes=[vtm.r])
        decay_tot(S1)
        if b >= 1:
            P.op("dve", lambda e: e.tensor_copy(out=Dball[:, b, :], in_=Dcols[:, 8:16]), reads=[Dcols.r], writes=[Dball.r])
        pa, pb_ = nbank(), nbank()
        for kt in range(8):
            tgt = pa if kt < 4 else pb_
            P.op("pe", lambda e, kt=kt, tgt=tgt: e.matmul(out=tgt[:, (kt % 4) * 128:(kt % 4 + 1) * 128], lhsT=spl[0][:, kt * 128:(kt + 1) * 128],
                                                          rhs=masks[:, M_TRII_S, :], start=True, stop=True),
                 reads=[spl[0].r, masks.r], writes=[tgt.r])
        for half, tgt in enumerate((pa, pb_)):
            P.op("act", lambda e, half=half, tgt=tgt: e.activation(out=Eb[0][:, half * 4:(half + 1) * 4, :], in_=tgt[:].rearrange("p (a b) -> p a b", a=4), func=AF.Exp),
                 reads=[tgt.r], writes=[Eb[0].r])
            P.op("act", lambda e, half=half, tgt=tgt: e.activation(out=Eb[1][:, half * 4:(half + 1) * 4, :], in_=tgt[:].rearrange("p (a b) -> p a b", a=4), func=AF.Exp, scale=-1.0),
                 reads=[tgt.r], writes=[Eb[1].r])
        P.op("dve", lambda e: e.tensor_tensor(out=qdf[:], in0=qT[:], in1=Eb[0][:], op=ALU.mult), reads=[qT.r, Eb[0].r], writes=[qdf.r])
        P.op("pool", lambda e: e.tensor_tensor(out=kdf[:], in0=kT[:], in1=Eb[1][:], op=ALU.mult), reads=[kT.r, Eb[1].r], writes=[kdf.r])
        pa, pb_ = nbank(), nbank()
        for kt in range(8):
            tgt = pa if kt < 4 else pb_
            P.op("pe", lambda e, kt=kt, tgt=tgt: e.matmul(out=tgt[:, (kt % 4) * 128:(kt % 4 + 1) * 128], lhsT=spl[1][:, kt * 128:(kt + 1) * 128],
                                                          rhs=masks[:, M_TRIE_S, :], start=True, stop=True),
                 reads=[spl[1].r, masks.r], writes=[tgt.r])
        for half, tgt in enumerate((pa, pb_)):
            P.op("act", lambda e, half=half, tgt=tgt: e.activation(out=Eb[0][:, half * 4:(half + 1) * 4, :], in_=tgt[:].rearrange("p (a b) -> p a b", a=4), func=AF.Exp, scale=-1.0),
                 reads=[tgt.r], writes=[Eb[0].r])
            P.op("act", lambda e, half=half, tgt=tgt: e.activation(out=Eb[1][:, half * 4:(half + 1) * 4, :], in_=tgt[:].rearrange("p (a b) -> p a b", a=4), func=AF.Exp),
                 reads=[tgt.r], writes=[Eb[1].r])
        P.op("dve", lambda e: e.tensor_tensor(out=qdb[:], in0=qT[:], in1=Eb[0][:], op=ALU.mult), reads=[qT.r, Eb[0].r], writes=[qdb.r])
        P.op("pool", lambda e: e.tensor_tensor(out=kdb[:], in0=kT[:], in1=Eb[1][:], op=ALU.mult), reads=[kT.r, Eb[1].r], writes=[kdb.r])
        tm_decays(S1)
        P.op("dve", lambda e: e.tensor_tensor(out=kendf[:], in0=ktm[:], in1=Gb[0][:], op=ALU.mult), reads=[ktm.r, Gb[0].r], writes=[kendf.r])
        P.op("pool", lambda e: e.tensor_tensor(out=kdbtm[:], in0=ktm[:], in1=Gb[1][:], op=ALU.mult), reads=[ktm.r, Gb[1].r], writes=[kdbtm.r])
        for h in range(4):
            if b >= 1:
                for di, (kd, qd) in enumerate(((kdf, qdf), (kdb, qdb))):
                    for j in range(2):
                        kt = 2 * h + j
                        P.op("pe", lambda e, kt=kt, j=j: e.matmul(out=pmisc[:, di * 128:(di + 1) * 128], lhsT=kd[:, kt, :], rhs=qd[:, kt, :], start=(j == 0), stop=(j == 1)),
                             reads=[kd.r, qd.r], writes=[pmisc.r])
                P.op("dve", lambda e: e.tensor_tensor(out=st1[:], in0=pmisc[:, 0:128], in1=masks[:, M_TRII, :], op=ALU.mult), reads=[pmisc.r, masks.r], writes=[st1.r])
                P.op("dve", lambda e: e.tensor_tensor(out=st2[:], in0=pmisc[:, 128:256], in1=masks[:, M_MASKB, :], op=ALU.mult), reads=[pmisc.r, masks.r], writes=[st2.r])
                P.op("pool", lambda e: e.tensor_tensor(out=ST[:], in0=st1[:], in1=st2[:], op=ALU.add), reads=[st1.r, st2.r], writes=[ST.r])
                po = nbank()
                P.op("pe", lambda e: e.matmul(out=po[:], lhsT=ST[:], rhs=vtm[:, h * 512:(h + 1) * 512], start=True, stop=False),
                     reads=[ST.r, vtm.r], writes=[po.r])
                for j in range(2):
                    kt = 2 * h + j
                    P.op("pe", lambda e, kt=kt, j=j: e.matmul(out=po[:], lhsT=qdf[:, kt, :], rhs=Sfb[:, kt, :], start=False, stop=(j == 1)),
                         reads=[qdf.r, Sfb.rk[kt]], writes=[po.r])
                P.op("act", lambda e: e.activation(out=opart[:, h * 512:(h + 1) * 512], in_=po[:], func=AF.Copy), reads=[po.r], writes=[opart.r])
            for j in range(2):
                kt = 2 * h + j
                pu = nbank()
                P.op("pe", lambda e, kt=kt: e.matmul(out=pu[:], lhsT=kendf[:, kt * 128:(kt + 1) * 128], rhs=vtm[:, h * 512:(h + 1) * 512], start=True, stop=True),
                     reads=[kendf.r, vtm.r], writes=[pu.r])
                P.op("dve", lambda e, kt=kt: e.scalar_tensor_tensor(out=Sf[:, kt, :], in0=Sf[:, kt, :], scalar=Dcols[:, kt:kt + 1], in1=pu[:], op0=ALU.mult, op1=ALU.add),
                     reads=[Sf.rk[kt], Dcols.r, pu.r], writes=[Sf.rk[kt]])
                P.op("pool", lambda e, kt=kt: e.tensor_copy(out=Sfb[:, kt, :], in_=Sf[:, kt, :]), reads=[Sf.rk[kt]], writes=[Sfb.rk[kt]])
        if b >= 1:
            P.dma("sp", s_qdb[b].rearrange("p (a b) -> p a b", a=8), qdb[:], reads=[qdb.r], writes=[r_scr["qdb"]])
            P.dma("sp", s_kdb[b], kdbtm[:], reads=[kdbtm.r], writes=[r_scr["kdb"]])
            P.dma("sp", s_o[b], opart[:], reads=[opart.r], writes=[r_scr["o"]])

    nc.pop_named_scope('sweep1a')
    nc.push_named_scope('sweep1b')
    Sb = Sba
    Sbb = Sfb
    ofull = opart
    on_bf = ub
    onT = uT
    for b in range(NB - 1, 0, -1):
        P.dma("sp", qdb[:], s_qdb[b].rearrange("p (a b) -> p a b", a=8), reads=[r_scr["qdb"]], writes=[qdb.r])
        P.dma("sp", kdbtm[:], s_kdb[b], reads=[r_scr["kdb"]], writes=[kdbtm.r])
        P.dma("sp", vtm[:], s_v[b], reads=[r_scr["v"]], writes=[vtm.r])
        P.dma("sp", xt[:], s_o[b], reads=[r_scr["o"]], writes=[xt.r])
        for kt in range(8):
            P.op("dve", lambda e, kt=kt: e.tensor_scalar(out=Sb[:, kt, :], in0=Sb[:, kt, :], scalar1=Dball[:, b, kt:kt + 1], scalar2=None, op0=ALU.mult),
                 reads=[Sb.rk[kt], Dball.r], writes=[Sb.rk[kt]])
            P.op("pool", lambda e, kt=kt: e.tensor_copy(out=Sbb[:, kt, :], in_=Sb[:, kt, :]), reads=[Sb.rk[kt]], writes=[Sbb.rk[kt]])
        for h in range(4):
            po = nbank()
            for j in range(2):
                kt = 2 * h + j
                P.op("pe", lambda e, kt=kt, j=j: e.matmul(out=po[:], lhsT=qdb[:, kt, :], rhs=Sbb[:, kt, :], start=(j == 0), stop=(j == 1)),
                     reads=[qdb.r, Sbb.rk[kt]], writes=[po.r])
            P.op("dve", lambda e: e.tensor_tensor(out=ofull[:, h * 512:(h + 1) * 512], in0=po[:], in1=xt[:, h * 512:(h + 1) * 512], op=ALU.add),
                 reads=[po.r, xt.r], writes=[ofull.r])
            for j in range(2):
                kt = 2 * h + j
                pu = nbank()
                P.op("pe", lambda e, kt=kt: e.matmul(out=pu[:], lhsT=kdbtm[:, kt * 128:(kt + 1) * 128], rhs=vtm[:, h * 512:(h + 1) * 512], start=True, stop=True),
                     reads=[kdbtm.r, vtm.r], writes=[pu.r])
                P.op("dve", lambda e, kt=kt: e.tensor_tensor(out=Sb[:, kt, :], in0=Sb[:, kt, :], in1=pu[:], op=ALU.add), reads=[Sb.rk[kt], pu.r], writes=[Sb.rk[kt]])
        rms_rows(ofull, 128, 512, on_bf, nh=4)
        transpose_rows(on_bf, 128, onT, 0, None)
        P.dma("sp", s_onT.rearrange("(vt p) t -> p vt t", p=128)[:, :, b * 128:(b + 1) * 128], onT[:], reads=[onT.r], writes=[r_scr["onT"]])
    P.barrier()
    es.close()
    esS.close()
    nc.pop_named_scope('sweep1b')
    nc.push_named_scope('sweep2')

    TB = 512
    NTB = 4
    W = TB + 32
    alloc_wbuf(sb)
    wdw = sb("wdw", [128, 16, 31])
    P.dma("sp", wdw[:], wdw_d, writes=[wdw.r])
    uW = sb("uW", [128, 16, W], BF16)
    htmp = sb("htmp", [128, TB])
    OFF = dict(A=0, B=16384, B2=32768, C=34816, Dd=67584, Ee=83968, Ff=100352, END=133120)
    arena = nc.alloc_sbuf_tensor("t_arena2", [128, OFF["END"] // 4], F32)
    RR = {k: Res("reg" + k) for k in ("A", "B", "B2", "C", "Dd", "Ee", "Ff", "Ff2")}

    class V:
        def __init__(self, start, nbytes, dt, regs, pattern=None, **kw):
            ap = arena[:, start // 4:(start + nbytes) // 4]
            if dt == BF16:
                ap = ap.bitcast(BF16)
            if pattern is not None:
                ap = ap.rearrange(pattern, **kw)
            self.t = ap
            self.r = [RR[k] for k in regs]

        def __getitem__(self, k):
            return self.t[k]

    gT = V(OFF["A"], 16 * W * 4, F32, ("A", "B", "B2"), "p (k t) -> p k t", k=16)
    yT = V(OFF["C"], 32768, F32, ("C",), "p (k t) -> p k t", k=16)
    yact = V(OFF["A"], 16384, BF16, ("A",), "p (k t) -> p k t", k=16)
    sgb = V(OFF["Ee"], 16384, BF16, ("Ee",), "p (k t) -> p k t", k=16)
    mb = V(OFF["C"], 32768, F32, ("C",), "p (k t) -> p k t", k=16)
    ract = V(OFF["Ff"] + 10752, 16384, BF16, ("Ff2",), "p (k t) -> p k t", k=16)
    ogT = V(OFF["B"], 16384, BF16, ("B",), "p (k t) -> p k t", k=16)
    sga = V(OFF["Dd"], 16384, BF16, ("Dd",), "p (k t) -> p k t", k=16)
    mixT = V(OFF["Ee"], 16384, BF16, ("Ee",), "p (k t) -> p k t", k=16)
    sg = V(OFF["Ff"], W * 4, F32, ("Ff",))
    ysq = V(OFF["Ff"] + 2304, 2048, F32, ("Ff",))
    mu = V(OFF["Ff"] + 4352, 2048, F32, ("Ff",))
    var = V(OFF["Ff"] + 6400, 2048, F32, ("Ff",))
    rs = V(OFF["Ff"] + 8448, 2048, F32, ("Ff",))
    mix2 = V(OFF["A"], 32768, F32, ("A", "B"), "p (k t) -> p k t", k=4)
    h1 = V(OFF["C"], 32768, F32, ("C",), "p (k t) -> p k t", k=4)
    hidT = V(OFF["Dd"], 65536, BF16, ("Dd", "Ee", "Ff", "Ff2"), "p (k t) -> p k t", k=64)
    ftm = mix2
    u2T = uW

    def fm_proj(src_ap_fn, nchunks, rhs, col0, ncols, evac):
        parts = [(0, ncols)] if ncols <= 512 else [(0, ncols // 2), (ncols // 2, ncols - ncols // 2)]
        for c in range(nchunks):
            wb = load_w(src_ap_fn(c))
            for n in range(4):
                for (c0, nn) in parts:
                    pb = nbank()
                    for kt in range(KT):
                        P.op("pe", lambda e, n=n, kt=kt: e.matmul(out=pb[:, 0:nn], lhsT=wb[:, kt, n * 128:(n + 1) * 128], rhs=rhs[:, kt, col0 + c0:col0 + c0 + nn],
                                                                  start=(kt == 0), stop=(kt == KT - 1)),
                             reads=[wb.rq[kt // 4], rhs.r], writes=[pb.r])
                    evac(c * 4 + n, pb, c0, nn)

    yv = yout.rearrange("(s k p) d -> s p k d", k=NTB, p=128)

    def prep_norm(sb_i, m):
        rr = 112 + TB * sb_i
        n = 128 if m < NTB else 32
        P.dma("sp", xt[0:n, :], xin[rr + 128 * m:rr + 128 * m + n, :], writes=[xt.r])
        rms_rows(xt, n, D, ub)

    def prep_tr(m):
        n = 128 if m < NTB else 32
        transpose_rows(ub, n, uW, 128 * m, (cols[:, CG_PREMIX, :], cols.r))

    for sbk in range(NTOK // TB):
        r0 = 112 + TB * sbk
        t0 = r0 + 16
        if sbk == 0:
            for m in range(NTB + 1):
                prep_norm(0, m)
                prep_tr(m)
        def ev_a(t, pb, c0, n):
            P.op("act", lambda e: e.activation(out=gT[:, t, c0:c0 + n], in_=pb[:, 0:n], func=AF.Copy), reads=[pb.r], writes=[gT.r])
        fm_proj(lambda c: w_in_l[12 + c], 4, uW, 0, W, ev_a)

        def ev_gt(t, pb, c0, n):
            P.op("act", lambda e: e.activation(out=sg[:, c0:c0 + n], in_=pb[:, 0:n], func=AF.Sigmoid), reads=[pb.r], writes=[sg.r])
            P.op("dve", lambda e: e.tensor_tensor(out=gT[:, t, c0:c0 + n], in0=gT[:, t, c0:c0 + n], in1=sg[:, c0:c0 + n], op=ALU.mult), reads=[gT.r, sg.r], writes=[gT.r])
        fm_proj(lambda c: w_in_l[16 + c], 4, uW, 0, W, ev_gt)
        yct = [Res("yct") for _ in range(16)]
        for ct in range(16):
            P.op("dve", lambda e, ct=ct: e.tensor_scalar(out=yT[:, ct, :], in0=gT[:, ct, 1:1 + TB], scalar1=wdw[:, ct, 0:1], scalar2=cols[:, C_BDW, ct:ct + 1],
                                                         op0=ALU.mult, op1=ALU.add),
                 reads=[gT.r, wdw.r, cols.r], writes=[yT.r, yct[ct]])
        for tap in range(1, 31):
            for ct in range(16):
                P.op("dve", lambda e, ct=ct, tap=tap: e.scalar_tensor_tensor(out=yT[:, ct, :], in0=gT[:, ct, tap + 1:tap + 1 + TB], scalar=wdw[:, ct, tap:tap + 1],
                                                                             in1=yT[:, ct, :], op0=ALU.mult, op1=ALU.add),
                     reads=[gT.r, wdw.r, yct[ct]], writes=[yT.r, yct[ct]])
        def ev_gb(t, pb, c0, n):
            P.op("act", lambda e: e.activation(out=sgb[:, t, :], in_=pb[:, 0:n], func=AF.Sigmoid), reads=[pb.r], writes=[sgb.r])
        fm_proj(lambda c: w_in_l[24 + c], 4, uW, 16, TB, ev_gb)

        def ev_ga(t, pb, c0, n):
            P.op("act", lambda e: e.activation(out=sga[:, t, :], in_=pb[:, 0:n], func=AF.Sigmoid), reads=[pb.r], writes=[sga.r])
        fm_proj(lambda c: w_in_l[20 + c], 4, uW, 16, TB, ev_ga)

        def ev_r(t, pb, c0, n):
            P.op("act", lambda e: e.activation(out=ract[:, t, :], in_=pb[:, 0:n], func=AF.Silu), reads=[pb.r], writes=[ract.r])
        fm_proj(lambda c: w_in_l[8 + c], 4, uW, 16, TB, ev_r)
        ps1, ps2 = nbank(), nbank()
        for ct in range(16):
            P.op("pe", lambda e, ct=ct: e.matmul(out=ps1[:], lhsT=ones[:], rhs=yT[:, ct, :], start=(ct == 0), stop=(ct == 15)),
                 reads=[ones.r, yT.r], writes=[ps1.r])
        for ct in range(16):
            P.op("act", lambda e, ct=ct: e.activation(out=ysq[:], in_=yT[:, ct, :], func=AF.Square), reads=[yT.r], writes=[ysq.r])
            P.op("pe", lambda e, ct=ct: e.matmul(out=ps2[:], lhsT=ones[:], rhs=ysq[:], start=(ct == 0), stop=(ct == 15)),
                 reads=[ones.r, ysq.r], writes=[ps2.r])
        P.op("dve", lambda e: e.tensor_scalar(out=mu[:], in0=ps1[:], scalar1=1.0 / D, scalar2=None, op0=ALU.mult), reads=[ps1.r], writes=[mu.r])
        P.op("dve", lambda e: e.tensor_tensor(out=var[:], in0=mu[:], in1=mu[:], op=ALU.mult), reads=[mu.r], writes=[var.r])
        P.op("dve", lambda e: e.scalar_tensor_tensor(out=var[:], in0=ps2[:], scalar=1.0 / D, in1=var[:], op0=ALU.mult, op1=ALU.subtract),
             reads=[ps2.r, var.r], writes=[var.r])
        P.op("act", lambda e: e.activation(out=rs[:], in_=var[:], func=AF.Sqrt, scale=1.0, bias=EPS), reads=[var.r], writes=[rs.r])
        P.op("dve", lambda e: e.reciprocal(out=rs[:], in_=rs[:]), reads=[rs.r], writes=[rs.r])
        for ct in range(16):
            P.op("dve", lambda e, ct=ct: e.tensor_tensor(out=yT[:, ct, :], in0=yT[:, ct, :], in1=mu[:], op=ALU.subtract), reads=[yT.r, mu.r], writes=[yT.r])
            P.op("pool", lambda e, ct=ct: e.tensor_tensor(out=yT[:, ct, :], in0=yT[:, ct, :], in1=rs[:], op=ALU.mult), reads=[yT.r, rs.r], writes=[yT.r])
        for ct in range(16):
            P.op("act", lambda e, ct=ct: e.activation(out=yact[:, ct, :], in_=yT[:, ct, :], func=AF.Silu, scale=cols[:, C_LNG, ct:ct + 1], bias=cols[:, C_LNB, ct:ct + 1]),
                 reads=[yT.r, cols.r], writes=[yact.r])
        def ev_yb(t, pb, c0, n):
            P.op("dve", lambda e: e.scalar_tensor_tensor(out=mb[:, t, :], in0=pb[:, 0:n], scalar=cols[:, C_BCO, t:t + 1], in1=sgb[:, t, :], op0=ALU.add, op1=ALU.mult),
                 reads=[pb.r, cols.r, sgb.r], writes=[mb.r])
        fm_proj(lambda c: w_conv_o_l[c], 4, yact, 0, TB, ev_yb)
        P.dma("sp", ogT[:], s_onT.rearrange("(vt p) t -> p vt t", p=128)[:, :, t0:t0 + TB], reads=[r_scr["onT"]], writes=[ogT.r])
        for vt in range(16):
            P.op("dve", lambda e, vt=vt: e.scalar_tensor_tensor(out=ogT[:, vt, :], in0=ogT[:, vt, :], scalar=cols[:, C_GGLA, vt:vt + 1], in1=ract[:, vt, :], op0=ALU.mult, op1=ALU.mult),
                 reads=[ogT.r, cols.r, ract.r], writes=[ogT.r])

        def ev_ya(t, pb, c0, n):
            P.op("dve", lambda e: e.tensor_tensor(out=htmp[:], in0=pb[:, 0:n], in1=sga[:, t, :], op=ALU.mult), reads=[pb.r, sga.r], writes=[htmp.r])
            P.op("pool", lambda e: e.tensor_tensor(out=mixT[:, t, :], in0=htmp[:], in1=mb[:, t, :], op=ALU.add), reads=[htmp.r, mb.r], writes=[mixT.r])
        fm_proj(lambda c: w_gla_o_l[c], 4, ogT, 0, TB, ev_ya)
        for c in range(4):
            wb = load_w(w_out_l[c])
            for tb in range(NTB):
                pb = nbank()
                for kt in range(KT):
                    P.op("pe", lambda e, kt=kt: e.matmul(out=pb[:], lhsT=mixT[:, kt, tb * 128:(tb + 1) * 128], rhs=wb[:, kt, :], start=(kt == 0), stop=(kt == KT - 1)),
                         reads=[wb.rq[kt // 4], mixT.r], writes=[pb.r])
                P.op("act", lambda e: e.activation(out=mix2[:, tb, c * 512:(c + 1) * 512], in_=pb[:], func=AF.Copy), reads=[pb.r], writes=[mix2.r])
        P.dma("sp", h1[:], xin[t0:t0 + TB, :].rearrange("(k p) d -> p k d", p=128), writes=[h1.r])
        P.dma("sp", xt[:], gbc_d[0], writes=[xt.r])
        for tb in range(NTB):
            for h in range(1):
                P.op("act", lambda e: e.activation(out=junk[:], in_=mix2[:, tb, :], func=AF.Square, accum_out=ssq[:, 0:1]), reads=[mix2.r], writes=[junk.r, ssq.r])
            P.op("act", lambda e: e.activation(out=rstd[:, 0:1], in_=ssq[:, 0:1], func=AF.Sqrt, scale=1.0 / D, bias=EPS), reads=[ssq.r], writes=[rstd.r])
            P.op("dve", lambda e: e.reciprocal(out=rstd[:, 0:1], in_=rstd[:, 0:1]), reads=[rstd.r], writes=[rstd.r])
            P.op("dve", lambda e: e.scalar_tensor_tensor(out=mix2[:, tb, :], in0=mix2[:, tb, :], scalar=rstd[:, 0:1], in1=xt[:], op0=ALU.mult, op1=ALU.mult),
                 reads=[mix2.r, rstd.r, xt.r], writes=[mix2.r])
            P.op("pool", lambda e: e.tensor_tensor(out=h1[:, tb, :], in0=h1[:, tb, :], in1=mix2[:, tb, :], op=ALU.add), reads=[h1.r, mix2.r], writes=[h1.r])
        for tb in range(NTB):
            P.op("act", lambda e: e.activation(out=junk[:], in_=h1[:, tb, :], func=AF.Square, accum_out=ssq[:, 0:1]), reads=[h1.r], writes=[junk.r, ssq.r])
            P.op("act", lambda e: e.activation(out=rstd[:, 0:1], in_=ssq[:, 0:1], func=AF.Sqrt, scale=1.0 / D, bias=EPS), reads=[ssq.r], writes=[rstd.r])
            P.op("dve", lambda e: e.reciprocal(out=rstd[:, 0:1], in_=rstd[:, 0:1]), reads=[rstd.r], writes=[rstd.r])
            P.op("dve", lambda e: e.tensor_scalar(out=ub[:], in0=h1[:, tb, :], scalar1=rstd[:, 0:1], scalar2=None, op0=ALU.mult), reads=[h1.r, rstd.r], writes=[ub.r])
            transpose_rows(ub, 128, u2T, 128 * tb, (cols[:, CG_PREMLP, :], cols.r))

        def ev_up(t, pb, c0, n):
            P.op("act", lambda e: e.activation(out=htmp[:], in_=pb[:, 0:n], func=AF.Relu), reads=[pb.r], writes=[htmp.r])
            P.op("pool", lambda e: e.tensor_tensor(out=hidT[:, t, :], in0=htmp[:], in1=htmp[:], op=ALU.mult), reads=[htmp.r], writes=[hidT.r])
        fm_proj(lambda c: w_up_l[c], 16, u2T, 0, TB, ev_up)
        nxt = sbk + 1 < NTOK // TB
        for c in range(4):
            if nxt:
                prep_norm(sbk + 1, c)
            pbs = [nbank() for _ in range(NTB)]
            for kg in range(4):
                wb = load_w(w_down_l[kg * 4 + c])
                for tb in range(NTB):
                    for kt in range(KT):
                        P.op("pe", lambda e, kt=kt: e.matmul(out=pbs[tb][:], lhsT=hidT[:, kg * 16 + kt, tb * 128:(tb + 1) * 128], rhs=wb[:, kt, :],
                                                             start=(kg == 0 and kt == 0), stop=(kg == 3 and kt == KT - 1)),
                             reads=[wb.rq[kt // 4], hidT.r], writes=[pbs[tb].r])
            for tb in range(NTB):
                P.op("act", lambda e: e.activation(out=ftm[:, tb, c * 512:(c + 1) * 512], in_=pbs[tb][:], func=AF.Copy), reads=[pbs[tb].r], writes=[ftm.r])
            if nxt:
                prep_tr(c)
        if nxt:
            prep_norm(sbk + 1, NTB)
            prep_tr(NTB)
        P.dma("sp", xt[:], gbc_d[1], writes=[xt.r])
        for tb in range(NTB):
            P.op("act", lambda e: e.activation(out=junk[:], in_=ftm[:, tb, :], func=AF.Square, accum_out=ssq[:, 0:1]), reads=[ftm.r], writes=[junk.r, ssq.r])
            P.op("act", lambda e: e.activation(out=rstd[:, 0:1], in_=ssq[:, 0:1], func=AF.Sqrt, scale=1.0 / D, bias=EPS), reads=[ssq.r], writes=[rstd.r])
            P.op("dve", lambda e: e.reciprocal(out=rstd[:, 0:1], in_=rstd[:, 0:1]), reads=[rstd.r], writes=[rstd.r])
            P.op("dve", lambda e: e.scalar_tensor_tensor(out=ftm[:, tb, :], in0=ftm[:, tb, :], scalar=rstd[:, 0:1], in1=xt[:], op0=ALU.mult, op1=ALU.mult),
                 reads=[ftm.r, rstd.r, xt.r], writes=[ftm.r])
            P.op("pool", lambda e: e.tensor_tensor(out=h1[:, tb, :], in0=h1[:, tb, :], in1=ftm[:, tb, :], op=ALU.add), reads=[h1.r, ftm.r], writes=[h1.r])
        P.dma("sp", yv[sbk], h1[:], reads=[h1.r], writes=[r_scr["yout"]])

    P.finish("sp", [r_scr["yout"]])
    nc.pop_named_scope('sweep2')
    return nc


_NC_CACHE = {}


def kernel(x_prompt, x_sample, meta_tokens, g_pre_mix, w_in, w_a2_f, b_a_f, w_a2_b, b_a_b,
           g_gla, w_gla_o, w_dw, b_dw, ln_g, ln_b, w_conv_o, b_conv_o, w_out, g_post_mix,
           g_pre_mlp, w_up, w_down, g_post_mlp):
    f32 = np.float32
    A = lambda a: np.ascontiguousarray(np.asarray(a, dtype=f32))
    x_prompt, x_sample, meta = A(x_prompt), A(x_sample), A(meta_tokens)
    col = lambda v: np.ascontiguousarray(np.asarray(v, f32).reshape(16, 128).T)
    cols = np.zeros((128, 8, 16), f32)
    for i, v in enumerate((g_pre_mix, g_pre_mlp, b_dw, ln_g, ln_b, g_gla, b_conv_o)):
        cols[:, i, :] = col(v)
    wdw = np.ascontiguousarray(np.asarray(w_dw, f32)[0].T.reshape(16, 128, 31).transpose(1, 0, 2))
    gbc = np.ascontiguousarray(np.stack([np.broadcast_to(np.asarray(g_post_mix, f32).reshape(1, D), (128, D)),
                                         np.broadcast_to(np.asarray(g_post_mlp, f32).reshape(1, D), (128, D))]))
    ident = np.eye(128).astype(ml_dtypes.bfloat16)
    jj, ii = np.meshgrid(np.arange(128), np.arange(128), indexing="ij")
    s = f32(-1.0 / 16)
    masks = np.zeros((128, 6, 128), f32)
    masks[:, 0, :] = (jj <= ii)
    masks[:, 1, :] = (jj > ii)
    masks[:, 2, :] = (jj <= ii) * s
    masks[:, 3, :] = (jj < ii) * s
    masks[:, 4, :] = (jj > ii) * s
    masks[:, 5, :] = s
    def chunks(Wm, col_starts, width=512):
        return np.ascontiguousarray(np.stack([Wm[:, cs:cs + width].reshape(16, 128, width).transpose(1, 0, 2) for cs in col_starts]))
    Win = A(w_in)[0]
    starts = [512 * i for i in range(8)] + [C_R + 512 * i for i in range(4)] + [C_A + 512 * i for i in range(4)] + \
             [C_GT + 512 * i for i in range(4)] + [C_GA + 512 * i for i in range(4)] + [C_GB + 512 * i for i in range(4)]
    Wd = A(w_down)[0]
    shared = dict(
        w_in_l=chunks(Win, starts), wz_l=chunks(Win, [C_ZF], 32)[0],
        w_a2_f=A(w_a2_f)[0], b_a_f=A(b_a_f), w_a2_b=A(w_a2_b)[0], b_a_b=A(b_a_b),
        w_gla_o_l=chunks(A(w_gla_o)[0], [0, 512, 1024, 1536]), w_conv_o_l=chunks(A(w_conv_o)[0], [0, 512, 1024, 1536]),
        w_out_l=chunks(A(w_out)[0], [0, 512, 1024, 1536]), w_up_l=chunks(A(w_up)[0], [512 * i for i in range(16)]),
        w_down_l=np.ascontiguousarray(np.concatenate([chunks(Wd[kg * 2048:(kg + 1) * 2048], [0, 512, 1024, 1536]) for kg in range(4)])),
        cols=cols, wdw=wdw, gbc=gbc, ident=ident, masks=masks)
    in_maps = []
    xs = x_sample[0]
    for c in range(8):
        xin = np.zeros((ROWS, D), f32)
        fl = np.zeros((128, 32), f32)
        xoth = np.zeros((NOTH, D), f32)
        if c < 4:
            xin[112:128] = meta
            xin[128:2176] = x_prompt[c]
        else:
            q = c - 4
            xin[112:128] = meta if q == 0 else xs[2048 * q - 16:2048 * q]
            xin[128:2176] = xs[2048 * q:2048 * (q + 1)]
            if q < 3:
                xin[2176:2192] = xs[2048 * (q + 1):2048 * (q + 1) + 16]
            xoth[112:128] = meta
            fl[:, 13] = 1.0 if q > 0 else 0.0
            others = [o for o in range(4) if o != q]
            for j, o in enumerate(others):
                xoth[128 + 2048 * j:128 + 2048 * (j + 1)] = xs[2048 * o:2048 * (o + 1)]
                fl[:, 16 + 2 * j] = 1.0 if o < q else 0.0
                fl[:, 17 + 2 * j] = 1.0 if o > q else 0.0
        fl[:, 12] = 1.0 if c <= 4 else 0.0
        d = dict(shared)
        d["xoth"] = xoth
        d["xin"] = xin
        d["flags"] = fl
        in_maps.append(d)
    if "nc" not in _NC_CACHE:
        _NC_CACHE["nc"] = build_program()
    if TRACE:
        res = run_bass_kernel_spmd(_NC_CACHE["nc"], in_maps, core_ids=list(range(8)), trace=True)
        print("EXEC_NS", res.exec_time_ns, res.mean_exec_time_ns, res.max_exec_time_core_id)
        _NC_CACHE["res"] = res
    else:
        res = run_bass_kernel_spmd(_NC_CACHE["nc"], in_maps, core_ids=list(range(8)))
    outs = [np.asarray(r["yout"], dtype=f32) for r in res.results]
    y_prompt = np.stack(outs[:4], axis=0)
    y_sample = np.concatenate(outs[4:], axis=0)[None]
    return (y_prompt, y_sample)
```

```python
import numpy as np
import ml_dtypes
from contextlib import ExitStack
import concourse.bass as bass
import concourse.mybir as mybir
from concourse.bass_utils import run_bass_kernel_spmd

F32 = mybir.dt.float32
BF16 = mybir.dt.bfloat16
AF = mybir.ActivationFunctionType
ALU = mybir.AluOpType

D = 2048
KT = 16
NB = 17
ROWS = 112 + 16 + 2048 + 16
NTOK = 2048
NIN = 14368
C_Q, C_K, C_V, C_R, C_ZF, C_ZB, C_A, C_GT, C_GA, C_GB = 0, 1024, 2048, 4096, 6144, 6160, 6176, 8224, 10272, 12320
EPS = 1e-6
NOTH = 128 + 3 * 2048
TRACE = False
SKIP_WDMA = False


class Res:
    __slots__ = ("name", "w", "r")

    def __init__(self, name=""):
        self.name = name
        self.w = None
        self.r = []


class Prog:
    SAME_ENGINE_SYNC = True

    def __init__(self, nc, n_dma_sems=10):
        self.nc = nc
        self.eng = {"pe": nc.tensor, "act": nc.scalar, "dve": nc.vector, "pool": nc.gpsimd, "sp": nc.sync}
        self.sem = {k: nc.alloc_semaphore("s_" + k) for k in self.eng}
        self.cnt = {k: 0 for k in self.eng}
        self.waited = {k: {} for k in self.eng}
        self.dsem, self.dcnt, self.dnext = {}, {}, {}
        for q in ("sp", "act", "pool"):
            self.dsem[q] = [nc.alloc_semaphore(f"d_{q}{i}") for i in range(n_dma_sems)]
            self.dcnt[q] = [0] * n_dma_sems
            self.dnext[q] = 0

    def _wait(self, e, ev):
        if ev is None:
            return
        sem, val, src = ev
        if src == e and not (self.SAME_ENGINE_SYNC and e != "pe"):
            return
        key = id(sem)
        if self.waited[e].get(key, 0) >= val:
            return
        self.waited[e][key] = val
        self.eng[e].wait_ge(sem, val)

    def _deps(self, e, reads, writes):
        for r in reads:
            self._wait(e, r.w)
        for w in writes:
            if w.w is not None and w.w[2] != e:
                self._wait(e, w.w)
            for ev in w.r:
                if ev[2] != e:
                    self._wait(e, ev)

    def _commit(self, ev, reads, writes):
        for r in reads:
            r.r.append(ev)
            if len(r.r) > 40:
                r.r = r.r[-40:]
        for w in writes:
            w.w = ev
            w.r = []

    @staticmethod
    def _flat(xs):
        out = []
        for x in xs:
            if isinstance(x, (list, tuple)):
                out.extend(x)
            else:
                out.append(x)
        return out

    def op(self, e, fn, reads=(), writes=()):
        reads, writes = self._flat(reads), self._flat(writes)
        self._deps(e, reads, writes)
        inst = fn(self.eng[e])
        self.cnt[e] += 1
        inst.then_inc(self.sem[e], 1)
        ev = (self.sem[e], self.cnt[e], e)
        self._commit(ev, reads, writes)
        return ev

    def dma(self, q, out, in_, reads=(), writes=(), **kw):
        reads, writes = self._flat(reads), self._flat(writes)
        i = self.dnext[q]
        self.dnext[q] = (i + 1) % len(self.dsem[q])
        sem = self.dsem[q][i]
        if self.dcnt[q][i] > 0:
            self._wait(q, (sem, self.dcnt[q][i], "dma"))
        self._deps(q, reads, writes)
        inst = self.eng[q].dma_start(out=out, in_=in_, **kw)
        self.dcnt[q][i] += 16
        inst.then_inc(sem, 16)
        ev = (sem, self.dcnt[q][i], "dma")
        self._commit(ev, reads, writes)
        return ev

    def barrier(self):
        evs = [(self.sem[k], self.cnt[k], k) for k in self.eng if self.cnt[k] > 0]
        for q in self.dsem:
            for i, sem in enumerate(self.dsem[q]):
                if self.dcnt[q][i] > 0:
                    evs.append((sem, self.dcnt[q][i], "dma"))
        for e in self.eng:
            for ev in evs:
                if ev[2] != e:
                    self._wait(e, ev)

    def finish(self, e, resources):
        for r in resources:
            self._wait(e, r.w)
            for ev in r.r:
                self._wait(e, ev)


class T:
    def __init__(self, t, name):
        self.t = t
        self.r = Res(name)

    def __getitem__(self, k):
        return self.t[k]


def build_program():
    nc = bass.Bass("TRN2", target_bir_lowering=False)
    P = Prog(nc)

    def din(name, shape, dt=F32):
        return nc.dram_tensor(name, list(shape), dt, kind="ExternalInput").ap()

    xin = din("xin", [ROWS, D])
    w_in_l = din("w_in_l", [28, 128, 16, 512])
    wz_l = din("wz_l", [128, 16, 32])
    w_a2_f = din("w_a2_f", [16, 1024]); b_a_f = din("b_a_f", [1, 1024])
    w_a2_b = din("w_a2_b", [16, 1024]); b_a_b = din("b_a_b", [1, 1024])
    w_gla_o_l = din("w_gla_o_l", [4, 128, 16, 512]); w_conv_o_l = din("w_conv_o_l", [4, 128, 16, 512]); w_out_l = din("w_out_l", [4, 128, 16, 512])
    w_up_l = din("w_up_l", [16, 128, 16, 512]); w_down_l = din("w_down_l", [16, 128, 16, 512])
    cols_d = din("cols", [128, 8, 16])
    wdw_d = din("wdw", [128, 16, 31])
    gbc_d = din("gbc", [2, 128, D])
    flags_d = din("flags", [128, 32])
    xoth = din("xoth", [NOTH, D])
    ident_d = din("ident", [128, 128], BF16)
    masks_d = din("masks", [128, 6, 128])
    yout = nc.dram_tensor("yout", [NTOK, D], F32, kind="ExternalOutput").ap()

    s_qdb = nc.dram_tensor("s_qdb", [NB, 128, 1024], BF16).ap()
    s_kdb = nc.dram_tensor("s_kdb", [NB, 128, 1024], BF16).ap()
    s_v = nc.dram_tensor("s_v", [NB, 128, D], BF16).ap()
    s_o = nc.dram_tensor("s_o", [NB, 128, D], F32).ap()
    s_onT = nc.dram_tensor("s_onT", [D, NB * 128], BF16).ap()
    r_scr = {k: Res(k) for k in ["qdb", "qcf", "kdb", "v", "o", "onT", "cc_in", "cc_out", "yout"]}

    def sb(name, shape, dt=F32):
        return T(nc.alloc_sbuf_tensor("t_" + name, list(shape), dt), name)

    psts = [T(nc.alloc_psum_tensor(f"pst{i}", [128, 8, 128], BF16), f"pst{i}") for i in range(2)]
    banks = [T(nc.alloc_psum_tensor(f"pb{i}", [128, 512], F32), f"pb{i}") for i in range(5)]
    pmisc = T(nc.alloc_psum_tensor("pmisc", [128, 512], F32), "pmisc")
    bstate = {"i": 0}

    def nbank():
        b = banks[bstate["i"] % 5]
        bstate["i"] += 1
        return b

    cols = sb("cols", [128, 8, 16])
    flags = sb("flags", [128, 32])
    ident = sb("ident", [128, 128], BF16)
    masks = sb("masks", [128, 6, 128])
    P.dma("sp", cols[:], cols_d, writes=[cols.r])
    P.dma("sp", flags[:], flags_d, writes=[flags.r])
    P.dma("sp", ident[:], ident_d, writes=[ident.r])
    P.dma("sp", masks[:], masks_d, writes=[masks.r])
    CG_PREMIX, CG_PREMLP, C_BDW, C_LNG, C_LNB, C_GGLA, C_BCO = 0, 1, 2, 3, 4, 5, 6
    M_TRII, M_MASKB, M_TRII_S, M_TRIE_S, M_SUF_S, M_ONES_S = 0, 1, 2, 3, 4, 5
    ones = sb("ones", [128, 128])
    P.op("pool", lambda e: e.memset(ones[:], 1.0), writes=[ones.r])
    gcol0 = sb("gcol0", [128, 16])
    P.op("dve", lambda e: e.tensor_scalar(out=gcol0[:], in0=cols[:, CG_PREMIX, :], scalar1=flags[:, 12:13], scalar2=None, op0=ALU.mult),
         reads=[cols.r, flags.r], writes=[gcol0.r])

    wbuf = []
    wstate = {"i": 0}

    def alloc_wbuf(alloc):
        wbuf.clear()
        for i in range(2):
            wb_ = alloc(f"wbuf{i}_{wstate['i']}", [128, 16, 512], BF16)
            wb_.rq = [Res("wq") for _ in range(4)]
            wbuf.append(wb_)

    def load_w(src_ap):
        wb = wbuf[wstate["i"] % 2]
        wstate["i"] += 1
        src = src_ap
        if SKIP_WDMA and wstate["i"] > 2:
            return wb
        for hf in range(2):
            P.dma("pool", wb[:, hf * 8:(hf + 1) * 8, :], src[:, hf * 8:(hf + 1) * 8, :], writes=[wb.rq[2 * hf], wb.rq[2 * hf + 1]])
        return wb

    xt = sb("xt", [128, D])
    junk = sb("junk", [128, D], BF16)
    ub = sb("ub", [128, D], BF16)
    ssq = sb("ssq", [128, 4])
    rstd = sb("rstd", [128, 4])

    def rms_rows(src, n, width, out_bf, nh=1):
        for h in range(nh):
            P.op("act", lambda e, h=h: e.activation(out=junk[0:n, h * width:(h + 1) * width], in_=src[0:n, h * width:(h + 1) * width],
                                                   func=AF.Square, accum_out=ssq[0:n, h:h + 1]),
                 reads=[src.r], writes=[junk.r, ssq.r])
        P.op("act", lambda e: e.activation(out=rstd[0:n, 0:nh], in_=ssq[0:n, 0:nh], func=AF.Sqrt, scale=1.0 / width, bias=EPS),
             reads=[ssq.r], writes=[rstd.r])
        P.op("dve", lambda e: e.reciprocal(out=rstd[0:n, 0:nh], in_=rstd[0:n, 0:nh]), reads=[rstd.r], writes=[rstd.r])
        if out_bf is not None:
            for h in range(nh):
                P.op("dve", lambda e, h=h: e.tensor_scalar(out=out_bf[0:n, h * width:(h + 1) * width], in0=src[0:n, h * width:(h + 1) * width],
                                                          scalar1=rstd[0:n, h:h + 1], scalar2=None, op0=ALU.mult),
                     reads=[src.r, rstd.r], writes=[out_bf.r])

    def transpose_rows(src_bf, n, dstT, col0, scale_cols):
        for g in range(2):
            pst = psts[g]
            for j in range(8):
                kt = g * 8 + j
                P.op("pe", lambda e, kt=kt, j=j: e.transpose(out=pst[:, j, 0:n], in_=src_bf[0:n, kt * 128:(kt + 1) * 128], identity=ident[0:n, 0:n]),
                     reads=[src_bf.r, ident.r], writes=[pst.r])
            for j in range(8):
                kt = g * 8 + j
                if scale_cols is None:
                    P.op("act", lambda e, kt=kt, j=j: e.activation(out=dstT[:, kt, col0:col0 + n], in_=pst[:, j, 0:n], func=AF.Copy),
                         reads=[pst.r], writes=[dstT.r])
                else:
                    sc, scr = scale_cols
                    P.op("act", lambda e, kt=kt, j=j: e.activation(out=dstT[:, kt, col0:col0 + n], in_=pst[:, j, 0:n], func=AF.Copy, scale=sc[:, kt:kt + 1]),
                         reads=[pst.r, scr], writes=[dstT.r])

    esS = ExitStack()
    Sf = T(esS.enter_context(nc.sbuf_tensor("t_Sf", [128, 8, 512], F32)), "Sf")
    Sba = T(esS.enter_context(nc.sbuf_tensor("t_Sba", [128, 8, 512], F32)), "Sba")
    Sf.rk = [Res("Sfk") for _ in range(8)]
    Sba.rk = [Res("Sbak") for _ in range(8)]
    P.op("pool", lambda e: e.memset(Sf[:], 0.0), writes=[Sf.rk])
    P.op("pool", lambda e: e.memset(Sba[:], 0.0), writes=[Sba.rk])

    def decay_z(S, b_flags):
        for d in range(2):
            pb = nbank()
            for kt in range(KT):
                P.op("pe", lambda e, kt=kt: e.matmul(out=pb[0:16, 0:128], lhsT=S.wz[:, kt, d * 16:(d + 1) * 16], rhs=S.uT[:, kt, :], start=(kt == 0), stop=(kt == KT - 1)),
                     reads=[S.wz.r, S.uT.r], writes=[pb.r])
            P.op("dve", lambda e: e.tensor_copy(out=S.zT[d][:], in_=pb[0:16, 0:128]), reads=[pb.r], writes=[S.zT[d].r])
            for cc in range(2):
                pb2 = nbank()
                P.op("pe", lambda e: e.matmul(out=pb2[:], lhsT=S.zT[d][:], rhs=S.wa2[d][:, cc * 512:(cc + 1) * 512], start=True, stop=False),
                     reads=[S.zT[d].r, S.wa2[d].r], writes=[pb2.r])
                P.op("pe", lambda e: e.matmul(out=pb2[:], lhsT=S.ones1[:], rhs=S.ba[d][:, cc * 512:(cc + 1) * 512], start=False, stop=True),
                     reads=[S.ones1.r, S.ba[d].r], writes=[pb2.r])
                P.op("act", lambda e: e.activation(out=S.spl[d][:, cc * 512:(cc + 1) * 512], in_=pb2[:], func=AF.Exp, scale=-1.0),
                     reads=[pb2.r], writes=[S.spl[d].r])
            P.op("act", lambda e: e.activation(out=S.spl[d][:], in_=S.spl[d][:], func=AF.Ln, bias=1.0, scale=1.0), reads=[S.spl[d].r], writes=[S.spl[d].r])
            if b_flags is not None:
                fc = b_flags[d]
                P.op("dve", lambda e: e.tensor_scalar(out=S.spl[d][:], in0=S.spl[d][:], scalar1=flags[:, fc:fc + 1], scalar2=None, op0=ALU.mult),
                     reads=[S.spl[d].r, flags.r], writes=[S.spl[d].r])
    def decay_tot(S):
        for d in range(2):
            for kt in range(8):
                P.op("pe", lambda e, kt=kt: e.matmul(out=pmisc[:, 256 + d * 8 + kt:256 + d * 8 + kt + 1], lhsT=S.spl[d][:, kt * 128:(kt + 1) * 128],
                                                     rhs=masks[:, M_ONES_S, 0:1], start=True, stop=True),
                     reads=[S.spl[d].r, masks.r], writes=[pmisc.r])
        P.op("act", lambda e: e.activation(out=S.Dcols[:], in_=pmisc[:, 256:272], func=AF.Exp), reads=[pmisc.r], writes=[S.Dcols.r])

    def tm_decays(S):
        for d, mi in ((0, M_SUF_S), (1, M_TRIE_S)):
            for cc in range(2):
                pb = nbank()
                P.op("pe", lambda e: e.matmul(out=pb[:], lhsT=masks[:, mi, :], rhs=S.spl[d][:, cc * 512:(cc + 1) * 512], start=True, stop=True),
                     reads=[S.spl[d].r, masks.r], writes=[pb.r])
                P.op("act", lambda e: e.activation(out=S.Gb[d][:, cc * 512:(cc + 1) * 512], in_=pb[:], func=AF.Exp), reads=[pb.r], writes=[S.Gb[d].r])

    class NS:
        pass

    nc.push_named_scope('phase0')
    es0 = ExitStack()

    def sb0(name, shape, dt=F32):
        return T(es0.enter_context(nc.sbuf_tensor("p0_" + name, list(shape), dt)), name)

    S0 = NS()
    S0.uT = sb0("uT", [128, 16, 128], BF16)
    S0.wz = sb0("wz", [128, 16, 32], BF16)
    for qd in range(4):
        P.dma("pool", S0.wz[:, qd * 4:(qd + 1) * 4, :], wz_l[:, qd * 4:(qd + 1) * 4, :], writes=[S0.wz.r])
    S0.wa2 = [sb0("wa2f", [16, 1024]), sb0("wa2b", [16, 1024])]
    S0.ba = [sb0("baf", [1, 1024]), sb0("bab", [1, 1024])]
    P.dma("sp", S0.wa2[0][:], w_a2_f, writes=[S0.wa2[0].r]); P.dma("sp", S0.wa2[1][:], w_a2_b, writes=[S0.wa2[1].r])
    P.dma("sp", S0.ba[0][:], b_a_f, writes=[S0.ba[0].r]); P.dma("sp", S0.ba[1][:], b_a_b, writes=[S0.ba[1].r])
    S0.ones1 = sb0("ones1", [1, 128])
    P.op("pool", lambda e: e.memset(S0.ones1[:], 1.0), writes=[S0.ones1.r])
    S0.zT = [sb0("zfT", [16, 128]), sb0("zbT", [16, 128])]
    S0.spl = [sb0("spf", [128, 1024]), sb0("spb", [128, 1024])]
    S0.Gb = [sb0("G0", [128, 1024]), sb0("G1", [128, 1024])]
    S0.Dcols = sb0("Dcols", [128, 16])
    wkv = sb0("wkv", [128, 16, 3072], BF16)
    wkv_r = [Res("wkv") for _ in range(6)]
    for c in range(6):
        srcw = w_in_l[2 + c]
        for qd in range(4):
            P.dma("pool", wkv[:, qd * 4:(qd + 1) * 4, c * 512:(c + 1) * 512], srcw[:, qd * 4:(qd + 1) * 4, :], writes=[wkv_r[c]])
    ktm0 = sb0("ktm", [128, 1024]); vtm0 = sb0("vtm", [128, D], BF16)
    kendf0 = sb0("kendf", [128, 1024], BF16); kdbtm0 = sb0("kdbtm", [128, 1024], BF16)
    Pacc = sb0("Pacc", [128, 16]); expP = sb0("expP", [128, 16])
    P.op("pool", lambda e: e.memset(Pacc[:], 0.0), writes=[Pacc.r])
    ub2 = sb0("ub2", [128, D], BF16)
    uT2 = sb0("uT2", [128, 16, 128], BF16)
    ubs = [ub, ub2]
    uTs = [S0.uT, uT2]

    def p0_front(pb_i):
        P.dma("sp", xt[:], xoth[pb_i * 128:(pb_i + 1) * 128, :], writes=[xt.r])
        rms_rows(xt, 128, D, ubs[pb_i % 2])
        transpose_rows(ubs[pb_i % 2], 128, uTs[pb_i % 2], 0, (cols[:, CG_PREMIX, :], cols.r))

    NP0 = NOTH // 128
    p0_front(0)
    for pb_i in range(NP0):
        if pb_i == 0:
            bf = (13, 14)
        else:
            j = (pb_i - 1) // 16
            bf = (16 + 2 * j, 17 + 2 * j)
        S0.uT = uTs[pb_i % 2]
        decay_z(S0, bf)
        for c in range(6):
            pb = nbank()
            for kt in range(KT):
                P.op("pe", lambda e, kt=kt: e.matmul(out=pb[:], lhsT=S0.uT[:, kt, :], rhs=wkv[:, kt, c * 512:(c + 1) * 512], start=(kt == 0), stop=(kt == KT - 1)),
                     reads=[wkv_r[c], S0.uT.r], writes=[pb.r])
            if c < 2:
                P.op("dve", lambda e: e.tensor_copy(out=ktm0[:, c * 512:(c + 1) * 512], in_=pb[:]), reads=[pb.r], writes=[ktm0.r])
            else:
                P.op("act", lambda e: e.activation(out=vtm0[:, (c - 2) * 512:(c - 1) * 512], in_=pb[:], func=AF.Copy), reads=[pb.r], writes=[vtm0.r])
        if pb_i + 1 < NP0:
            p0_front(pb_i + 1)
        decay_tot(S0)
        P.op("act", lambda e: e.activation(out=expP[:], in_=Pacc[:], func=AF.Exp), reads=[Pacc.r], writes=[expP.r])
        tm_decays(S0)
        P.op("dve", lambda e: e.scalar_tensor_tensor(out=kendf0[:], in0=ktm0[:], scalar=flags[:, bf[0]:bf[0] + 1], in1=S0.Gb[0][:], op0=ALU.mult, op1=ALU.mult),
             reads=[ktm0.r, flags.r, S0.Gb[0].r], writes=[kendf0.r])
        P.op("dve", lambda e: e.scalar_tensor_tensor(out=kdbtm0[:], in0=ktm0[:], scalar=flags[:, bf[1]:bf[1] + 1], in1=S0.Gb[1][:], op0=ALU.mult, op1=ALU.mult),
             reads=[ktm0.r, flags.r, S0.Gb[1].r], writes=[kdbtm0.r])
        for kt in range(8):
            h = kt // 2
            pu = nbank()
            P.op("pe", lambda e: e.matmul(out=pu[:], lhsT=kendf0[:, kt * 128:(kt + 1) * 128], rhs=vtm0[:, h * 512:(h + 1) * 512], start=True, stop=True),
                 reads=[kendf0.r, vtm0.r], writes=[pu.r])
            P.op("dve", lambda e: e.scalar_tensor_tensor(out=Sf[:, kt, :], in0=Sf[:, kt, :], scalar=S0.Dcols[:, kt:kt + 1], in1=pu[:], op0=ALU.mult, op1=ALU.add),
                 reads=[Sf.rk[kt], S0.Dcols.r, pu.r], writes=[Sf.rk[kt]])
            pu2 = nbank()
            P.op("pe", lambda e: e.matmul(out=pu2[:], lhsT=kdbtm0[:, kt * 128:(kt + 1) * 128], rhs=vtm0[:, h * 512:(h + 1) * 512], start=True, stop=True),
                 reads=[kdbtm0.r, vtm0.r], writes=[pu2.r])
            P.op("dve", lambda e: e.scalar_tensor_tensor(out=Sba[:, kt, :], in0=pu2[:], scalar=expP[:, 8 + kt:9 + kt], in1=Sba[:, kt, :], op0=ALU.mult, op1=ALU.add),
                 reads=[Sba.rk[kt], expP.r, pu2.r], writes=[Sba.rk[kt]])
        P.op("dve", lambda e: e.tensor_tensor(out=Pacc[:], in0=Pacc[:], in1=pmisc[:, 256:272], op=ALU.add), reads=[Pacc.r, pmisc.r], writes=[Pacc.r])
    S0.uT = uTs[0]
    for b in range(NB):
        P.dma("sp", xt[:], xin[b * 128:(b + 1) * 128, :], writes=[xt.r])
        rms_rows(xt, 128, D, ub)
        transpose_rows(ub, 128, S0.uT, 0, ((gcol0, gcol0.r) if b == 0 else (cols[:, CG_PREMIX, :], cols.r)))
        for c in range(2, 6):
            pb = nbank()
            for kt in range(KT):
                P.op("pe", lambda e, kt=kt: e.matmul(out=pb[:], lhsT=S0.uT[:, kt, :], rhs=wkv[:, kt, c * 512:(c + 1) * 512], start=(kt == 0), stop=(kt == KT - 1)),
                     reads=[wkv_r[c], S0.uT.r], writes=[pb.r])
            P.op("act", lambda e: e.activation(out=vtm0[:, (c - 2) * 512:(c - 1) * 512], in_=pb[:], func=AF.Copy), reads=[pb.r], writes=[vtm0.r])
        P.dma("sp", s_v[b], vtm0[:], reads=[vtm0.r], writes=[r_scr["v"]])
    P.barrier()
    es0.close()
    nc.pop_named_scope('phase0')
    nc.push_named_scope('sweep1a')

    es = ExitStack()

    def sb1(name, shape, dt=F32):
        return T(es.enter_context(nc.sbuf_tensor("t_" + name, list(shape), dt)), name)

    alloc_wbuf(sb1)
    S1 = NS()
    uT = S1.uT = sb1("uT", [128, 16, 128], BF16)
    S1.wz = sb1("wz", [128, 16, 32], BF16)
    for qd in range(4):
        P.dma("pool", S1.wz[:, qd * 4:(qd + 1) * 4, :], wz_l[:, qd * 4:(qd + 1) * 4, :], writes=[S1.wz.r])
    S1.wa2 = [sb1("wa2f", [16, 1024]), sb1("wa2b", [16, 1024])]
    S1.ba = [sb1("baf", [1, 1024]), sb1("bab", [1, 1024])]
    P.dma("sp", S1.wa2[0][:], w_a2_f, writes=[S1.wa2[0].r]); P.dma("sp", S1.wa2[1][:], w_a2_b, writes=[S1.wa2[1].r])
    P.dma("sp", S1.ba[0][:], b_a_f, writes=[S1.ba[0].r]); P.dma("sp", S1.ba[1][:], b_a_b, writes=[S1.ba[1].r])
    S1.ones1 = sb1("ones1", [1, 128])
    P.op("pool", lambda e: e.memset(S1.ones1[:], 1.0), writes=[S1.ones1.r])
    qT = sb1("qT", [128, 8, 128]); kT = sb1("kT", [128, 8, 128])
    ktm = sb1("ktm", [128, 1024]); vtm = sb1("vtm", [128, D], BF16)
    S1.zT = [sb1("zfT", [16, 128]), sb1("zbT", [16, 128])]
    spl = S1.spl = [sb1("spf", [128, 1024]), sb1("spb", [128, 1024])]
    Eb = [sb1("E0", [128, 8, 128]), sb1("E1", [128, 8, 128])]
    Gb = S1.Gb = [sb1("G0", [128, 1024]), sb1("G1", [128, 1024])]
    qdf = sb1("qdf", [128, 8, 128], BF16); kdf = sb1("kdf", [128, 8, 128], BF16)
    qdb = sb1("qdb", [128, 8, 128], BF16); kdb = sb1("kdb", [128, 8, 128], BF16)
    kendf = sb1("kendf", [128, 1024], BF16); kdbtm = sb1("kdbtm", [128, 1024], BF16)
    st1 = sb1("st1", [128, 128]); st2 = sb1("st2", [128, 128]); ST = sb1("ST", [128, 128], BF16)
    Sfb = sb1("Sfb", [128, 8, 512], BF16)
    opart = sb1("opart", [128, D])
    Dcols = S1.Dcols = sb1("Dcols", [128, 16])
    Dball = sb1("Dball", [128, NB, 8])
    Sfb.rk = [Res("Sfbk") for _ in range(8)]
    P.op("act", lambda e: e.activation(out=Sfb[:], in_=Sf[:], func=AF.Copy), reads=[Sf.rk], writes=[Sfb.rk])

    for b in range(NB):
        P.dma("sp", xt[:], xin[b * 128:(b + 1) * 128, :], writes=[xt.r])
        rms_rows(xt, 128, D, ub)
        transpose_rows(ub, 128, uT, 0, ((gcol0, gcol0.r) if b == 0 else (cols[:, CG_PREMIX, :], cols.r)))
        decay_z(S1, (12, 12) if b == 0 else None)
        P.dma("sp", vtm[:], s_v[b], reads=[r_scr["v"]], writes=[vtm.r])
        for c in range(4):
            wb = load_w(w_in_l[c])
            if c < 4:
                pb = nbank()
                for n in range(4):
                    for kt in range(KT):
                        P.op("pe", lambda e, n=n, kt=kt: e.matmul(out=pb[:, n * 128:(n + 1) * 128], lhsT=wb[:, kt, n * 128:(n + 1) * 128], rhs=uT[:, kt, :],
                                                                  start=(kt == 0), stop=(kt == KT - 1)),
                             reads=[wb.rq[kt // 4], uT.r], writes=[pb.r])
                dst = qT if c < 2 else kT
                cc = c % 2
                P.op("act", lambda e: e.activation(out=dst[:, cc * 4:(cc + 1) * 4, :], in_=pb[:].rearrange("p (a b) -> p a b", a=4), func=AF.Copy,
                                                   scale=(1.0 / 16 if c < 2 else 1.0)),
                     reads=[pb.r], writes=[dst.r])
            if c >= 2:
                pb = nbank()
                for kt in range(KT):
                    P.op("pe", lambda e, kt=kt: e.matmul(out=pb[:], lhsT=uT[:, kt, :], rhs=wb[:, kt, :], start=(kt == 0), stop=(kt == KT - 1)),
                         reads=[wb.rq[kt // 4], uT.r], writes=[pb.r])
                if c < 4:
                    P.op("dve", lambda e: e.tensor_copy(out=ktm[:, (c - 2) * 512:(c - 1) * 512], in_=pb[:]), reads=[pb.r], writes=[ktm.r])
                else:
                    P.op("dve", lambda e: e.tensor_copy(out=vtm[:, (c - 4) * 512:(c - 3) * 512], in_=pb[:]), reads=[pb.r], writes=[vtm.r])
        decay_tot(S1)
        if b >= 1:
            P.op("dve", lambda e: e.tensor_copy(out=Dball[:, b, :], in_=Dcols[:, 8:16]), reads=[Dcols.r], writes=[Dball.r])
        pa, pb_ = nbank(), nbank()
        for kt in range(8):
            tgt = pa if kt < 4 else pb_
            P.op("pe", lambda e, kt=kt, tgt=tgt: e.matmul(out=tgt[:, (kt % 4) * 128:(kt % 4 + 1) * 128], lhsT=spl[0][:, kt * 128:(kt + 1) * 128],
                                                          rhs=masks[:, M_TRII_S, :], start=True, stop=True),
                 reads=[spl[0].r, masks.r], writes=[tgt.r])
        for half, tgt in enumerate((pa, pb_)):
            P.op("act", lambda e, half=half, tgt=tgt: e.activation(out=Eb[0][:, half * 4:(half + 1) * 4, :], in_=tgt[:].rearrange("p (a b) -> p a b", a=4), func=AF.Exp),
                 reads=[tgt.r], writes=[Eb[0].r])
            P.op("act", lambda e, half=half, tgt=tgt: e.activation(out=Eb[1][:, half * 4:(half + 1) * 4, :], in_=tgt[:].rearrange("p (a b) -> p a b", a=4), func=AF.Exp, scale=-1.0),
                 reads=[tgt.r], writes=[Eb[1].r])
        P.op("dve", lambda e: e.tensor_tensor(out=qdf[:], in0=qT[:], in1=Eb[0][:], op=ALU.mult), reads=[qT.r, Eb[0].r], writes=[qdf.r])
        P.op("pool", lambda e: e.tensor_tensor(out=kdf[:], in0=kT[:], in1=Eb[1][:], op=ALU.mult), reads=[kT.r, Eb[1].r], writes=[kdf.r])
        pa, pb_ = nbank(), nbank()
        for kt in range(8):
            tgt = pa if kt < 4 else pb_
            P.op("pe", lambda e, kt=kt, tgt=tgt: e.matmul(out=tgt[:, (kt % 4) * 128:(kt % 4 + 1) * 128], lhsT=spl[1][:, kt * 128:(kt + 1) * 128],
                                                          rhs=masks[:, M_TRIE_S, :], start=True, stop=True),
                 reads=[spl[1].r, masks.r], writes=[tgt.r])
        for half, tgt in enumerate((pa, pb_)):
            P.op("act", lambda e, half=half, tgt=tgt: e.activation(out=Eb[0][:, half * 4:(half + 1) * 4, :], in_=tgt[:].rearrange("p (a b) -> p a b", a=4), func=AF.Exp, scale=-1.0),
                 reads=[tgt.r], writes=[Eb[0].r])
            P.op("act", lambda e, half=half, tgt=tgt: e.activation(out=Eb[1][:, half * 4:(half + 1) * 4, :], in_=tgt[:].rearrange("p (a b) -> p a b", a=4), func=AF.Exp),
                 reads=[tgt.r], writes=[Eb[1].r])
        P.op("dve", lambda e: e.tensor_tensor(out=qdb[:], in0=qT[:], in1=Eb[0][:], op=ALU.mult), reads=[qT.r, Eb[0].r], writes=[qdb.r])
        P.op("pool", lambda e: e.tensor_tensor(out=kdb[:], in0=kT[:], in1=Eb[1][:], op=ALU.mult), reads=[kT.r, Eb[1].r], writes=[kdb.r])
        tm_decays(S1)
        P.op("dve", lambda e: e.tensor_tensor(out=kendf[:], in0=ktm[:], in1=Gb[0][:], op=ALU.mult), reads=[ktm.r, Gb[0].r], writes=[kendf.r])
        P.op("pool", lambda e: e.tensor_tensor(out=kdbtm[:], in0=ktm[:], in1=Gb[1][:], op=ALU.mult), reads=[ktm.r, Gb[1].r], writes=[kdbtm.r])
        for h in range(4):
            if b >= 1:
                for di, (kd, qd) in enumerate(((kdf, qdf), (kdb, qdb))):
                    for j in range(2):
                        kt = 2 * h + j
                        P.op("pe", lambda e, kt=kt, j=j: e.matmul(out=pmisc[:, di * 128:(di + 1) * 128], lhsT=kd[:, kt, :], rhs=qd[:, kt, :], start=(j == 0), stop=(j == 1)),
                             reads=[kd.r, qd.r], writes=[pmisc.r])
                P.op("dve", lambda e: e.tensor_tensor(out=st1[:], in0=pmisc[:, 0:128], in1=masks[:, M_TRII, :], op=ALU.mult), reads=[pmisc.r, masks.r], writes=[st1.r])
                P.op("dve", lambda e: e.tensor_tensor(out=st2[:], in0=pmisc[:, 128:256], in1=masks[:, M_MASKB, :], op=ALU.mult), reads=[pmisc.r, masks.r], writes=[st2.r])
                P.op("pool", lambda e: e.tensor_tensor(out=ST[:], in0=st1[:], in1=st2[:], op=ALU.add), reads=[st1.r, st2.r], writes=[ST.r])
                po = nbank()
                P.op("pe", lambda e: e.matmul(out=po[:], lhsT=ST[:], rhs=vtm[:, h * 512:(h + 1) * 512], start=True, stop=False),
                     reads=[ST.r, vtm.r], writes=[po.r])
                for j in range(2):
                    kt = 2 * h + j
                    P.op("pe", lambda e, kt=kt, j=j: e.matmul(out=po[:], lhsT=qdf[:, kt, :], rhs=Sfb[:, kt, :], start=False, stop=(j == 1)),
                         reads=[qdf.r, Sfb.rk[kt]], writes=[po.r])
                P.op("act", lambda e: e.activation(out=opart[:, h * 512:(h + 1) * 512], in_=po[:], func=AF.Copy), reads=[po.r], writes=[opart.r])
            for j in range(2):
                kt = 2 * h + j
                pu = nbank()
                P.op("pe", lambda e, kt=kt: e.matmul(out=pu[:], lhsT=kendf[:, kt * 128:(kt + 1) * 128], rhs=vtm[:, h * 512:(h + 1) * 512], start=True, stop=True),
                     reads=[kendf.r, vtm.r], writes=[pu.r])
                P.op("dve", lambda e, kt=kt: e.scalar_tensor_tensor(out=Sf[:, kt, :], in0=Sf[:, kt, :], scalar=Dcols[:, kt:kt + 1], in1=pu[:], op0=ALU.mult, op1=ALU.add),
                     reads=[Sf.rk[kt], Dcols.r, pu.r], writes=[Sf.rk[kt]])
                P.op("pool", lambda e, kt=kt: e.tensor_copy(out=Sfb[:, kt, :], in_=Sf[:, kt, :]), reads=[Sf.rk[kt]], writes=[Sfb.rk[kt]])
        if b >= 1:
            P.dma("sp", s_qdb[b].rearrange("p (a b) -> p a b", a=8), qdb[:], reads=[qdb.r], writes=[r_scr["qdb"]])
            P.dma("sp", s_kdb[b], kdbtm[:], reads=[kdbtm.r], writes=[r_scr["kdb"]])
            P.dma("sp", s_o[b], opart[:], reads=[opart.r], writes=[r_scr["o"]])

    nc.pop_named_scope('sweep1a')
    nc.push_named_scope('sweep1b')
    Sb = Sba
    Sbb = Sfb
    ofull = opart
    on_bf = ub
    onT = uT
    for b in range(NB - 1, 0, -1):
        P.dma("sp", qdb[:], s_qdb[b].rearrange("p (a b) -> p a b", a=8), reads=[r_scr["qdb"]], writes=[qdb.r])
        P.dma("sp", kdbtm[:], s_kdb[b], reads=[r_scr["kdb"]], writes=[kdbtm.r])
        P.dma("sp", vtm[:], s_v[b], reads=[r_scr["v"]], writes=[vtm.r])
        P.dma("sp", xt[:], s_o[b], reads=[r_scr["o"]], writes=[xt.r])
        for kt in range(8):
            P.op("dve", lambda e, kt=kt: e.tensor_scalar(out=Sb[:, kt, :], in0=Sb[:, kt, :], scalar1=Dball[:, b, kt:kt + 1], scalar2=None, op0=ALU.mult),
                 reads=[Sb.rk[kt], Dball.r], writes=[Sb.rk[kt]])
            P.op("pool", lambda e, kt=kt: e.tensor_copy(out=Sbb[:, kt, :], in_=Sb[:, kt, :]), reads=[Sb.rk[kt]], writes=[Sbb.rk[kt]])
        for h in range(4):
            po = nbank()
            for j in range(2):
                kt = 2 * h + j
                P.op("pe", lambda e, kt=kt, j=j: e.matmul(out=po[:], lhsT=qdb[:, kt, :], rhs=Sbb[:, kt, :], start=(j == 0), stop=(j == 1)),
                     reads=[qdb.r, Sbb.rk[kt]], writes=[po.r])
            P.op("dve", lambda e: e.tensor_tensor(out=ofull[:, h * 512:(h + 1) * 512], in0=po[:], in1=xt[:, h * 512:(h + 1) * 512], op=ALU.add),
                 reads=[po.r, xt.r], writes=[ofull.r])
            for j in range(2):
                kt = 2 * h + j
                pu = nbank()
                P.op("pe", lambda e, kt=kt: e.matmul(out=pu[:], lhsT=kdbtm[:, kt * 128:(kt + 1) * 128], rhs=vtm[:, h * 512:(h + 1) * 512], start=True, stop=True),
                     reads=[kdbtm.r, vtm.r], writes=[pu.r])
                P.op("dve", lambda e, kt=kt: e.tensor_tensor(out=Sb[:, kt, :], in0=Sb[:, kt, :], in1=pu[:], op=ALU.add), reads=[Sb.rk[kt], pu.r], writes=[Sb.rk[kt]])
        rms_rows(ofull, 128, 512, on_bf, nh=4)
        transpose_rows(on_bf, 128, onT, 0, None)
        P.dma("sp", s_onT.rearrange("(vt p) t -> p vt t", p=128)[:, :, b * 128:(b + 1) * 128], onT[:], reads=[onT.r], writes=[r_scr["onT"]])
    P.barrier()
    es.close()
    esS.close()
    nc.pop_named_scope('sweep1b')
    nc.push_named_scope('sweep2')

    TB = 512
    NTB = 4
    W = TB + 32
    alloc_wbuf(sb)
    wdw = sb("wdw", [128, 16, 31])
    P.dma("sp", wdw[:], wdw_d, writes=[wdw.r])
    uW = sb("uW", [128, 16, W], BF16)
    htmp = sb("htmp", [128, TB])
    OFF = dict(A=0, B=16384, B2=32768, C=34816, Dd=67584, Ee=83968, Ff=100352, END=133120)
    arena = nc.alloc_sbuf_tensor("t_arena2", [128, OFF["END"] // 4], F32)
    RR = {k: Res("reg" + k) for k in ("A", "B", "B2", "C", "Dd", "Ee", "Ff", "Ff2")}

    class V:
        def __init__(self, start, nbytes, dt, regs, pattern=None, **kw):
            ap = arena[:, start // 4:(start + nbytes) // 4]
            if dt == BF16:
                ap = ap.bitcast(BF16)
            if pattern is not None:
                ap = ap.rearrange(pattern, **kw)
            self.t = ap
            self.r = [RR[k] for k in regs]

        def __getitem__(self, k):
            return self.t[k]

    gT = V(OFF["A"], 16 * W * 4, F32, ("A", "B", "B2"), "p (k t) -> p k t", k=16)
    yT = V(OFF["C"], 32768, F32, ("C",), "p (k t) -> p k t", k=16)
    yact = V(OFF["A"], 16384, BF16, ("A",), "p (k t) -> p k t", k=16)
    sgb = V(OFF["Ee"], 16384, BF16, ("Ee",), "p (k t) -> p k t", k=16)
    mb = V(OFF["C"], 32768, F32, ("C",), "p (k t) -> p k t", k=16)
    ract = V(OFF["Ff"] + 10752, 16384, BF16, ("Ff2",), "p (k t) -> p k t", k=16)
    ogT = V(OFF["B"], 16384, BF16, ("B",), "p (k t) -> p k t", k=16)
    sga = V(OFF["Dd"], 16384, BF16, ("Dd",), "p (k t) -> p k t", k=16)
    mixT = V(OFF["Ee"], 16384, BF16, ("Ee",), "p (k t) -> p k t", k=16)
    sg = V(OFF["Ff"], W * 4, F32, ("Ff",))
    ysq = V(OFF["Ff"] + 2304, 2048, F32, ("Ff",))
    mu = V(OFF["Ff"] + 4352, 2048, F32, ("Ff",))
    var = V(OFF["Ff"] + 6400, 2048, F32, ("Ff",))
    rs = V(OFF["Ff"] + 8448, 2048, F32, ("Ff",))
    mix2 = V(OFF["A"], 32768, F32, ("A", "B"), "p (k t) -> p k t", k=4)
    h1 = V(OFF["C"], 32768, F32, ("C",), "p (k t) -> p k t", k=4)
    hidT = V(OFF["Dd"], 65536, BF16, ("Dd", "Ee", "Ff", "Ff2"), "p (k t) -> p k t", k=64)
    ftm = mix2
    u2T = uW

    def fm_proj(src_ap_fn, nchunks, rhs, col0, ncols, evac):
        parts = [(0, ncols)] if ncols <= 512 else [(0, ncols // 2), (ncols // 2, ncols - ncols // 2)]
        for c in range(nchunks):
            wb = load_w(src_ap_fn(c))
            for n in range(4):
                for (c0, nn) in parts:
                    pb = nbank()
                    for kt in range(KT):
                        P.op("pe", lambda e, n=n, kt=kt: e.matmul(out=pb[:, 0:nn], lhsT=wb[:, kt, n * 128:(n + 1) * 128], rhs=rhs[:, kt, col0 + c0:col0 + c0 + nn],
                                                                  start=(kt == 0), stop=(kt == KT - 1)),
                             reads=[wb.rq[kt // 4], rhs.r], writes=[pb.r])
                    evac(c * 4 + n, pb, c0, nn)

    yv = yout.rearrange("(s k p) d -> s p k d", k=NTB, p=128)

    def prep_norm(sb_i, m):
        rr = 112 + TB * sb_i
        n = 128 if m < NTB else 32
        P.dma("sp", xt[0:n, :], xin[rr + 128 * m:rr + 128 * m + n, :], writes=[xt.r])
        rms_rows(xt, n, D, ub)

    def prep_tr(m):
        n = 128 if m < NTB else 32
        transpose_rows(ub, n, uW, 128 * m, (cols[:, CG_PREMIX, :], cols.r))

    for sbk in range(NTOK // TB):
        r0 = 112 + TB * sbk
        t0 = r0 + 16
        if sbk == 0:
            for m in range(NTB + 1):
                prep_norm(0, m)
                prep_tr(m)
        def ev_a(t, pb, c0, n):
            P.op("act", lambda e: e.activation(out=gT[:, t, c0:c0 + n], in_=pb[:, 0:n], func=AF.Copy), reads=[pb.r], writes=[gT.r])
        fm_proj(lambda c: w_in_l[12 + c], 4, uW, 0, W, ev_a)

        def ev_gt(t, pb, c0, n):
            P.op("act", lambda e: e.activation(out=sg[:, c0:c0 + n], in_=pb[:, 0:n], func=AF.Sigmoid), reads=[pb.r], writes=[sg.r])
            P.op("dve", lambda e: e.tensor_tensor(out=gT[:, t, c0:c0 + n], in0=gT[:, t, c0:c0 + n], in1=sg[:, c0:c0 + n], op=ALU.mult), reads=[gT.r, sg.r], writes=[gT.r])
        fm_proj(lambda c: w_in_l[16 + c], 4, uW, 0, W, ev_gt)
        yct = [Res("yct") for _ in range(16)]
        for ct in range(16):
            P.op("dve", lambda e, ct=ct: e.tensor_scalar(out=yT[:, ct, :], in0=gT[:, ct, 1:1 + TB], scalar1=wdw[:, ct, 0:1], scalar2=cols[:, C_BDW, ct:ct + 1],
                                                         op0=ALU.mult, op1=ALU.add),
                 reads=[gT.r, wdw.r, cols.r], writes=[yT.r, yct[ct]])
        for tap in range(1, 31):
            for ct in range(16):
                P.op("dve", lambda e, ct=ct, tap=tap: e.scalar_tensor_tensor(out=yT[:, ct, :], in0=gT[:, ct, tap + 1:tap + 1 + TB], scalar=wdw[:, ct, tap:tap + 1],
                                                                             in1=yT[:, ct, :], op0=ALU.mult, op1=ALU.add),
                     reads=[gT.r, wdw.r, yct[ct]], writes=[yT.r, yct[ct]])
        def ev_gb(t, pb, c0, n):
            P.op("act", lambda e: e.activation(out=sgb[:, t, :], in_=pb[:, 0:n], func=AF.Sigmoid), reads=[pb.r], writes=[sgb.r])
        fm_proj(lambda c: w_in_l[24 + c], 4, uW, 16, TB, ev_gb)

        def ev_ga(t, pb, c0, n):
            P.op("act", lambda e: e.activation(out=sga[:, t, :], in_=pb[:, 0:n], func=AF.Sigmoid), reads=[pb.r], writes=[sga.r])
        fm_proj(lambda c: w_in_l[20 + c], 4, uW, 16, TB, ev_ga)

        def ev_r(t, pb, c0, n):
            P.op("act", lambda e: e.activation(out=ract[:, t, :], in_=pb[:, 0:n], func=AF.Silu), reads=[pb.r], writes=[ract.r])
        fm_proj(lambda c: w_in_l[8 + c], 4, uW, 16, TB, ev_r)
        ps1, ps2 = nbank(), nbank()
        for ct in range(16):
            P.op("pe", lambda e, ct=ct: e.matmul(out=ps1[:], lhsT=ones[:], rhs=yT[:, ct, :], start=(ct == 0), stop=(ct == 15)),
                 reads=[ones.r, yT.r], writes=[ps1.r])
        for ct in range(16):
            P.op("act", lambda e, ct=ct: e.activation(out=ysq[:], in_=yT[:, ct, :], func=AF.Square), reads=[yT.r], writes=[ysq.r])
            P.op("pe", lambda e, ct=ct: e.matmul(out=ps2[:], lhsT=ones[:], rhs=ysq[:], start=(ct == 0), stop=(ct == 15)),
                 reads=[ones.r, ysq.r], writes=[ps2.r])
        P.op("dve", lambda e: e.tensor_scalar(out=mu[:], in0=ps1[:], scalar1=1.0 / D, scalar2=None, op0=ALU.mult), reads=[ps1.r], writes=[mu.r])
        P.op("dve", lambda e: e.tensor_tensor(out=var[:], in0=mu[:], in1=mu[:], op=ALU.mult), reads=[mu.r], writes=[var.r])
        P.op("dve", lambda e: e.scalar_tensor_tensor(out=var[:], in0=ps2[:], scalar=1.0 / D, in1=var[:], op0=ALU.mult, op1=ALU.subtract),
             reads=[ps2.r, var.r], writes=[var.r])
        P.op("act", lambda e: e.activation(out=rs[:], in_=var[:], func=AF.Sqrt, scale=1.0, bias=EPS), reads=[var.r], writes=[rs.r])
        P.op("dve", lambda e: e.reciprocal(out=rs[:], in_=rs[:]), reads=[rs.r], writes=[rs.r])
        for ct in range(16):
            P.op("dve", lambda e, ct=ct: e.tensor_tensor(out=yT[:, ct, :], in0=yT[:, ct, :], in1=mu[:], op=ALU.subtract), reads=[yT.r, mu.r], writes=[yT.r])
            P.op("dve", lambda e, ct=ct: e.tensor_tensor(out=yT[:, ct, :], in0=yT[:, ct, :], in1=rs[:], op=ALU.mult), reads=[yT.r, rs.r], writes=[yT.r])
        for ct in range(16):
            P.op("act", lambda e, ct=ct: e.activation(out=yact[:, ct, :], in_=yT[:, ct, :], func=AF.Silu, scale=cols[:, C_LNG, ct:ct + 1], bias=cols[:, C_LNB, ct:ct + 1]),
                 reads=[yT.r, cols.r], writes=[yact.r])
        def ev_yb(t, pb, c0, n):
            P.op("dve", lambda e: e.scalar_tensor_tensor(out=mb[:, t, :], in0=pb[:, 0:n], scalar=cols[:, C_BCO, t:t + 1], in1=sgb[:, t, :], op0=ALU.add, op1=ALU.mult),
                 reads=[pb.r, cols.r, sgb.r], writes=[mb.r])
        fm_proj(lambda c: w_conv_o_l[c], 4, yact, 0, TB, ev_yb)
        P.dma("sp", ogT[:], s_onT.rearrange("(vt p) t -> p vt t", p=128)[:, :, t0:t0 + TB], reads=[r_scr["onT"]], writes=[ogT.r])
        for vt in range(16):
            P.op("dve", lambda e, vt=vt: e.scalar_tensor_tensor(out=ogT[:, vt, :], in0=ogT[:, vt, :], scalar=cols[:, C_GGLA, vt:vt + 1], in1=ract[:, vt, :], op0=ALU.mult, op1=ALU.mult),
                 reads=[ogT.r, cols.r, ract.r], writes=[ogT.r])

        def ev_ya(t, pb, c0, n):
            P.op("dve", lambda e: e.tensor_tensor(out=htmp[:], in0=pb[:, 0:n], in1=sga[:, t, :], op=ALU.mult), reads=[pb.r, sga.r], writes=[htmp.r])
            P.op("dve", lambda e: e.tensor_tensor(out=mixT[:, t, :], in0=htmp[:], in1=mb[:, t, :], op=ALU.add), reads=[htmp.r, mb.r], writes=[mixT.r])
        fm_proj(lambda c: w_gla_o_l[c], 4, ogT, 0, TB, ev_ya)
        for c in range(4):
            wb = load_w(w_out_l[c])
            for tb in range(NTB):
                pb = nbank()
                for kt in range(KT):
                    P.op("pe", lambda e, kt=kt: e.matmul(out=pb[:], lhsT=mixT[:, kt, tb * 128:(tb + 1) * 128], rhs=wb[:, kt, :], start=(kt == 0), stop=(kt == KT - 1)),
                         reads=[wb.rq[kt // 4], mixT.r], writes=[pb.r])
                P.op("act", lambda e: e.activation(out=mix2[:, tb, c * 512:(c + 1) * 512], in_=pb[:], func=AF.Copy), reads=[pb.r], writes=[mix2.r])
        P.dma("sp", h1[:], xin[t0:t0 + TB, :].rearrange("(k p) d -> p k d", p=128), writes=[h1.r])
        P.dma("sp", xt[:], gbc_d[0], writes=[xt.r])
        for tb in range(NTB):
            for h in range(1):
                P.op("act", lambda e: e.activation(out=junk[:], in_=mix2[:, tb, :], func=AF.Square, accum_out=ssq[:, 0:1]), reads=[mix2.r], writes=[junk.r, ssq.r])
            P.op("act", lambda e: e.activation(out=rstd[:, 0:1], in_=ssq[:, 0:1], func=AF.Sqrt, scale=1.0 / D, bias=EPS), reads=[ssq.r], writes=[rstd.r])
            P.op("dve", lambda e: e.reciprocal(out=rstd[:, 0:1], in_=rstd[:, 0:1]), reads=[rstd.r], writes=[rstd.r])
            P.op("dve", lambda e: e.scalar_tensor_tensor(out=mix2[:, tb, :], in0=mix2[:, tb, :], scalar=rstd[:, 0:1], in1=xt[:], op0=ALU.mult, op1=ALU.mult),
                 reads=[mix2.r, rstd.r, xt.r], writes=[mix2.r])
            P.op("dve", lambda e: e.tensor_tensor(out=h1[:, tb, :], in0=h1[:, tb, :], in1=mix2[:, tb, :], op=ALU.add), reads=[h1.r, mix2.r], writes=[h1.r])
        for tb in range(NTB):
            P.op("act", lambda e: e.activation(out=junk[:], in_=h1[:, tb, :], func=AF.Square, accum_out=ssq[:, 0:1]), reads=[h1.r], writes=[junk.r, ssq.r])
            P.op("act", lambda e: e.activation(out=rstd[:, 0:1], in_=ssq[:, 0:1], func=AF.Sqrt, scale=1.0 / D, bias=EPS), reads=[ssq.r], writes=[rstd.r])
            P.op("dve", lambda e: e.reciprocal(out=rstd[:, 0:1], in_=rstd[:, 0:1]), reads=[rstd.r], writes=[rstd.r])
            P.op("dve", lambda e: e.tensor_scalar(out=ub[:], in0=h1[:, tb, :], scalar1=rstd[:, 0:1], scalar2=None, op0=ALU.mult), reads=[h1.r, rstd.r], writes=[ub.r])
            transpose_rows(ub, 128, u2T, 128 * tb, (cols[:, CG_PREMLP, :], cols.r))

        def ev_up(t, pb, c0, n):
            P.op("act", lambda e: e.activation(out=htmp[:], in_=pb[:, 0:n], func=AF.Relu), reads=[pb.r], writes=[htmp.r])
            P.op("dve", lambda e: e.tensor_tensor(out=hidT[:, t, :], in0=htmp[:], in1=htmp[:], op=ALU.mult), reads=[htmp.r], writes=[hidT.r])
        fm_proj(lambda c: w_up_l[c], 16, u2T, 0, TB, ev_up)
        nxt = sbk + 1 < NTOK // TB
        for c in range(4):
            if nxt:
                prep_norm(sbk + 1, c)
            pbs = [nbank() for _ in range(NTB)]
            for kg in range(4):
                wb = load_w(w_down_l[kg * 4 + c])
                for tb in range(NTB):
                    for kt in range(KT):
                        P.op("pe", lambda e, kt=kt: e.matmul(out=pbs[tb][:], lhsT=hidT[:, kg * 16 + kt, tb * 128:(tb + 1) * 128], rhs=wb[:, kt, :],
                                                             start=(kg == 0 and kt == 0), stop=(kg == 3 and kt == KT - 1)),
                             reads=[wb.rq[kt // 4], hidT.r], writes=[pbs[tb].r])
            for tb in range(NTB):
                P.op("act", lambda e: e.activation(out=ftm[:, tb, c * 512:(c + 1) * 512], in_=pbs[tb][:], func=AF.Copy), reads=[pbs[tb].r], writes=[ftm.r])
            if nxt:
                prep_tr(c)
        if nxt:
            prep_norm(sbk + 1, NTB)
            prep_tr(NTB)
        P.dma("sp", xt[:], gbc_d[1], writes=[xt.r])
        for tb in range(NTB):
            P.op("act", lambda e: e.activation(out=junk[:], in_=ftm[:, tb, :], func=AF.Square, accum_out=ssq[:, 0:1]), reads=[ftm.r], writes=[junk.r, ssq.r])
            P.op("act", lambda e: e.activation(out=rstd[:, 0:1], in_=ssq[:, 0:1], func=AF.Sqrt, scale=1.0 / D, bias=EPS), reads=[ssq.r], writes=[rstd.r])
            P.op("dve", lambda e: e.reciprocal(out=rstd[:, 0:1], in_=rstd[:, 0:1]), reads=[rstd.r], writes=[rstd.r])
            P.op("dve", lambda e: e.scalar_tensor_tensor(out=ftm[:, tb, :], in0=ftm[:, tb, :], scalar=rstd[:, 0:1], in1=xt[:], op0=ALU.mult, op1=ALU.mult),
                 reads=[ftm.r, rstd.r, xt.r], writes=[ftm.r])
            P.op("dve", lambda e: e.tensor_tensor(out=h1[:, tb, :], in0=h1[:, tb, :], in1=ftm[:, tb, :], op=ALU.add), reads=[h1.r, ftm.r], writes=[h1.r])
        P.dma("sp", yv[sbk], h1[:], reads=[h1.r], writes=[r_scr["yout"]])

    P.finish("sp", [r_scr["yout"]])
    nc.pop_named_scope('sweep2')
    return nc


_NC_CACHE = {}


def kernel(x_prompt, x_sample, meta_tokens, g_pre_mix, w_in, w_a2_f, b_a_f, w_a2_b, b_a_b,
           g_gla, w_gla_o, w_dw, b_dw, ln_g, ln_b, w_conv_o, b_conv_o, w_out, g_post_mix,
           g_pre_mlp, w_up, w_down, g_post_mlp):
    f32 = np.float32
    A = lambda a: np.ascontiguousarray(np.asarray(a, dtype=f32))
    x_prompt, x_sample, meta = A(x_prompt), A(x_sample), A(meta_tokens)
    col = lambda v: np.ascontiguousarray(np.asarray(v, f32).reshape(16, 128).T)
    cols = np.zeros((128, 8, 16), f32)
    for i, v in enumerate((g_pre_mix, g_pre_mlp, b_dw, ln_g, ln_b, g_gla, b_conv_o)):
        cols[:, i, :] = col(v)
    wdw = np.ascontiguousarray(np.asarray(w_dw, f32)[0].T.reshape(16, 128, 31).transpose(1, 0, 2))
    gbc = np.ascontiguousarray(np.stack([np.broadcast_to(np.asarray(g_post_mix, f32).reshape(1, D), (128, D)),
                                         np.broadcast_to(np.asarray(g_post_mlp, f32).reshape(1, D), (128, D))]))
    ident = np.eye(128).astype(ml_dtypes.bfloat16)
    jj, ii = np.meshgrid(np.arange(128), np.arange(128), indexing="ij")
    s = f32(-1.0 / 16)
    masks = np.zeros((128, 6, 128), f32)
    masks[:, 0, :] = (jj <= ii)
    masks[:, 1, :] = (jj > ii)
    masks[:, 2, :] = (jj <= ii) * s
    masks[:, 3, :] = (jj < ii) * s
    masks[:, 4, :] = (jj > ii) * s
    masks[:, 5, :] = s
    def chunks(Wm, col_starts, width=512):
        return np.ascontiguousarray(np.stack([Wm[:, cs:cs + width].reshape(16, 128, width).transpose(1, 0, 2) for cs in col_starts]))
    Win = A(w_in)[0]
    starts = [512 * i for i in range(8)] + [C_R + 512 * i for i in range(4)] + [C_A + 512 * i for i in range(4)] + \
             [C_GT + 512 * i for i in range(4)] + [C_GA + 512 * i for i in range(4)] + [C_GB + 512 * i for i in range(4)]
    Wd = A(w_down)[0]
    shared = dict(
        w_in_l=chunks(Win, starts), wz_l=chunks(Win, [C_ZF], 32)[0],
        w_a2_f=A(w_a2_f)[0], b_a_f=A(b_a_f), w_a2_b=A(w_a2_b)[0], b_a_b=A(b_a_b),
        w_gla_o_l=chunks(A(w_gla_o)[0], [0, 512, 1024, 1536]), w_conv_o_l=chunks(A(w_conv_o)[0], [0, 512, 1024, 1536]),
        w_out_l=chunks(A(w_out)[0], [0, 512, 1024, 1536]), w_up_l=chunks(A(w_up)[0], [512 * i for i in range(16)]),
        w_down_l=np.ascontiguousarray(np.concatenate([chunks(Wd[kg * 2048:(kg + 1) * 2048], [0, 512, 1024, 1536]) for kg in range(4)])),
        cols=cols, wdw=wdw, gbc=gbc, ident=ident, masks=masks)
    in_maps = []
    xs = x_sample[0]
    for c in range(8):
        xin = np.zeros((ROWS, D), f32)
        fl = np.zeros((128, 32), f32)
        xoth = np.zeros((NOTH, D), f32)
        if c < 4:
            xin[112:128] = meta
            xin[128:2176] = x_prompt[c]
        else:
            q = c - 4
            xin[112:128] = meta if q == 0 else xs[2048 * q - 16:2048 * q]
            xin[128:2176] = xs[2048 * q:2048 * (q + 1)]
            if q < 3:
                xin[2176:2192] = xs[2048 * (q + 1):2048 * (q + 1) + 16]
            xoth[112:128] = meta
            fl[:, 13] = 1.0 if q > 0 else 0.0
            others = [o for o in range(4) if o != q]
            for j, o in enumerate(others):
                xoth[128 + 2048 * j:128 + 2048 * (j + 1)] = xs[2048 * o:2048 * (o + 1)]
                fl[:, 16 + 2 * j] = 1.0 if o < q else 0.0
                fl[:, 17 + 2 * j] = 1.0 if o > q else 0.0
        fl[:, 12] = 1.0 if c <= 4 else 0.0
        d = dict(shared)
        d["xoth"] = xoth
        d["xin"] = xin
        d["flags"] = fl
        in_maps.append(d)
    if "nc" not in _NC_CACHE:
        _NC_CACHE["nc"] = build_program()
    if TRACE:
        res = run_bass_kernel_spmd(_NC_CACHE["nc"], in_maps, core_ids=list(range(8)), trace=True)
        print("EXEC_NS", res.exec_time_ns, res.mean_exec_time_ns, res.max_exec_time_core_id)
        _NC_CACHE["res"] = res
    else:
        res = run_bass_kernel_spmd(_NC_CACHE["nc"], in_maps, core_ids=list(range(8)))
    outs = [np.asarray(r["yout"], dtype=f32) for r in res.results]
    y_prompt = np.stack(outs[:4], axis=0)
    y_sample = np.concatenate(outs[4:], axis=0)[None]
    return (y_prompt, y_sample)
```

```python
import numpy as np
import ml_dtypes
from contextlib import ExitStack
import concourse.bass as bass
import concourse.mybir as mybir
from concourse.bass_utils import run_bass_kernel_spmd

F32 = mybir.dt.float32
BF16 = mybir.dt.bfloat16
AF = mybir.ActivationFunctionType
ALU = mybir.AluOpType

D = 2048
KT = 16
NB = 17
ROWS = 112 + 16 + 2048 + 16
NTOK = 2048
NIN = 14368
C_Q, C_K, C_V, C_R, C_ZF, C_ZB, C_A, C_GT, C_GA, C_GB = 0, 1024, 2048, 4096, 6144, 6160, 6176, 8224, 10272, 12320
EPS = 1e-6
NOTH = 128 + 3 * 2048
TRACE = False
SKIP_WDMA = False


class Res:
    __slots__ = ("name", "w", "r")

    def __init__(self, name=""):
        self.name = name
        self.w = None
        self.r = []


class Prog:
    SAME_ENGINE_SYNC = True

    def __init__(self, nc, n_dma_sems=10):
        self.nc = nc
        self.eng = {"pe": nc.tensor, "act": nc.scalar, "dve": nc.vector, "pool": nc.gpsimd, "sp": nc.sync}
        self.sem = {k: nc.alloc_semaphore("s_" + k) for k in self.eng}
        self.cnt = {k: 0 for k in self.eng}
        self.waited = {k: {} for k in self.eng}
        self.dsem, self.dcnt, self.dnext = {}, {}, {}
        for q in ("sp", "act", "pool"):
            self.dsem[q] = [nc.alloc_semaphore(f"d_{q}{i}") for i in range(n_dma_sems)]
            self.dcnt[q] = [0] * n_dma_sems
            self.dnext[q] = 0

    def _wait(self, e, ev):
        if ev is None:
            return
        sem, val, src = ev
        if src == e and not (self.SAME_ENGINE_SYNC and e != "pe"):
            return
        key = id(sem)
        if self.waited[e].get(key, 0) >= val:
            return
        self.waited[e][key] = val
        self.eng[e].wait_ge(sem, val)

    def _deps(self, e, reads, writes):
        for r in reads:
            self._wait(e, r.w)
        for w in writes:
            if w.w is not None and w.w[2] != e:
                self._wait(e, w.w)
            for ev in w.r:
                if ev[2] != e:
                    self._wait(e, ev)

    def _commit(self, ev, reads, writes):
        for r in reads:
            r.r.append(ev)
            if len(r.r) > 40:
                r.r = r.r[-40:]
        for w in writes:
            w.w = ev
            w.r = []

    @staticmethod
    def _flat(xs):
        out = []
        for x in xs:
            if isinstance(x, (list, tuple)):
                out.extend(x)
            else:
                out.append(x)
        return out

    def op(self, e, fn, reads=(), writes=()):
        reads, writes = self._flat(reads), self._flat(writes)
        self._deps(e, reads, writes)
        inst = fn(self.eng[e])
        self.cnt[e] += 1
        inst.then_inc(self.sem[e], 1)
        ev = (self.sem[e], self.cnt[e], e)
        self._commit(ev, reads, writes)
        return ev

    def dma(self, q, out, in_, reads=(), writes=(), **kw):
        reads, writes = self._flat(reads), self._flat(writes)
        i = self.dnext[q]
        self.dnext[q] = (i + 1) % len(self.dsem[q])
        sem = self.dsem[q][i]
        if self.dcnt[q][i] > 0:
            self._wait(q, (sem, self.dcnt[q][i], "dma"))
        self._deps(q, reads, writes)
        inst = self.eng[q].dma_start(out=out, in_=in_, **kw)
        self.dcnt[q][i] += 16
        inst.then_inc(sem, 16)
        ev = (sem, self.dcnt[q][i], "dma")
        self._commit(ev, reads, writes)
        return ev

    def barrier(self):
        evs = [(self.sem[k], self.cnt[k], k) for k in self.eng if self.cnt[k] > 0]
        for q in self.dsem:
            for i, sem in enumerate(self.dsem[q]):
                if self.dcnt[q][i] > 0:
                    evs.append((sem, self.dcnt[q][i], "dma"))
        for e in self.eng:
            for ev in evs:
                if ev[2] != e:
                    self._wait(e, ev)

    def finish(self, e, resources):
        for r in resources:
            self._wait(e, r.w)
            for ev in r.r:
                self._wait(e, ev)


class T:
    def __init__(self, t, name):
        self.t = t
        self.r = Res(name)

    def __getitem__(self, k):
        return self.t[k]


def build_program():
    nc = bass.Bass("TRN2", target_bir_lowering=False)
    P = Prog(nc)

    def din(name, shape, dt=F32):
        return nc.dram_tensor(name, list(shape), dt, kind="ExternalInput").ap()

    xin = din("xin", [ROWS, D])
    w_in_l = din("w_in_l", [28, 128, 16, 512])
    wz_l = din("wz_l", [128, 16, 32])
    w_a2_f = din("w_a2_f", [16, 1024]); b_a_f = din("b_a_f", [1, 1024])
    w_a2_b = din("w_a2_b", [16, 1024]); b_a_b = din("b_a_b", [1, 1024])
    w_gla_o_l = din("w_gla_o_l", [4, 128, 16, 512]); w_conv_o_l = din("w_conv_o_l", [4, 128, 16, 512]); w_out_l = din("w_out_l", [4, 128, 16, 512])
    w_up_l = din("w_up_l", [16, 128, 16, 512]); w_down_l = din("w_down_l", [16, 128, 16, 512])
    cols_d = din("cols", [128, 8, 16])
    wdw_d = din("wdw", [128, 16, 31])
    gbc_d = din("gbc", [2, 128, D])
    flags_d = din("flags", [128, 32])
    xoth = din("xoth", [NOTH, D])
    ident_d = din("ident", [128, 128], BF16)
    masks_d = din("masks", [128, 6, 128])
    yout = nc.dram_tensor("yout", [NTOK, D], F32, kind="ExternalOutput").ap()

    s_qdb = nc.dram_tensor("s_qdb", [NB, 128, 1024], BF16).ap()
    s_kdb = nc.dram_tensor("s_kdb", [NB, 128, 1024], BF16).ap()
    s_v = nc.dram_tensor("s_v", [NB, 128, D], BF16).ap()
    s_o = nc.dram_tensor("s_o", [NB, 128, D], F32).ap()
    s_onT = nc.dram_tensor("s_onT", [D, NB * 128], BF16).ap()
    r_scr = {k: Res(k) for k in ["qdb", "qcf", "kdb", "v", "o", "onT", "cc_in", "cc_out", "yout"]}

    def sb(name, shape, dt=F32):
        return T(nc.alloc_sbuf_tensor("t_" + name, list(shape), dt), name)

    psts = [T(nc.alloc_psum_tensor(f"pst{i}", [128, 8, 128], BF16), f"pst{i}") for i in range(2)]
    banks = [T(nc.alloc_psum_tensor(f"pb{i}", [128, 512], F32), f"pb{i}") for i in range(5)]
    pmisc = T(nc.alloc_psum_tensor("pmisc", [128, 512], F32), "pmisc")
    bstate = {"i": 0}

    def nbank():
        b = banks[bstate["i"] % 5]
        bstate["i"] += 1
        return b

    cols = sb("cols", [128, 8, 16])
    flags = sb("flags", [128, 32])
    ident = sb("ident", [128, 128], BF16)
    masks = sb("masks", [128, 6, 128])
    P.dma("sp", cols[:], cols_d, writes=[cols.r])
    P.dma("sp", flags[:], flags_d, writes=[flags.r])
    P.dma("sp", ident[:], ident_d, writes=[ident.r])
    P.dma("sp", masks[:], masks_d, writes=[masks.r])
    CG_PREMIX, CG_PREMLP, C_BDW, C_LNG, C_LNB, C_GGLA, C_BCO = 0, 1, 2, 3, 4, 5, 6
    M_TRII, M_MASKB, M_TRII_S, M_TRIE_S, M_SUF_S, M_ONES_S = 0, 1, 2, 3, 4, 5
    ones = sb("ones", [128, 128])
    P.op("pool", lambda e: e.memset(ones[:], 1.0), writes=[ones.r])
    gcol0 = sb("gcol0", [128, 16])
    P.op("dve", lambda e: e.tensor_scalar(out=gcol0[:], in0=cols[:, CG_PREMIX, :], scalar1=flags[:, 12:13], scalar2=None, op0=ALU.mult),
         reads=[cols.r, flags.r], writes=[gcol0.r])

    wbuf = []
    wstate = {"i": 0}

    def alloc_wbuf(alloc):
        wbuf.clear()
        for i in range(2):
            wb_ = alloc(f"wbuf{i}_{wstate['i']}", [128, 16, 512], BF16)
            wb_.rq = [Res("wq") for _ in range(4)]
            wbuf.append(wb_)

    def load_w(src_ap):
        wb = wbuf[wstate["i"] % 2]
        wstate["i"] += 1
        src = src_ap
        if SKIP_WDMA and wstate["i"] > 2:
            return wb
        for hf in range(2):
            P.dma("pool", wb[:, hf * 8:(hf + 1) * 8, :], src[:, hf * 8:(hf + 1) * 8, :], writes=[wb.rq[2 * hf], wb.rq[2 * hf + 1]])
        return wb

    xt = sb("xt", [128, D])
    junk = sb("junk", [128, D], BF16)
    ub = sb("ub", [128, D], BF16)
    ssq = sb("ssq", [128, 4])
    rstd = sb("rstd", [128, 4])

    def rms_rows(src, n, width, out_bf, nh=1):
        for h in range(nh):
            P.op("act", lambda e, h=h: e.activation(out=junk[0:n, h * width:(h + 1) * width], in_=src[0:n, h * width:(h + 1) * width],
                                                   func=AF.Square, accum_out=ssq[0:n, h:h + 1]),
                 reads=[src.r], writes=[junk.r, ssq.r])
        P.op("act", lambda e: e.activation(out=rstd[0:n, 0:nh], in_=ssq[0:n, 0:nh], func=AF.Sqrt, scale=1.0 / width, bias=EPS),
             reads=[ssq.r], writes=[rstd.r])
        P.op("dve", lambda e: e.reciprocal(out=rstd[0:n, 0:nh], in_=rstd[0:n, 0:nh]), reads=[rstd.r], writes=[rstd.r])
        if out_bf is not None:
            for h in range(nh):
                P.op("dve", lambda e, h=h: e.tensor_scalar(out=out_bf[0:n, h * width:(h + 1) * width], in0=src[0:n, h * width:(h + 1) * width],
                                                          scalar1=rstd[0:n, h:h + 1], scalar2=None, op0=ALU.mult),
                     reads=[src.r, rstd.r], writes=[out_bf.r])

    def transpose_rows(src_bf, n, dstT, col0, scale_cols):
        for g in range(2):
            pst = psts[g]
            for j in range(8):
                kt = g * 8 + j
                P.op("pe", lambda e, kt=kt, j=j: e.transpose(out=pst[:, j, 0:n], in_=src_bf[0:n, kt * 128:(kt + 1) * 128], identity=ident[0:n, 0:n]),
                     reads=[src_bf.r, ident.r], writes=[pst.r])
            for j in range(8):
                kt = g * 8 + j
                if scale_cols is None:
                    P.op("act", lambda e, kt=kt, j=j: e.activation(out=dstT[:, kt, col0:col0 + n], in_=pst[:, j, 0:n], func=AF.Copy),
                         reads=[pst.r], writes=[dstT.r])
                else:
                    sc, scr = scale_cols
                    P.op("act", lambda e, kt=kt, j=j: e.activation(out=dstT[:, kt, col0:col0 + n], in_=pst[:, j, 0:n], func=AF.Copy, scale=sc[:, kt:kt + 1]),
                         reads=[pst.r, scr], writes=[dstT.r])

    esS = ExitStack()
    Sf = T(esS.enter_context(nc.sbuf_tensor("t_Sf", [128, 8, 512], F32)), "Sf")
    Sba = T(esS.enter_context(nc.sbuf_tensor("t_Sba", [128, 8, 512], F32)), "Sba")
    Sf.rk = [Res("Sfk") for _ in range(8)]
    Sba.rk = [Res("Sbak") for _ in range(8)]
    P.op("pool", lambda e: e.memset(Sf[:], 0.0), writes=[Sf.rk])
    P.op("pool", lambda e: e.memset(Sba[:], 0.0), writes=[Sba.rk])

    def decay_z(S, b_flags):
        for d in range(2):
            pb = nbank()
            for kt in range(KT):
                P.op("pe", lambda e, kt=kt: e.matmul(out=pb[0:16, 0:128], lhsT=S.wz[:, kt, d * 16:(d + 1) * 16], rhs=S.uT[:, kt, :], start=(kt == 0), stop=(kt == KT - 1)),
                     reads=[S.wz.r, S.uT.r], writes=[pb.r])
            P.op("dve", lambda e: e.tensor_copy(out=S.zT[d][:], in_=pb[0:16, 0:128]), reads=[pb.r], writes=[S.zT[d].r])
            for cc in range(2):
                pb2 = nbank()
                P.op("pe", lambda e: e.matmul(out=pb2[:], lhsT=S.zT[d][:], rhs=S.wa2[d][:, cc * 512:(cc + 1) * 512], start=True, stop=False),
                     reads=[S.zT[d].r, S.wa2[d].r], writes=[pb2.r])
                P.op("pe", lambda e: e.matmul(out=pb2[:], lhsT=S.ones1[:], rhs=S.ba[d][:, cc * 512:(cc + 1) * 512], start=False, stop=True),
                     reads=[S.ones1.r, S.ba[d].r], writes=[pb2.r])
                P.op("act", lambda e: e.activation(out=S.spl[d][:, cc * 512:(cc + 1) * 512], in_=pb2[:], func=AF.Exp, scale=-1.0),
                     reads=[pb2.r], writes=[S.spl[d].r])
            P.op("act", lambda e: e.activation(out=S.spl[d][:], in_=S.spl[d][:], func=AF.Ln, bias=1.0, scale=1.0), reads=[S.spl[d].r], writes=[S.spl[d].r])
            if b_flags is not None:
                fc = b_flags[d]
                P.op("dve", lambda e: e.tensor_scalar(out=S.spl[d][:], in0=S.spl[d][:], scalar1=flags[:, fc:fc + 1], scalar2=None, op0=ALU.mult),
                     reads=[S.spl[d].r, flags.r], writes=[S.spl[d].r])
    def decay_tot(S):
        for d in range(2):
            for kt in range(8):
                P.op("pe", lambda e, kt=kt: e.matmul(out=pmisc[:, 256 + d * 8 + kt:256 + d * 8 + kt + 1], lhsT=S.spl[d][:, kt * 128:(kt + 1) * 128],
                                                     rhs=masks[:, M_ONES_S, 0:1], start=True, stop=True),
                     reads=[S.spl[d].r, masks.r], writes=[pmisc.r])
        P.op("act", lambda e: e.activation(out=S.Dcols[:], in_=pmisc[:, 256:272], func=AF.Exp), reads=[pmisc.r], writes=[S.Dcols.r])

    def tm_decays(S):
        for d, mi in ((0, M_SUF_S), (1, M_TRIE_S)):
            for cc in range(2):
                pb = nbank()
                P.op("pe", lambda e: e.matmul(out=pb[:], lhsT=masks[:, mi, :], rhs=S.spl[d][:, cc * 512:(cc + 1) * 512], start=True, stop=True),
                     reads=[S.spl[d].r, masks.r], writes=[pb.r])
                P.op("act", lambda e: e.activation(out=S.Gb[d][:, cc * 512:(cc + 1) * 512], in_=pb[:], func=AF.Exp), reads=[pb.r], writes=[S.Gb[d].r])

    class NS:
        pass

    nc.push_named_scope('phase0')
    es0 = ExitStack()

    def sb0(name, shape, dt=F32):
        return T(es0.enter_context(nc.sbuf_tensor("p0_" + name, list(shape), dt)), name)

    S0 = NS()
    S0.uT = sb0("uT", [128, 16, 128], BF16)
    S0.wz = sb0("wz", [128, 16, 32], BF16)
    for qd in range(4):
        P.dma("pool", S0.wz[:, qd * 4:(qd + 1) * 4, :], wz_l[:, qd * 4:(qd + 1) * 4, :], writes=[S0.wz.r])
    S0.wa2 = [sb0("wa2f", [16, 1024]), sb0("wa2b", [16, 1024])]
    S0.ba = [sb0("baf", [1, 1024]), sb0("bab", [1, 1024])]
    P.dma("sp", S0.wa2[0][:], w_a2_f, writes=[S0.wa2[0].r]); P.dma("sp", S0.wa2[1][:], w_a2_b, writes=[S0.wa2[1].r])
    P.dma("sp", S0.ba[0][:], b_a_f, writes=[S0.ba[0].r]); P.dma("sp", S0.ba[1][:], b_a_b, writes=[S0.ba[1].r])
    S0.ones1 = sb0("ones1", [1, 128])
    P.op("pool", lambda e: e.memset(S0.ones1[:], 1.0), writes=[S0.ones1.r])
    S0.zT = [sb0("zfT", [16, 128]), sb0("zbT", [16, 128])]
    S0.spl = [sb0("spf", [128, 1024]), sb0("spb", [128, 1024])]
    S0.Gb = [sb0("G0", [128, 1024]), sb0("G1", [128, 1024])]
    S0.Dcols = sb0("Dcols", [128, 16])
    wkv = sb0("wkv", [128, 16, 3072], BF16)
    wkv_r = [Res("wkv") for _ in range(6)]
    for c in range(6):
        srcw = w_in_l[2 + c]
        for qd in range(4):
            P.dma("pool", wkv[:, qd * 4:(qd + 1) * 4, c * 512:(c + 1) * 512], srcw[:, qd * 4:(qd + 1) * 4, :], writes=[wkv_r[c]])
    ktm0 = sb0("ktm", [128, 1024]); vtm0 = sb0("vtm", [128, D], BF16)
    kendf0 = sb0("kendf", [128, 1024], BF16); kdbtm0 = sb0("kdbtm", [128, 1024], BF16)
    Pacc = sb0("Pacc", [128, 16]); expP = sb0("expP", [128, 16])
    P.op("pool", lambda e: e.memset(Pacc[:], 0.0), writes=[Pacc.r])
    ub2 = sb0("ub2", [128, D], BF16)
    uT2 = sb0("uT2", [128, 16, 128], BF16)
    ubs = [ub, ub2]
    uTs = [S0.uT, uT2]

    def p0_front(pb_i):
        P.dma("sp", xt[:], xoth[pb_i * 128:(pb_i + 1) * 128, :], writes=[xt.r])
        rms_rows(xt, 128, D, ubs[pb_i % 2])
        transpose_rows(ubs[pb_i % 2], 128, uTs[pb_i % 2], 0, (cols[:, CG_PREMIX, :], cols.r))

    NP0 = NOTH // 128
    p0_front(0)
    for pb_i in range(NP0):
        if pb_i == 0:
            bf = (13, 14)
        else:
            j = (pb_i - 1) // 16
            bf = (16 + 2 * j, 17 + 2 * j)
        S0.uT = uTs[pb_i % 2]
        decay_z(S0, bf)
        for c in range(6):
            pb = nbank()
            for kt in range(KT):
                P.op("pe", lambda e, kt=kt: e.matmul(out=pb[:], lhsT=S0.uT[:, kt, :], rhs=wkv[:, kt, c * 512:(c + 1) * 512], start=(kt == 0), stop=(kt == KT - 1)),
                     reads=[wkv_r[c], S0.uT.r], writes=[pb.r])
            if c < 2:
                P.op("dve", lambda e: e.tensor_copy(out=ktm0[:, c * 512:(c + 1) * 512], in_=pb[:]), reads=[pb.r], writes=[ktm0.r])
            else:
                P.op("act", lambda e: e.activation(out=vtm0[:, (c - 2) * 512:(c - 1) * 512], in_=pb[:], func=AF.Copy), reads=[pb.r], writes=[vtm0.r])
        if pb_i + 1 < NP0:
            p0_front(pb_i + 1)
        decay_tot(S0)
        P.op("act", lambda e: e.activation(out=expP[:], in_=Pacc[:], func=AF.Exp), reads=[Pacc.r], writes=[expP.r])
        tm_decays(S0)
        P.op("dve", lambda e: e.scalar_tensor_tensor(out=kendf0[:], in0=ktm0[:], scalar=flags[:, bf[0]:bf[0] + 1], in1=S0.Gb[0][:], op0=ALU.mult, op1=ALU.mult),
             reads=[ktm0.r, flags.r, S0.Gb[0].r], writes=[kendf0.r])
        P.op("dve", lambda e: e.scalar_tensor_tensor(out=kdbtm0[:], in0=ktm0[:], scalar=flags[:, bf[1]:bf[1] + 1], in1=S0.Gb[1][:], op0=ALU.mult, op1=ALU.mult),
             reads=[ktm0.r, flags.r, S0.Gb[1].r], writes=[kdbtm0.r])
        for kt in range(8):
            h = kt // 2
            pu = nbank()
            P.op("pe", lambda e: e.matmul(out=pu[:], lhsT=kendf0[:, kt * 128:(kt + 1) * 128], rhs=vtm0[:, h * 512:(h + 1) * 512], start=True, stop=True),
                 reads=[kendf0.r, vtm0.r], writes=[pu.r])
            P.op("dve", lambda e: e.scalar_tensor_tensor(out=Sf[:, kt, :], in0=Sf[:, kt, :], scalar=S0.Dcols[:, kt:kt + 1], in1=pu[:], op0=ALU.mult, op1=ALU.add),
                 reads=[Sf.rk[kt], S0.Dcols.r, pu.r], writes=[Sf.rk[kt]])
            pu2 = nbank()
            P.op("pe", lambda e: e.matmul(out=pu2[:], lhsT=kdbtm0[:, kt * 128:(kt + 1) * 128], rhs=vtm0[:, h * 512:(h + 1) * 512], start=True, stop=True),
                 reads=[kdbtm0.r, vtm0.r], writes=[pu2.r])
            P.op("dve", lambda e: e.scalar_tensor_tensor(out=Sba[:, kt, :], in0=pu2[:], scalar=expP[:, 8 + kt:9 + kt], in1=Sba[:, kt, :], op0=ALU.mult, op1=ALU.add),
                 reads=[Sba.rk[kt], expP.r, pu2.r], writes=[Sba.rk[kt]])
        P.op("dve", lambda e: e.tensor_tensor(out=Pacc[:], in0=Pacc[:], in1=pmisc[:, 256:272], op=ALU.add), reads=[Pacc.r, pmisc.r], writes=[Pacc.r])
    S0.uT = uTs[0]
    for b in range(NB):
        P.dma("sp", xt[:], xin[b * 128:(b + 1) * 128, :], writes=[xt.r])
        rms_rows(xt, 128, D, ub)
        transpose_rows(ub, 128, S0.uT, 0, ((gcol0, gcol0.r) if b == 0 else (cols[:, CG_PREMIX, :], cols.r)))
        for c in range(2, 6):
            pb = nbank()
            for kt in range(KT):
                P.op("pe", lambda e, kt=kt: e.matmul(out=pb[:], lhsT=S0.uT[:, kt, :], rhs=wkv[:, kt, c * 512:(c + 1) * 512], start=(kt == 0), stop=(kt == KT - 1)),
                     reads=[wkv_r[c], S0.uT.r], writes=[pb.r])
            P.op("act", lambda e: e.activation(out=vtm0[:, (c - 2) * 512:(c - 1) * 512], in_=pb[:], func=AF.Copy), reads=[pb.r], writes=[vtm0.r])
        P.dma("sp", s_v[b], vtm0[:], reads=[vtm0.r], writes=[r_scr["v"]])
    P.barrier()
    es0.close()
    nc.pop_named_scope('phase0')
    nc.push_named_scope('sweep1a')

    es = ExitStack()

    def sb1(name, shape, dt=F32):
        return T(es.enter_context(nc.sbuf_tensor("t_" + name, list(shape), dt)), name)

    alloc_wbuf(sb1)
    S1 = NS()
    uT = S1.uT = sb1("uT", [128, 16, 128], BF16)
    S1.wz = sb1("wz", [128, 16, 32], BF16)
    for qd in range(4):
        P.dma("pool", S1.wz[:, qd * 4:(qd + 1) * 4, :], wz_l[:, qd * 4:(qd + 1) * 4, :], writes=[S1.wz.r])
    S1.wa2 = [sb1("wa2f", [16, 1024]), sb1("wa2b", [16, 1024])]
    S1.ba = [sb1("baf", [1, 1024]), sb1("bab", [1, 1024])]
    P.dma("sp", S1.wa2[0][:], w_a2_f, writes=[S1.wa2[0].r]); P.dma("sp", S1.wa2[1][:], w_a2_b, writes=[S1.wa2[1].r])
    P.dma("sp", S1.ba[0][:], b_a_f, writes=[S1.ba[0].r]); P.dma("sp", S1.ba[1][:], b_a_b, writes=[S1.ba[1].r])
    S1.ones1 = sb1("ones1", [1, 128])
    P.op("pool", lambda e: e.memset(S1.ones1[:], 1.0), writes=[S1.ones1.r])
    qT = sb1("qT", [128, 8, 128]); kT = sb1("kT", [128, 8, 128])
    ktm = sb1("ktm", [128, 1024]); vtm = sb1("vtm", [128, D], BF16)
    S1.zT = [sb1("zfT", [16, 128]), sb1("zbT", [16, 128])]
    spl = S1.spl = [sb1("spf", [128, 1024]), sb1("spb", [128, 1024])]
    Eb = [sb1("E0", [128, 8, 128]), sb1("E1", [128, 8, 128])]
    Gb = S1.Gb = [sb1("G0", [128, 1024]), sb1("G1", [128, 1024])]
    qdf = sb1("qdf", [128, 8, 128], BF16); kdf = sb1("kdf", [128, 8, 128], BF16)
    qdb = sb1("qdb", [128, 8, 128], BF16); kdb = sb1("kdb", [128, 8, 128], BF16)
    kendf = sb1("kendf", [128, 1024], BF16); kdbtm = sb1("kdbtm", [128, 1024], BF16)
    st1 = sb1("st1", [128, 128]); st2 = sb1("st2", [128, 128]); ST = sb1("ST", [128, 128], BF16)
    Sfb = sb1("Sfb", [128, 8, 512], BF16)
    opart = sb1("opart", [128, D])
    Dcols = S1.Dcols = sb1("Dcols", [128, 16])
    Dball = sb1("Dball", [128, NB, 8])
    Sfb.rk = [Res("Sfbk") for _ in range(8)]
    P.op("act", lambda e: e.activation(out=Sfb[:], in_=Sf[:], func=AF.Copy), reads=[Sf.rk], writes=[Sfb.rk])

    for b in range(NB):
        P.dma("sp", xt[:], xin[b * 128:(b + 1) * 128, :], writes=[xt.r])
        rms_rows(xt, 128, D, ub)
        transpose_rows(ub, 128, uT, 0, ((gcol0, gcol0.r) if b == 0 else (cols[:, CG_PREMIX, :], cols.r)))
        decay_z(S1, (12, 12) if b == 0 else None)
        P.dma("sp", vtm[:], s_v[b], reads=[r_scr["v"]], writes=[vtm.r])
        for c in range(4):
            wb = load_w(w_in_l[c])
            if c < 4:
                pb = nbank()
                for n in range(4):
                    for kt in range(KT):
                        P.op("pe", lambda e, n=n, kt=kt: e.matmul(out=pb[:, n * 128:(n + 1) * 128], lhsT=wb[:, kt, n * 128:(n + 1) * 128], rhs=uT[:, kt, :],
                                                                  start=(kt == 0), stop=(kt == KT - 1)),
                             reads=[wb.rq[kt // 4], uT.r], writes=[pb.r])
                dst = qT if c < 2 else kT
                cc = c % 2
                P.op("act", lambda e: e.activation(out=dst[:, cc * 4:(cc + 1) * 4, :], in_=pb[:].rearrange("p (a b) -> p a b", a=4), func=AF.Copy,
                                                   scale=(1.0 / 16 if c < 2 else 1.0)),
                     reads=[pb.r], writes=[dst.r])
            if c >= 2:
                pb = nbank()
                for kt in range(KT):
                    P.op("pe", lambda e, kt=kt: e.matmul(out=pb[:], lhsT=uT[:, kt, :], rhs=wb[:, kt, :], start=(kt == 0), stop=(kt == KT - 1)),
                         reads=[wb.rq[kt // 4], uT.r], writes=[pb.r])
                if c < 4:
                    P.op("dve", lambda e: e.tensor_copy(out=ktm[:, (c - 2) * 512:(c - 1) * 512], in_=pb[:]), reads=[pb.r], writes=[ktm.r])
                else:
                    P.op("dve", lambda e: e.tensor_copy(out=vtm[:, (c - 4) * 512:(c - 3) * 512], in_=pb[:]), reads=[pb.r], writes=[vtm.r])
        decay_tot(S1)
        if b >= 1:
            P.op("dve", lambda e: e.tensor_copy(out=Dball[:, b, :], in_=Dcols[:, 8:16]), reads=[Dcols.r], writes=[Dball.r])
        pa, pb_ = nbank(), nbank()
        for kt in range(8):
            tgt = pa if kt < 4 else pb_
            P.op("pe", lambda e, kt=kt, tgt=tgt: e.matmul(out=tgt[:, (kt % 4) * 128:(kt % 4 + 1) * 128], lhsT=spl[0][:, kt * 128:(kt + 1) * 128],
                                                          rhs=masks[:, M_TRII_S, :], start=True, stop=True),
                 reads=[spl[0].r, masks.r], writes=[tgt.r])
        for half, tgt in enumerate((pa, pb_)):
            P.op("act", lambda e, half=half, tgt=tgt: e.activation(out=Eb[0][:, half * 4:(half + 1) * 4, :], in_=tgt[:].rearrange("p (a b) -> p a b", a=4), func=AF.Exp),
                 reads=[tgt.r], writes=[Eb[0].r])
            P.op("act", lambda e, half=half, tgt=tgt: e.activation(out=Eb[1][:, half * 4:(half + 1) * 4, :], in_=tgt[:].rearrange("p (a b) -> p a b", a=4), func=AF.Exp, scale=-1.0),
                 reads=[tgt.r], writes=[Eb[1].r])
        P.op("dve", lambda e: e.tensor_tensor(out=qdf[:], in0=qT[:], in1=Eb[0][:], op=ALU.mult), reads=[qT.r, Eb[0].r], writes=[qdf.r])
        P.op("dve", lambda e: e.tensor_tensor(out=kdf[:], in0=kT[:], in1=Eb[1][:], op=ALU.mult), reads=[kT.r, Eb[1].r], writes=[kdf.r])
        pa, pb_ = nbank(), nbank()
        for kt in range(8):
            tgt = pa if kt < 4 else pb_
            P.op("pe", lambda e, kt=kt, tgt=tgt: e.matmul(out=tgt[:, (kt % 4) * 128:(kt % 4 + 1) * 128], lhsT=spl[1][:, kt * 128:(kt + 1) * 128],
                                                          rhs=masks[:, M_TRIE_S, :], start=True, stop=True),
                 reads=[spl[1].r, masks.r], writes=[tgt.r])
        for half, tgt in enumerate((pa, pb_)):
            P.op("act", lambda e, half=half, tgt=tgt: e.activation(out=Eb[0][:, half * 4:(half + 1) * 4, :], in_=tgt[:].rearrange("p (a b) -> p a b", a=4), func=AF.Exp, scale=-1.0),
                 reads=[tgt.r], writes=[Eb[0].r])
            P.op("act", lambda e, half=half, tgt=tgt: e.activation(out=Eb[1][:, half * 4:(half + 1) * 4, :], in_=tgt[:].rearrange("p (a b) -> p a b", a=4), func=AF.Exp),
                 reads=[tgt.r], writes=[Eb[1].r])
        P.op("dve", lambda e: e.tensor_tensor(out=qdb[:], in0=qT[:], in1=Eb[0][:], op=ALU.mult), reads=[qT.r, Eb[0].r], writes=[qdb.r])
        P.op("dve", lambda e: e.tensor_tensor(out=kdb[:], in0=kT[:], in1=Eb[1][:], op=ALU.mult), reads=[kT.r, Eb[1].r], writes=[kdb.r])
        tm_decays(S1)
        P.op("dve", lambda e: e.tensor_tensor(out=kendf[:], in0=ktm[:], in1=Gb[0][:], op=ALU.mult), reads=[ktm.r, Gb[0].r], writes=[kendf.r])
        P.op("dve", lambda e: e.tensor_tensor(out=kdbtm[:], in0=ktm[:], in1=Gb[1][:], op=ALU.mult), reads=[ktm.r, Gb[1].r], writes=[kdbtm.r])
        for h in range(4):
            if b >= 1:
                for di, (kd, qd) in enumerate(((kdf, qdf), (kdb, qdb))):
                    for j in range(2):
                        kt = 2 * h + j
                        P.op("pe", lambda e, kt=kt, j=j: e.matmul(out=pmisc[:, di * 128:(di + 1) * 128], lhsT=kd[:, kt, :], rhs=qd[:, kt, :], start=(j == 0), stop=(j == 1)),
                             reads=[kd.r, qd.r], writes=[pmisc.r])
                P.op("dve", lambda e: e.tensor_tensor(out=st1[:], in0=pmisc[:, 0:128], in1=masks[:, M_TRII, :], op=ALU.mult), reads=[pmisc.r, masks.r], writes=[st1.r])
                P.op("dve", lambda e: e.tensor_tensor(out=st2[:], in0=pmisc[:, 128:256], in1=masks[:, M_MASKB, :], op=ALU.mult), reads=[pmisc.r, masks.r], writes=[st2.r])
                P.op("dve", lambda e: e.tensor_tensor(out=ST[:], in0=st1[:], in1=st2[:], op=ALU.add), reads=[st1.r, st2.r], writes=[ST.r])
                po = nbank()
                P.op("pe", lambda e: e.matmul(out=po[:], lhsT=ST[:], rhs=vtm[:, h * 512:(h + 1) * 512], start=True, stop=False),
                     reads=[ST.r, vtm.r], writes=[po.r])
                for j in range(2):
                    kt = 2 * h + j
                    P.op("pe", lambda e, kt=kt, j=j: e.matmul(out=po[:], lhsT=qdf[:, kt, :], rhs=Sfb[:, kt, :], start=False, stop=(j == 1)),
                         reads=[qdf.r, Sfb.rk[kt]], writes=[po.r])
                P.op("act", lambda e: e.activation(out=opart[:, h * 512:(h + 1) * 512], in_=po[:], func=AF.Copy), reads=[po.r], writes=[opart.r])
            for j in range(2):
                kt = 2 * h + j
                pu = nbank()
                P.op("pe", lambda e, kt=kt: e.matmul(out=pu[:], lhsT=kendf[:, kt * 128:(kt + 1) * 128], rhs=vtm[:, h * 512:(h + 1) * 512], start=True, stop=True),
                     reads=[kendf.r, vtm.r], writes=[pu.r])
                P.op("dve", lambda e, kt=kt: e.scalar_tensor_tensor(out=Sf[:, kt, :], in0=Sf[:, kt, :], scalar=Dcols[:, kt:kt + 1], in1=pu[:], op0=ALU.mult, op1=ALU.add),
                     reads=[Sf.rk[kt], Dcols.r, pu.r], writes=[Sf.rk[kt]])
                P.op("act", lambda e, kt=kt: e.activation(out=Sfb[:, kt, :], in_=Sf[:, kt, :], func=AF.Copy), reads=[Sf.rk[kt]], writes=[Sfb.rk[kt]])
        if b >= 1:
            P.dma("sp", s_qdb[b].rearrange("p (a b) -> p a b", a=8), qdb[:], reads=[qdb.r], writes=[r_scr["qdb"]])
            P.dma("sp", s_kdb[b], kdbtm[:], reads=[kdbtm.r], writes=[r_scr["kdb"]])
            P.dma("sp", s_o[b], opart[:], reads=[opart.r], writes=[r_scr["o"]])

    nc.pop_named_scope('sweep1a')
    nc.push_named_scope('sweep1b')
    Sb = Sba
    Sbb = Sfb
    ofull = opart
    on_bf = ub
    onT = uT
    for b in range(NB - 1, 0, -1):
        P.dma("sp", qdb[:], s_qdb[b].rearrange("p (a b) -> p a b", a=8), reads=[r_scr["qdb"]], writes=[qdb.r])
        P.dma("sp", kdbtm[:], s_kdb[b], reads=[r_scr["kdb"]], writes=[kdbtm.r])
        P.dma("sp", vtm[:], s_v[b], reads=[r_scr["v"]], writes=[vtm.r])
        P.dma("sp", xt[:], s_o[b], reads=[r_scr["o"]], writes=[xt.r])
        for kt in range(8):
            P.op("dve", lambda e, kt=kt: e.tensor_scalar(out=Sb[:, kt, :], in0=Sb[:, kt, :], scalar1=Dball[:, b, kt:kt + 1], scalar2=None, op0=ALU.mult),
                 reads=[Sb.rk[kt], Dball.r], writes=[Sb.rk[kt]])
            P.op("act", lambda e, kt=kt: e.activation(out=Sbb[:, kt, :], in_=Sb[:, kt, :], func=AF.Copy), reads=[Sb.rk[kt]], writes=[Sbb.rk[kt]])
        for h in range(4):
            po = nbank()
            for j in range(2):
                kt = 2 * h + j
                P.op("pe", lambda e, kt=kt, j=j: e.matmul(out=po[:], lhsT=qdb[:, kt, :], rhs=Sbb[:, kt, :], start=(j == 0), stop=(j == 1)),
                     reads=[qdb.r, Sbb.rk[kt]], writes=[po.r])
            P.op("dve", lambda e: e.tensor_tensor(out=ofull[:, h * 512:(h + 1) * 512], in0=po[:], in1=xt[:, h * 512:(h + 1) * 512], op=ALU.add),
                 reads=[po.r, xt.r], writes=[ofull.r])
            for j in range(2):
                kt = 2 * h + j
                pu = nbank()
                P.op("pe", lambda e, kt=kt: e.matmul(out=pu[:], lhsT=kdbtm[:, kt * 128:(kt + 1) * 128], rhs=vtm[:, h * 512:(h + 1) * 512], start=True, stop=True),
                     reads=[kdbtm.r, vtm.r], writes=[pu.r])
                P.op("dve", lambda e, kt=kt: e.tensor_tensor(out=Sb[:, kt, :], in0=Sb[:, kt, :], in1=pu[:], op=ALU.add), reads=[Sb.rk[kt], pu.r], writes=[Sb.rk[kt]])
        rms_rows(ofull, 128, 512, on_bf, nh=4)
        transpose_rows(on_bf, 128, onT, 0, None)
        P.dma("sp", s_onT.rearrange("(vt p) t -> p vt t", p=128)[:, :, b * 128:(b + 1) * 128], onT[:], reads=[onT.r], writes=[r_scr["onT"]])
    P.barrier()
    es.close()
    esS.close()
    nc.pop_named_scope('sweep1b')
    nc.push_named_scope('sweep2')

    TB = 512
    NTB = 4
    W = TB + 32
    alloc_wbuf(sb)
    wdw = sb("wdw", [128, 16, 31])
    P.dma("sp", wdw[:], wdw_d, writes=[wdw.r])
    uW = sb("uW", [128, 16, W], BF16)
    htmp = sb("htmp", [128, TB])
    OFF = dict(A=0, B=16384, B2=32768, C=34816, Dd=67584, Ee=83968, Ff=100352, END=133120)
    arena = nc.alloc_sbuf_tensor("t_arena2", [128, OFF["END"] // 4], F32)
    RR = {k: Res("reg" + k) for k in ("A", "B", "B2", "C", "Dd", "Ee", "Ff", "Ff2")}

    class V:
        def __init__(self, start, nbytes, dt, regs, pattern=None, **kw):
            ap = arena[:, start // 4:(start + nbytes) // 4]
            if dt == BF16:
                ap = ap.bitcast(BF16)
            if pattern is not None:
                ap = ap.rearrange(pattern, **kw)
            self.t = ap
            self.r = [RR[k] for k in regs]

        def __getitem__(self, k):
            return self.t[k]

    gT = V(OFF["A"], 16 * W * 4, F32, ("A", "B", "B2"), "p (k t) -> p k t", k=16)
    yT = V(OFF["C"], 32768, F32, ("C",), "p (k t) -> p k t", k=16)
    yact = V(OFF["A"], 16384, BF16, ("A",), "p (k t) -> p k t", k=16)
    sgb = V(OFF["Ee"], 16384, BF16, ("Ee",), "p (k t) -> p k t", k=16)
    mb = V(OFF["C"], 32768, F32, ("C",), "p (k t) -> p k t", k=16)
    ract = V(OFF["Ff"] + 10752, 16384, BF16, ("Ff2",), "p (k t) -> p k t", k=16)
    ogT = V(OFF["B"], 16384, BF16, ("B",), "p (k t) -> p k t", k=16)
    sga = V(OFF["Dd"], 16384, BF16, ("Dd",), "p (k t) -> p k t", k=16)
    mixT = V(OFF["Ee"], 16384, BF16, ("Ee",), "p (k t) -> p k t", k=16)
    sg = V(OFF["Ff"], W * 4, F32, ("Ff",))
    ysq = V(OFF["Ff"] + 2304, 2048, F32, ("Ff",))
    mu = V(OFF["Ff"] + 4352, 2048, F32, ("Ff",))
    var = V(OFF["Ff"] + 6400, 2048, F32, ("Ff",))
    rs = V(OFF["Ff"] + 8448, 2048, F32, ("Ff",))
    mix2 = V(OFF["A"], 32768, F32, ("A", "B"), "p (k t) -> p k t", k=4)
    h1 = V(OFF["C"], 32768, F32, ("C",), "p (k t) -> p k t", k=4)
    hidT = V(OFF["Dd"], 65536, BF16, ("Dd", "Ee", "Ff", "Ff2"), "p (k t) -> p k t", k=64)
    ftm = mix2
    u2T = uW

    def fm_proj(src_ap_fn, nchunks, rhs, col0, ncols, evac):
        parts = [(0, ncols)] if ncols <= 512 else [(0, ncols // 2), (ncols // 2, ncols - ncols // 2)]
        for c in range(nchunks):
            wb = load_w(src_ap_fn(c))
            for n in range(4):
                for (c0, nn) in parts:
                    pb = nbank()
                    for kt in range(KT):
                        P.op("pe", lambda e, n=n, kt=kt: e.matmul(out=pb[:, 0:nn], lhsT=wb[:, kt, n * 128:(n + 1) * 128], rhs=rhs[:, kt, col0 + c0:col0 + c0 + nn],
                                                                  start=(kt == 0), stop=(kt == KT - 1)),
                             reads=[wb.rq[kt // 4], rhs.r], writes=[pb.r])
                    evac(c * 4 + n, pb, c0, nn)

    yv = yout.rearrange("(s k p) d -> s p k d", k=NTB, p=128)

    def prep_norm(sb_i, m):
        rr = 112 + TB * sb_i
        n = 128 if m < NTB else 32
        P.dma("sp", xt[0:n, :], xin[rr + 128 * m:rr + 128 * m + n, :], writes=[xt.r])
        rms_rows(xt, n, D, ub)

    def prep_tr(m):
        n = 128 if m < NTB else 32
        transpose_rows(ub, n, uW, 128 * m, (cols[:, CG_PREMIX, :], cols.r))

    for sbk in range(NTOK // TB):
        r0 = 112 + TB * sbk
        t0 = r0 + 16
        if sbk == 0:
            for m in range(NTB + 1):
                prep_norm(0, m)
                prep_tr(m)
        def ev_a(t, pb, c0, n):
            P.op("act", lambda e: e.activation(out=gT[:, t, c0:c0 + n], in_=pb[:, 0:n], func=AF.Copy), reads=[pb.r], writes=[gT.r])
        fm_proj(lambda c: w_in_l[12 + c], 4, uW, 0, W, ev_a)

        def ev_gt(t, pb, c0, n):
            P.op("act", lambda e: e.activation(out=sg[:, c0:c0 + n], in_=pb[:, 0:n], func=AF.Sigmoid), reads=[pb.r], writes=[sg.r])
            P.op("dve", lambda e: e.tensor_tensor(out=gT[:, t, c0:c0 + n], in0=gT[:, t, c0:c0 + n], in1=sg[:, c0:c0 + n], op=ALU.mult), reads=[gT.r, sg.r], writes=[gT.r])
        fm_proj(lambda c: w_in_l[16 + c], 4, uW, 0, W, ev_gt)
        yct = [Res("yct") for _ in range(16)]
        for ct in range(16):
            P.op("dve", lambda e, ct=ct: e.tensor_scalar(out=yT[:, ct, :], in0=gT[:, ct, 1:1 + TB], scalar1=wdw[:, ct, 0:1], scalar2=cols[:, C_BDW, ct:ct + 1],
                                                         op0=ALU.mult, op1=ALU.add),
                 reads=[gT.r, wdw.r, cols.r], writes=[yT.r, yct[ct]])
        for tap in range(1, 31):
            for ct in range(16):
                P.op("dve", lambda e, ct=ct, tap=tap: e.scalar_tensor_tensor(out=yT[:, ct, :], in0=gT[:, ct, tap + 1:tap + 1 + TB], scalar=wdw[:, ct, tap:tap + 1],
                                                                             in1=yT[:, ct, :], op0=ALU.mult, op1=ALU.add),
                     reads=[gT.r, wdw.r, yct[ct]], writes=[yT.r, yct[ct]])
        def ev_gb(t, pb, c0, n):
            P.op("act", lambda e: e.activation(out=sgb[:, t, :], in_=pb[:, 0:n], func=AF.Sigmoid), reads=[pb.r], writes=[sgb.r])
        fm_proj(lambda c: w_in_l[24 + c], 4, uW, 16, TB, ev_gb)

        def ev_ga(t, pb, c0, n):
            P.op("act", lambda e: e.activation(out=sga[:, t, :], in_=pb[:, 0:n], func=AF.Sigmoid), reads=[pb.r], writes=[sga.r])
        fm_proj(lambda c: w_in_l[20 + c], 4, uW, 16, TB, ev_ga)

        def ev_r(t, pb, c0, n):
            P.op("act", lambda e: e.activation(out=ract[:, t, :], in_=pb[:, 0:n], func=AF.Silu), reads=[pb.r], writes=[ract.r])
        fm_proj(lambda c: w_in_l[8 + c], 4, uW, 16, TB, ev_r)
        ps1, ps2 = nbank(), nbank()
        for ct in range(16):
            P.op("pe", lambda e, ct=ct: e.matmul(out=ps1[:], lhsT=ones[:], rhs=yT[:, ct, :], start=(ct == 0), stop=(ct == 15)),
                 reads=[ones.r, yT.r], writes=[ps1.r])
        for ct in range(16):
            P.op("act", lambda e, ct=ct: e.activation(out=ysq[:], in_=yT[:, ct, :], func=AF.Square), reads=[yT.r], writes=[ysq.r])
            P.op("pe", lambda e, ct=ct: e.matmul(out=ps2[:], lhsT=ones[:], rhs=ysq[:], start=(ct == 0), stop=(ct == 15)),
                 reads=[ones.r, ysq.r], writes=[ps2.r])
        P.op("dve", lambda e: e.tensor_scalar(out=mu[:], in0=ps1[:], scalar1=1.0 / D, scalar2=None, op0=ALU.mult), reads=[ps1.r], writes=[mu.r])
        P.op("dve", lambda e: e.tensor_tensor(out=var[:], in0=mu[:], in1=mu[:], op=ALU.mult), reads=[mu.r], writes=[var.r])
        P.op("dve", lambda e: e.scalar_tensor_tensor(out=var[:], in0=ps2[:], scalar=1.0 / D, in1=var[:], op0=ALU.mult, op1=ALU.subtract),
             reads=[ps2.r, var.r], writes=[var.r])
        P.op("act", lambda e: e.activation(out=rs[:], in_=var[:], func=AF.Sqrt, scale=1.0, bias=EPS), reads=[var.r], writes=[rs.r])
        P.op("dve", lambda e: e.reciprocal(out=rs[:], in_=rs[:]), reads=[rs.r], writes=[rs.r])
        for ct in range(16):
            P.op("dve", lambda e, ct=ct: e.tensor_tensor(out=yT[:, ct, :], in0=yT[:, ct, :], in1=mu[:], op=ALU.subtract), reads=[yT.r, mu.r], writes=[yT.r])
            P.op("dve", lambda e, ct=ct: e.tensor_tensor(out=yT[:, ct, :], in0=yT[:, ct, :], in1=rs[:], op=ALU.mult), reads=[yT.r, rs.r], writes=[yT.r])
        for ct in range(16):
            P.op("act", lambda e, ct=ct: e.activation(out=yact[:, ct, :], in_=yT[:, ct, :], func=AF.Silu, scale=cols[:, C_LNG, ct:ct + 1], bias=cols[:, C_LNB, ct:ct + 1]),
                 reads=[yT.r, cols.r], writes=[yact.r])
        def ev_yb(t, pb, c0, n):
            P.op("dve", lambda e: e.scalar_tensor_tensor(out=mb[:, t, :], in0=pb[:, 0:n], scalar=cols[:, C_BCO, t:t + 1], in1=sgb[:, t, :], op0=ALU.add, op1=ALU.mult),
                 reads=[pb.r, cols.r, sgb.r], writes=[mb.r])
        fm_proj(lambda c: w_conv_o_l[c], 4, yact, 0, TB, ev_yb)
        P.dma("sp", ogT[:], s_onT.rearrange("(vt p) t -> p vt t", p=128)[:, :, t0:t0 + TB], reads=[r_scr["onT"]], writes=[ogT.r])
        for vt in range(16):
            P.op("dve", lambda e, vt=vt: e.scalar_tensor_tensor(out=ogT[:, vt, :], in0=ogT[:, vt, :], scalar=cols[:, C_GGLA, vt:vt + 1], in1=ract[:, vt, :], op0=ALU.mult, op1=ALU.mult),
                 reads=[ogT.r, cols.r, ract.r], writes=[ogT.r])

        def ev_ya(t, pb, c0, n):
            P.op("dve", lambda e: e.tensor_tensor(out=htmp[:], in0=pb[:, 0:n], in1=sga[:, t, :], op=ALU.mult), reads=[pb.r, sga.r], writes=[htmp.r])
            P.op("dve", lambda e: e.tensor_tensor(out=mixT[:, t, :], in0=htmp[:], in1=mb[:, t, :], op=ALU.add), reads=[htmp.r, mb.r], writes=[mixT.r])
        fm_proj(lambda c: w_gla_o_l[c], 4, ogT, 0, TB, ev_ya)
        for c in range(4):
            wb = load_w(w_out_l[c])
            for tb in range(NTB):
                pb = nbank()
                for kt in range(KT):
                    P.op("pe", lambda e, kt=kt: e.matmul(out=pb[:], lhsT=mixT[:, kt, tb * 128:(tb + 1) * 128], rhs=wb[:, kt, :], start=(kt == 0), stop=(kt == KT - 1)),
                         reads=[wb.rq[kt // 4], mixT.r], writes=[pb.r])
                P.op("act", lambda e: e.activation(out=mix2[:, tb, c * 512:(c + 1) * 512], in_=pb[:], func=AF.Copy), reads=[pb.r], writes=[mix2.r])
        P.dma("sp", h1[:], xin[t0:t0 + TB, :].rearrange("(k p) d -> p k d", p=128), writes=[h1.r])
        P.dma("sp", xt[:], gbc_d[0], writes=[xt.r])
        for tb in range(NTB):
            for h in range(1):
                P.op("act", lambda e: e.activation(out=junk[:], in_=mix2[:, tb, :], func=AF.Square, accum_out=ssq[:, 0:1]), reads=[mix2.r], writes=[junk.r, ssq.r])
            P.op("act", lambda e: e.activation(out=rstd[:, 0:1], in_=ssq[:, 0:1], func=AF.Sqrt, scale=1.0 / D, bias=EPS), reads=[ssq.r], writes=[rstd.r])
            P.op("dve", lambda e: e.reciprocal(out=rstd[:, 0:1], in_=rstd[:, 0:1]), reads=[rstd.r], writes=[rstd.r])
            P.op("dve", lambda e: e.scalar_tensor_tensor(out=mix2[:, tb, :], in0=mix2[:, tb, :], scalar=rstd[:, 0:1], in1=xt[:], op0=ALU.mult, op1=ALU.mult),
                 reads=[mix2.r, rstd.r, xt.r], writes=[mix2.r])
            P.op("dve", lambda e: e.tensor_tensor(out=h1[:, tb, :], in0=h1[:, tb, :], in1=mix2[:, tb, :], op=ALU.add), reads=[h1.r, mix2.r], writes=[h1.r])
        for tb in range(NTB):
            P.op("act", lambda e: e.activation(out=junk[:], in_=h1[:, tb, :], func=AF.Square, accum_out=ssq[:, 0:1]), reads=[h1.r], writes=[junk.r, ssq.r])
            P.op("act", lambda e: e.activation(out=rstd[:, 0:1], in_=ssq[:, 0:1], func=AF.Sqrt, scale=1.0 / D, bias=EPS), reads=[ssq.r], writes=[rstd.r])
            P.op("dve", lambda e: e.reciprocal(out=rstd[:, 0:1], in_=rstd[:, 0:1]), reads=[rstd.r], writes=[rstd.r])
            P.op("dve", lambda e: e.tensor_scalar(out=ub[:], in0=h1[:, tb, :], scalar1=rstd[:, 0:1], scalar2=None, op0=ALU.mult), reads=[h1.r, rstd.r], writes=[ub.r])
            transpose_rows(ub, 128, u2T, 128 * tb, (cols[:, CG_PREMLP, :], cols.r))

        def ev_up(t, pb, c0, n):
            P.op("act", lambda e: e.activation(out=htmp[:], in_=pb[:, 0:n], func=AF.Relu), reads=[pb.r], writes=[htmp.r])
            P.op("dve", lambda e: e.tensor_tensor(out=hidT[:, t, :], in0=htmp[:], in1=htmp[:], op=ALU.mult), reads=[htmp.r], writes=[hidT.r])
        fm_proj(lambda c: w_up_l[c], 16, u2T, 0, TB, ev_up)
        nxt = sbk + 1 < NTOK // TB
        for c in range(4):
            if nxt:
                prep_norm(sbk + 1, c)
            pbs = [nbank() for _ in range(NTB)]
            for kg in range(4):
                wb = load_w(w_down_l[kg * 4 + c])
                for tb in range(NTB):
                    for kt in range(KT):
                        P.op("pe", lambda e, kt=kt: e.matmul(out=pbs[tb][:], lhsT=hidT[:, kg * 16 + kt, tb * 128:(tb + 1) * 128], rhs=wb[:, kt, :],
                                                             start=(kg == 0 and kt == 0), stop=(kg == 3 and kt == KT - 1)),
                             reads=[wb.rq[kt // 4], hidT.r], writes=[pbs[tb].r])
            for tb in range(NTB):
                P.op("act", lambda e: e.activation(out=ftm[:, tb, c * 512:(c + 1) * 512], in_=pbs[tb][:], func=AF.Copy), reads=[pbs[tb].r], writes=[ftm.r])
            if nxt:
                prep_tr(c)
        if nxt:
            prep_norm(sbk + 1, NTB)
            prep_tr(NTB)
        P.dma("sp", xt[:], gbc_d[1], writes=[xt.r])
        for tb in range(NTB):
            P.op("act", lambda e: e.activation(out=junk[:], in_=ftm[:, tb, :], func=AF.Square, accum_out=ssq[:, 0:1]), reads=[ftm.r], writes=[junk.r, ssq.r])
            P.op("act", lambda e: e.activation(out=rstd[:, 0:1], in_=ssq[:, 0:1], func=AF.Sqrt, scale=1.0 / D, bias=EPS), reads=[ssq.r], writes=[rstd.r])
            P.op("dve", lambda e: e.reciprocal(out=rstd[:, 0:1], in_=rstd[:, 0:1]), reads=[rstd.r], writes=[rstd.r])
            P.op("dve", lambda e: e.scalar_tensor_tensor(out=ftm[:, tb, :], in0=ftm[:, tb, :], scalar=rstd[:, 0:1], in1=xt[:], op0=ALU.mult, op1=ALU.mult),
                 reads=[ftm.r, rstd.r, xt.r], writes=[ftm.r])
            P.op("dve", lambda e: e.tensor_tensor(out=h1[:, tb, :], in0=h1[:, tb, :], in1=ftm[:, tb, :], op=ALU.add), reads=[h1.r, ftm.r], writes=[h1.r])
        P.dma("sp", yv[sbk], h1[:], reads=[h1.r], writes=[r_scr["yout"]])

    P.finish("sp", [r_scr["yout"]])
    nc.pop_named_scope('sweep2')
    return nc


_NC_CACHE = {}


def kernel(x_prompt, x_sample, meta_tokens, g_pre_mix, w_in, w_a2_f, b_a_f, w_a2_b, b_a_b,
           g_gla, w_gla_o, w_dw, b_dw, ln_g, ln_b, w_conv_o, b_conv_o, w_out, g_post_mix,
           g_pre_mlp, w_up, w_down, g_post_mlp):
    f32 = np.float32
    A = lambda a: np.ascontiguousarray(np.asarray(a, dtype=f32))
    x_prompt, x_sample, meta = A(x_prompt), A(x_sample), A(meta_tokens)
    col = lambda v: np.ascontiguousarray(np.asarray(v, f32).reshape(16, 128).T)
    cols = np.zeros((128, 8, 16), f32)
    for i, v in enumerate((g_pre_mix, g_pre_mlp, b_dw, ln_g, ln_b, g_gla, b_conv_o)):
        cols[:, i, :] = col(v)
    wdw = np.ascontiguousarray(np.asarray(w_dw, f32)[0].T.reshape(16, 128, 31).transpose(1, 0, 2))
    gbc = np.ascontiguousarray(np.stack([np.broadcast_to(np.asarray(g_post_mix, f32).reshape(1, D), (128, D)),
                                         np.broadcast_to(np.asarray(g_post_mlp, f32).reshape(1, D), (128, D))]))
    ident = np.eye(128).astype(ml_dtypes.bfloat16)
    jj, ii = np.meshgrid(np.arange(128), np.arange(128), indexing="ij")
    s = f32(-1.0 / 16)
    masks = np.zeros((128, 6, 128), f32)
    masks[:, 0, :] = (jj <= ii)
    masks[:, 1, :] = (jj > ii)
    masks[:, 2, :] = (jj <= ii) * s
    masks[:, 3, :] = (jj < ii) * s
    masks[:, 4, :] = (jj > ii) * s
    masks[:, 5, :] = s
    def chunks(Wm, col_starts, width=512):
        return np.ascontiguousarray(np.stack([Wm[:, cs:cs + width].reshape(16, 128, width).transpose(1, 0, 2) for cs in col_starts]))
    Win = A(w_in)[0]
    starts = [512 * i for i in range(8)] + [C_R + 512 * i for i in range(4)] + [C_A + 512 * i for i in range(4)] + \
             [C_GT + 512 * i for i in range(4)] + [C_GA + 512 * i for i in range(4)] + [C_GB + 512 * i for i in range(4)]
    Wd = A(w_down)[0]
    shared = dict(
        w_in_l=chunks(Win, starts), wz_l=chunks(Win, [C_ZF], 32)[0],
        w_a2_f=A(w_a2_f)[0], b_a_f=A(b_a_f), w_a2_b=A(w_a2_b)[0], b_a_b=A(b_a_b),
        w_gla_o_l=chunks(A(w_gla_o)[0], [0, 512, 1024, 1536]), w_conv_o_l=chunks(A(w_conv_o)[0], [0, 512, 1024, 1536]),
        w_out_l=chunks(A(w_out)[0], [0, 512, 1024, 1536]), w_up_l=chunks(A(w_up)[0], [512 * i for i in range(16)]),
        w_down_l=np.ascontiguousarray(np.concatenate([chunks(Wd[kg * 2048:(kg + 1) * 2048], [0, 512, 1024, 1536]) for kg in range(4)])),
        cols=cols, wdw=wdw, gbc=gbc, ident=ident, masks=masks)
    in_maps = []
    xs = x_sample[0]
    for c in range(8):
        xin = np.zeros((ROWS, D), f32)
        fl = np.zeros((128, 32), f32)
        xoth = np.zeros((NOTH, D), f32)
        if c < 4:
            xin[112:128] = meta
            xin[128:2176] = x_prompt[c]
        else:
            q = c - 4
            xin[112:128] = meta if q == 0 else xs[2048 * q - 16:2048 * q]
            xin[128:2176] = xs[2048 * q:2048 * (q + 1)]
            if q < 3:
                xin[2176:2192] = xs[2048 * (q + 1):2048 * (q + 1) + 16]
            xoth[112:128] = meta
            fl[:, 13] = 1.0 if q > 0 else 0.0
            others = [o for o in range(4) if o != q]
            for j, o in enumerate(others):
                xoth[128 + 2048 * j:128 + 2048 * (j + 1)] = xs[2048 * o:2048 * (o + 1)]
                fl[:, 16 + 2 * j] = 1.0 if o < q else 0.0
                fl[:, 17 + 2 * j] = 1.0 if o > q else 0.0
        fl[:, 12] = 1.0 if c <= 4 else 0.0
        d = dict(shared)
        d["xoth"] = xoth
        d["xin"] = xin
        d["flags"] = fl
        in_maps.append(d)
    if "nc" not in _NC_CACHE:
        _NC_CACHE["nc"] = build_program()
    if TRACE:
        res = run_bass_kernel_spmd(_NC_CACHE["nc"], in_maps, core_ids=list(range(8)), trace=True)
        print("EXEC_NS", res.exec_time_ns, res.mean_exec_time_ns, res.max_exec_time_core_id)
        _NC_CACHE["res"] = res
    else:
        res = run_bass_kernel_spmd(_NC_CACHE["nc"], in_maps, core_ids=list(range(8)))
    outs = [np.asarray(r["yout"], dtype=f32) for r in res.results]
    y_prompt = np.stack(outs[:4], axis=0)
    y_sample = np.concatenate(outs[4:], axis=0)[None]
    return (y_prompt, y_sample)
```

```python
import numpy as np
import ml_dtypes
from contextlib import ExitStack
import concourse.bass as bass
import concourse.mybir as mybir
from concourse.bass_utils import run_bass_kernel_spmd

F32 = mybir.dt.float32
BF16 = mybir.dt.bfloat16
AF = mybir.ActivationFunctionType
ALU = mybir.AluOpType

D = 2048
KT = 16
NB = 17
ROWS = 112 + 16 + 2048 + 16
NTOK = 2048
NIN = 14368
C_Q, C_K, C_V, C_R, C_ZF, C_ZB, C_A, C_GT, C_GA, C_GB = 0, 1024, 2048, 4096, 6144, 6160, 6176, 8224, 10272, 12320
EPS = 1e-6
NOTH = 128 + 3 * 2048
TRACE = False
SKIP_WDMA = False


class Res:
    __slots__ = ("name", "w", "r")

    def __init__(self, name=""):
        self.name = name
        self.w = None
        self.r = []


class Prog:
    SAME_ENGINE_SYNC = True

    def __init__(self, nc, n_dma_sems=10):
        self.nc = nc
        self.eng = {"pe": nc.tensor, "act": nc.scalar, "dve": nc.vector, "pool": nc.gpsimd, "sp": nc.sync}
        self.sem = {k: nc.alloc_semaphore("s_" + k) for k in self.eng}
        self.cnt = {k: 0 for k in self.eng}
        self.waited = {k: {} for k in self.eng}
        self.dsem, self.dcnt, self.dnext = {}, {}, {}
        for q in ("sp", "act", "pool"):
            self.dsem[q] = [nc.alloc_semaphore(f"d_{q}{i}") for i in range(n_dma_sems)]
            self.dcnt[q] = [0] * n_dma_sems
            self.dnext[q] = 0

    def _wait(self, e, ev):
        if ev is None:
            return
        sem, val, src = ev
        if src == e and not (self.SAME_ENGINE_SYNC and e != "pe"):
            return
        key = id(sem)
        if self.waited[e].get(key, 0) >= val:
            return
        self.waited[e][key] = val
        self.eng[e].wait_ge(sem, val)

    def _deps(self, e, reads, writes):
        for r in reads:
            self._wait(e, r.w)
        for w in writes:
            if w.w is not None and w.w[2] != e:
                self._wait(e, w.w)
            for ev in w.r:
                if ev[2] != e:
                    self._wait(e, ev)

    def _commit(self, ev, reads, writes):
        for r in reads:
            r.r.append(ev)
            if len(r.r) > 40:
                r.r = r.r[-40:]
        for w in writes:
            w.w = ev
            w.r = []

    @staticmethod
    def _flat(xs):
        out = []
        for x in xs:
            if isinstance(x, (list, tuple)):
                out.extend(x)
            else:
                out.append(x)
        return out

    def op(self, e, fn, reads=(), writes=()):
        reads, writes = self._flat(reads), self._flat(writes)
        self._deps(e, reads, writes)
        inst = fn(self.eng[e])
        self.cnt[e] += 1
        inst.then_inc(self.sem[e], 1)
        ev = (self.sem[e], self.cnt[e], e)
        self._commit(ev, reads, writes)
        return ev

    def dma(self, q, out, in_, reads=(), writes=(), **kw):
        reads, writes = self._flat(reads), self._flat(writes)
        i = self.dnext[q]
        self.dnext[q] = (i + 1) % len(self.dsem[q])
        sem = self.dsem[q][i]
        if self.dcnt[q][i] > 0:
            self._wait(q, (sem, self.dcnt[q][i], "dma"))
        self._deps(q, reads, writes)
        inst = self.eng[q].dma_start(out=out, in_=in_, **kw)
        self.dcnt[q][i] += 16
        inst.then_inc(sem, 16)
        ev = (sem, self.dcnt[q][i], "dma")
        self._commit(ev, reads, writes)
        return ev

    def barrier(self):
        evs = [(self.sem[k], self.cnt[k], k) for k in self.eng if self.cnt[k] > 0]
        for q in self.dsem:
            for i, sem in enumerate(self.dsem[q]):
                if self.dcnt[q][i] > 0:
                    evs.append((sem, self.dcnt[q][i], "dma"))
        for e in self.eng:
            for ev in evs:
                if ev[2] != e:
                    self._wait(e, ev)

    def finish(self, e, resources):
        for r in resources:
            self._wait(e, r.w)
            for ev in r.r:
                self._wait(e, ev)


class T:
    def __init__(self, t, name):
        self.t = t
        self.r = Res(name)

    def __getitem__(self, k):
        return self.t[k]


def build_program():
    nc = bass.Bass("TRN2", target_bir_lowering=False)
    P = Prog(nc)

    def din(name, shape, dt=F32):
        return nc.dram_tensor(name, list(shape), dt, kind="ExternalInput").ap()

    xin = din("xin", [ROWS, D])
    w_in_l = din("w_in_l", [28, 128, 16, 512])
    wz_l = din("wz_l", [128, 16, 32])
    w_a2_f = din("w_a2_f", [16, 1024]); b_a_f = din("b_a_f", [1, 1024])
    w_a2_b = din("w_a2_b", [16, 1024]); b_a_b = din("b_a_b", [1, 1024])
    w_gla_o_l = din("w_gla_o_l", [4, 128, 16, 512]); w_conv_o_l = din("w_conv_o_l", [4, 128, 16, 512]); w_out_l = din("w_out_l", [4, 128, 16, 512])
    w_up_l = din("w_up_l", [16, 128, 16, 512]); w_down_l = din("w_down_l", [16, 128, 16, 512])
    cols_d = din("cols", [128, 8, 16])
    wdw_d = din("wdw", [128, 16, 31])
    gbc_d = din("gbc", [2, 128, D])
    flags_d = din("flags", [128, 32])
    xoth = din("xoth", [NOTH, D])
    ident_d = din("ident", [128, 128], BF16)
    masks_d = din("masks", [128, 6, 128])
    yout = nc.dram_tensor("yout", [NTOK, D], F32, kind="ExternalOutput").ap()

    s_qdb = nc.dram_tensor("s_qdb", [NB, 128, 1024], BF16).ap()
    s_kdb = nc.dram_tensor("s_kdb", [NB, 128, 1024], BF16).ap()
    s_v = nc.dram_tensor("s_v", [NB, 128, D], BF16).ap()
    s_o = nc.dram_tensor("s_o", [NB, 128, D], F32).ap()
    s_onT = nc.dram_tensor("s_onT", [D, NB * 128], BF16).ap()
    r_scr = {k: Res(k) for k in ["qdb", "qcf", "kdb", "v", "o", "onT", "cc_in", "cc_out", "yout"]}

    def sb(name, shape, dt=F32):
        return T(nc.alloc_sbuf_tensor("t_" + name, list(shape), dt), name)

    psts = [T(nc.alloc_psum_tensor(f"pst{i}", [128, 8, 128], BF16), f"pst{i}") for i in range(2)]
    banks = [T(nc.alloc_psum_tensor(f"pb{i}", [128, 512], F32), f"pb{i}") for i in range(5)]
    pmisc = T(nc.alloc_psum_tensor("pmisc", [128, 512], F32), "pmisc")
    bstate = {"i": 0}

    def nbank():
        b = banks[bstate["i"] % 5]
        bstate["i"] += 1
        return b

    cols = sb("cols", [128, 8, 16])
    flags = sb("flags", [128, 32])
    ident = sb("ident", [128, 128], BF16)
    masks = sb("masks", [128, 6, 128])
    P.dma("sp", cols[:], cols_d, writes=[cols.r])
    P.dma("sp", flags[:], flags_d, writes=[flags.r])
    P.dma("sp", ident[:], ident_d, writes=[ident.r])
    P.dma("sp", masks[:], masks_d, writes=[masks.r])
    CG_PREMIX, CG_PREMLP, C_BDW, C_LNG, C_LNB, C_GGLA, C_BCO = 0, 1, 2, 3, 4, 5, 6
    M_TRII, M_MASKB, M_TRII_S, M_TRIE_S, M_SUF_S, M_ONES_S = 0, 1, 2, 3, 4, 5
    ones = sb("ones", [128, 128])
    P.op("pool", lambda e: e.memset(ones[:], 1.0), writes=[ones.r])
    gcol0 = sb("gcol0", [128, 16])
    P.op("dve", lambda e: e.tensor_scalar(out=gcol0[:], in0=cols[:, CG_PREMIX, :], scalar1=flags[:, 12:13], scalar2=None, op0=ALU.mult),
         reads=[cols.r, flags.r], writes=[gcol0.r])

    wbuf = []
    wstate = {"i": 0}

    def alloc_wbuf(alloc):
        wbuf.clear()
        for i in range(2):
            wb_ = alloc(f"wbuf{i}_{wstate['i']}", [128, 16, 512], BF16)
            wb_.rq = [Res("wq") for _ in range(4)]
            wbuf.append(wb_)

    def load_w(src_ap):
        wb = wbuf[wstate["i"] % 2]
        wstate["i"] += 1
        src = src_ap
        if SKIP_WDMA and wstate["i"] > 2:
            return wb
        for hf in range(2):
            P.dma("pool", wb[:, hf * 8:(hf + 1) * 8, :], src[:, hf * 8:(hf + 1) * 8, :], writes=[wb.rq[2 * hf], wb.rq[2 * hf + 1]])
        return wb

    xt = sb("xt", [128, D])
    junk = sb("junk", [128, D], BF16)
    ub = sb("ub", [128, D], BF16)
    ssq = sb("ssq", [128, 4])
    rstd = sb("rstd", [128, 4])

    def rms_rows(src, n, width, out_bf, nh=1):
        for h in range(nh):
            P.op("act", lambda e, h=h: e.activation(out=junk[0:n, h * width:(h + 1) * width], in_=src[0:n, h * width:(h + 1) * width],
                                                   func=AF.Square, accum_out=ssq[0:n, h:h + 1]),
                 reads=[src.r], writes=[junk.r, ssq.r])
        P.op("act", lambda e: e.activation(out=rstd[0:n, 0:nh], in_=ssq[0:n, 0:nh], func=AF.Sqrt, scale=1.0 / width, bias=EPS),
             reads=[ssq.r], writes=[rstd.r])
        P.op("dve", lambda e: e.reciprocal(out=rstd[0:n, 0:nh], in_=rstd[0:n, 0:nh]), reads=[rstd.r], writes=[rstd.r])
        if out_bf is not None:
            for h in range(nh):
                P.op("dve", lambda e, h=h: e.tensor_scalar(out=out_bf[0:n, h * width:(h + 1) * width], in0=src[0:n, h * width:(h + 1) * width],
                                                          scalar1=rstd[0:n, h:h + 1], scalar2=None, op0=ALU.mult),
                     reads=[src.r, rstd.r], writes=[out_bf.r])

    def transpose_rows(src_bf, n, dstT, col0, scale_cols):
        for g in range(2):
            pst = psts[g]
            for j in range(8):
                kt = g * 8 + j
                P.op("pe", lambda e, kt=kt, j=j: e.transpose(out=pst[:, j, 0:n], in_=src_bf[0:n, kt * 128:(kt + 1) * 128], identity=ident[0:n, 0:n]),
                     reads=[src_bf.r, ident.r], writes=[pst.r])
            for j in range(8):
                kt = g * 8 + j
                if scale_cols is None:
                    P.op("act", lambda e, kt=kt, j=j: e.activation(out=dstT[:, kt, col0:col0 + n], in_=pst[:, j, 0:n], func=AF.Copy),
                         reads=[pst.r], writes=[dstT.r])
                else:
                    sc, scr = scale_cols
                    P.op("act", lambda e, kt=kt, j=j: e.activation(out=dstT[:, kt, col0:col0 + n], in_=pst[:, j, 0:n], func=AF.Copy, scale=sc[:, kt:kt + 1]),
                         reads=[pst.r, scr], writes=[dstT.r])

    esS = ExitStack()
    Sf = T(esS.enter_context(nc.sbuf_tensor("t_Sf", [128, 8, 512], F32)), "Sf")
    Sba = T(esS.enter_context(nc.sbuf_tensor("t_Sba", [128, 8, 512], F32)), "Sba")
    Sf.rk = [Res("Sfk") for _ in range(8)]
    Sba.rk = [Res("Sbak") for _ in range(8)]
    P.op("pool", lambda e: e.memset(Sf[:], 0.0), writes=[Sf.rk])
    P.op("pool", lambda e: e.memset(Sba[:], 0.0), writes=[Sba.rk])

    def decay_z(S, b_flags):
        for d in range(2):
            pb = nbank()
            for kt in range(KT):
                P.op("pe", lambda e, kt=kt: e.matmul(out=pb[0:16, 0:128], lhsT=S.wz[:, kt, d * 16:(d + 1) * 16], rhs=S.uT[:, kt, :], start=(kt == 0), stop=(kt == KT - 1)),
                     reads=[S.wz.r, S.uT.r], writes=[pb.r])
            P.op("dve", lambda e: e.tensor_copy(out=S.zT[d][:], in_=pb[0:16, 0:128]), reads=[pb.r], writes=[S.zT[d].r])
            for cc in range(2):
                pb2 = nbank()
                P.op("pe", lambda e: e.matmul(out=pb2[:], lhsT=S.zT[d][:], rhs=S.wa2[d][:, cc * 512:(cc + 1) * 512], start=True, stop=False),
                     reads=[S.zT[d].r, S.wa2[d].r], writes=[pb2.r])
                P.op("pe", lambda e: e.matmul(out=pb2[:], lhsT=S.ones1[:], rhs=S.ba[d][:, cc * 512:(cc + 1) * 512], start=False, stop=True),
                     reads=[S.ones1.r, S.ba[d].r], writes=[pb2.r])
                P.op("act", lambda e: e.activation(out=S.spl[d][:, cc * 512:(cc + 1) * 512], in_=pb2[:], func=AF.Exp, scale=-1.0),
                     reads=[pb2.r], writes=[S.spl[d].r])
            P.op("act", lambda e: e.activation(out=S.spl[d][:], in_=S.spl[d][:], func=AF.Ln, bias=1.0, scale=1.0), reads=[S.spl[d].r], writes=[S.spl[d].r])
            if b_flags is not None:
                fc = b_flags[d]
                P.op("dve", lambda e: e.tensor_scalar(out=S.spl[d][:], in0=S.spl[d][:], scalar1=flags[:, fc:fc + 1], scalar2=None, op0=ALU.mult),
                     reads=[S.spl[d].r, flags.r], writes=[S.spl[d].r])
    def decay_tot(S):
        for d in range(2):
            for kt in range(8):
                P.op("pe", lambda e, kt=kt: e.matmul(out=pmisc[:, 256 + d * 8 + kt:256 + d * 8 + kt + 1], lhsT=S.spl[d][:, kt * 128:(kt + 1) * 128],
                                                     rhs=masks[:, M_ONES_S, 0:1], start=True, stop=True),
                     reads=[S.spl[d].r, masks.r], writes=[pmisc.r])
        P.op("act", lambda e: e.activation(out=S.Dcols[:], in_=pmisc[:, 256:272], func=AF.Exp), reads=[pmisc.r], writes=[S.Dcols.r])

    def tm_decays(S):
        for d, mi in ((0, M_SUF_S), (1, M_TRIE_S)):
            for cc in range(2):
                pb = nbank()
                P.op("pe", lambda e: e.matmul(out=pb[:], lhsT=masks[:, mi, :], rhs=S.spl[d][:, cc * 512:(cc + 1) * 512], start=True, stop=True),
                     reads=[S.spl[d].r, masks.r], writes=[pb.r])
                P.op("act", lambda e: e.activation(out=S.Gb[d][:, cc * 512:(cc + 1) * 512], in_=pb[:], func=AF.Exp), reads=[pb.r], writes=[S.Gb[d].r])

    class NS:
        pass

    nc.push_named_scope('phase0')
    es0 = ExitStack()

    def sb0(name, shape, dt=F32):
        return T(es0.enter_context(nc.sbuf_tensor("p0_" + name, list(shape), dt)), name)

    S0 = NS()
    S0.uT = sb0("uT", [128, 16, 128], BF16)
    S0.wz = sb0("wz", [128, 16, 32], BF16)
    for qd in range(4):
        P.dma("pool", S0.wz[:, qd * 4:(qd + 1) * 4, :], wz_l[:, qd * 4:(qd + 1) * 4, :], writes=[S0.wz.r])
    S0.wa2 = [sb0("wa2f", [16, 1024]), sb0("wa2b", [16, 1024])]
    S0.ba = [sb0("baf", [1, 1024]), sb0("bab", [1, 1024])]
    P.dma("sp", S0.wa2[0][:], w_a2_f, writes=[S0.wa2[0].r]); P.dma("sp", S0.wa2[1][:], w_a2_b, writes=[S0.wa2[1].r])
    P.dma("sp", S0.ba[0][:], b_a_f, writes=[S0.ba[0].r]); P.dma("sp", S0.ba[1][:], b_a_b, writes=[S0.ba[1].r])
    S0.ones1 = sb0("ones1", [1, 128])
    P.op("pool", lambda e: e.memset(S0.ones1[:], 1.0), writes=[S0.ones1.r])
    S0.zT = [sb0("zfT", [16, 128]), sb0("zbT", [16, 128])]
    S0.spl = [sb0("spf", [128, 1024]), sb0("spb", [128, 1024])]
    S0.Gb = [sb0("G0", [128, 1024]), sb0("G1", [128, 1024])]
    S0.Dcols = sb0("Dcols", [128, 16])
    wkv = sb0("wkv", [128, 16, 3072], BF16)
    wkv_r = [Res("wkv") for _ in range(6)]
    for c in range(6):
        srcw = w_in_l[2 + c]
        for qd in range(4):
            P.dma("pool", wkv[:, qd * 4:(qd + 1) * 4, c * 512:(c + 1) * 512], srcw[:, qd * 4:(qd + 1) * 4, :], writes=[wkv_r[c]])
    ktm0 = sb0("ktm", [128, 1024]); vtm0 = sb0("vtm", [128, D], BF16)
    kendf0 = sb0("kendf", [128, 1024], BF16); kdbtm0 = sb0("kdbtm", [128, 1024], BF16)
    Pacc = sb0("Pacc", [128, 16]); expP = sb0("expP", [128, 16])
    P.op("pool", lambda e: e.memset(Pacc[:], 0.0), writes=[Pacc.r])
    ub2 = sb0("ub2", [128, D], BF16)
    uT2 = sb0("uT2", [128, 16, 128], BF16)
    ubs = [ub, ub2]
    uTs = [S0.uT, uT2]

    def p0_front(pb_i):
        P.dma("sp", xt[:], xoth[pb_i * 128:(pb_i + 1) * 128, :], writes=[xt.r])
        rms_rows(xt, 128, D, ubs[pb_i % 2])
        transpose_rows(ubs[pb_i % 2], 128, uTs[pb_i % 2], 0, (cols[:, CG_PREMIX, :], cols.r))

    NP0 = NOTH // 128
    p0_front(0)
    for pb_i in range(NP0):
        if pb_i == 0:
            bf = (13, 14)
        else:
            j = (pb_i - 1) // 16
            bf = (16 + 2 * j, 17 + 2 * j)
        S0.uT = uTs[pb_i % 2]
        decay_z(S0, bf)
        for c in range(6):
            pb = nbank()
            for kt in range(KT):
                P.op("pe", lambda e, kt=kt: e.matmul(out=pb[:], lhsT=S0.uT[:, kt, :], rhs=wkv[:, kt, c * 512:(c + 1) * 512], start=(kt == 0), stop=(kt == KT - 1)),
                     reads=[wkv_r[c], S0.uT.r], writes=[pb.r])
            if c < 2:
                P.op("dve", lambda e: e.tensor_copy(out=ktm0[:, c * 512:(c + 1) * 512], in_=pb[:]), reads=[pb.r], writes=[ktm0.r])
            else:
                P.op("act", lambda e: e.activation(out=vtm0[:, (c - 2) * 512:(c - 1) * 512], in_=pb[:], func=AF.Copy), reads=[pb.r], writes=[vtm0.r])
        if pb_i + 1 < NP0:
            p0_front(pb_i + 1)
        decay_tot(S0)
        P.op("act", lambda e: e.activation(out=expP[:], in_=Pacc[:], func=AF.Exp), reads=[Pacc.r], writes=[expP.r])
        tm_decays(S0)
        P.op("dve", lambda e: e.scalar_tensor_tensor(out=kendf0[:], in0=ktm0[:], scalar=flags[:, bf[0]:bf[0] + 1], in1=S0.Gb[0][:], op0=ALU.mult, op1=ALU.mult),
             reads=[ktm0.r, flags.r, S0.Gb[0].r], writes=[kendf0.r])
        P.op("dve", lambda e: e.scalar_tensor_tensor(out=kdbtm0[:], in0=ktm0[:], scalar=flags[:, bf[1]:bf[1] + 1], in1=S0.Gb[1][:], op0=ALU.mult, op1=ALU.mult),
             reads=[ktm0.r, flags.r, S0.Gb[1].r], writes=[kdbtm0.r])
        for kt in range(8):
            h = kt // 2
            pu = nbank()
            P.op("pe", lambda e: e.matmul(out=pu[:], lhsT=kendf0[:, kt * 128:(kt + 1) * 128], rhs=vtm0[:, h * 512:(h + 1) * 512], start=True, stop=True),
                 reads=[kendf0.r, vtm0.r], writes=[pu.r])
            P.op("dve", lambda e: e.scalar_tensor_tensor(out=Sf[:, kt, :], in0=Sf[:, kt, :], scalar=S0.Dcols[:, kt:kt + 1], in1=pu[:], op0=ALU.mult, op1=ALU.add),
                 reads=[Sf.rk[kt], S0.Dcols.r, pu.r], writes=[Sf.rk[kt]])
            pu2 = nbank()
            P.op("pe", lambda e: e.matmul(out=pu2[:], lhsT=kdbtm0[:, kt * 128:(kt + 1) * 128], rhs=vtm0[:, h * 512:(h + 1) * 512], start=True, stop=True),
                 reads=[kdbtm0.r, vtm0.r], writes=[pu2.r])
            P.op("dve", lambda e: e.scalar_tensor_tensor(out=Sba[:, kt, :], in0=pu2[:], scalar=expP[:, 8 + kt:9 + kt], in1=Sba[:, kt, :], op0=ALU.mult, op1=ALU.add),
                 reads=[Sba.rk[kt], expP.r, pu2.r], writes=[Sba.rk[kt]])
        P.op("dve", lambda e: e.tensor_tensor(out=Pacc[:], in0=Pacc[:], in1=pmisc[:, 256:272], op=ALU.add), reads=[Pacc.r, pmisc.r], writes=[Pacc.r])
    S0.uT = uTs[0]
    P.dma("sp", xt[:], xin[0:128, :], writes=[xt.r])
    for b in range(NB):
        rms_rows(xt, 128, D, ub)
        transpose_rows(ub, 128, S0.uT, 0, ((gcol0, gcol0.r) if b == 0 else (cols[:, CG_PREMIX, :], cols.r)))
        if b + 1 < NB:
            P.dma("sp", xt[:], xin[(b + 1) * 128:(b + 2) * 128, :], writes=[xt.r])
        for c in range(2, 6):
            pb = nbank()
            for kt in range(KT):
                P.op("pe", lambda e, kt=kt: e.matmul(out=pb[:], lhsT=S0.uT[:, kt, :], rhs=wkv[:, kt, c * 512:(c + 1) * 512], start=(kt == 0), stop=(kt == KT - 1)),
                     reads=[wkv_r[c], S0.uT.r], writes=[pb.r])
            P.op("act", lambda e: e.activation(out=vtm0[:, (c - 2) * 512:(c - 1) * 512], in_=pb[:], func=AF.Copy), reads=[pb.r], writes=[vtm0.r])
        P.dma("sp", s_v[b], vtm0[:], reads=[vtm0.r], writes=[r_scr["v"]])
    P.barrier()
    es0.close()
    nc.pop_named_scope('phase0')
    nc.push_named_scope('sweep1a')

    es = ExitStack()

    def sb1(name, shape, dt=F32):
        return T(es.enter_context(nc.sbuf_tensor("t_" + name, list(shape), dt)), name)

    alloc_wbuf(sb1)
    S1 = NS()
    uT = S1.uT = sb1("uT", [128, 16, 128], BF16)
    S1.wz = sb1("wz", [128, 16, 32], BF16)
    for qd in range(4):
        P.dma("pool", S1.wz[:, qd * 4:(qd + 1) * 4, :], wz_l[:, qd * 4:(qd + 1) * 4, :], writes=[S1.wz.r])
    S1.wa2 = [sb1("wa2f", [16, 1024]), sb1("wa2b", [16, 1024])]
    S1.ba = [sb1("baf", [1, 1024]), sb1("bab", [1, 1024])]
    P.dma("sp", S1.wa2[0][:], w_a2_f, writes=[S1.wa2[0].r]); P.dma("sp", S1.wa2[1][:], w_a2_b, writes=[S1.wa2[1].r])
    P.dma("sp", S1.ba[0][:], b_a_f, writes=[S1.ba[0].r]); P.dma("sp", S1.ba[1][:], b_a_b, writes=[S1.ba[1].r])
    S1.ones1 = sb1("ones1", [1, 128])
    P.op("pool", lambda e: e.memset(S1.ones1[:], 1.0), writes=[S1.ones1.r])
    qT = sb1("qT", [128, 8, 128]); kT = sb1("kT", [128, 8, 128])
    ktm = sb1("ktm", [128, 1024]); vtm = sb1("vtm", [128, D], BF16)
    S1.zT = [sb1("zfT", [16, 128]), sb1("zbT", [16, 128])]
    spl = S1.spl = [sb1("spf", [128, 1024]), sb1("spb", [128, 1024])]
    Eb = [sb1("E0", [128, 8, 128]), sb1("E1", [128, 8, 128])]
    Gb = S1.Gb = [sb1("G0", [128, 1024]), sb1("G1", [128, 1024])]
    qdf = sb1("qdf", [128, 8, 128], BF16); kdf = sb1("kdf", [128, 8, 128], BF16)
    qdb = sb1("qdb", [128, 8, 128], BF16); kdb = sb1("kdb", [128, 8, 128], BF16)
    kendf = sb1("kendf", [128, 1024], BF16); kdbtm = sb1("kdbtm", [128, 1024], BF16)
    st1 = sb1("st1", [128, 128]); st2 = sb1("st2", [128, 128]); ST = sb1("ST", [128, 128], BF16)
    Sfb = sb1("Sfb", [128, 8, 512], BF16)
    opart = sb1("opart", [128, D])
    Dcols = S1.Dcols = sb1("Dcols", [128, 16])
    Dball = sb1("Dball", [128, NB, 8])
    Sfb.rk = [Res("Sfbk") for _ in range(8)]
    P.op("act", lambda e: e.activation(out=Sfb[:], in_=Sf[:], func=AF.Copy), reads=[Sf.rk], writes=[Sfb.rk])

    for b in range(NB):
        P.dma("sp", xt[:], xin[b * 128:(b + 1) * 128, :], writes=[xt.r])
        rms_rows(xt, 128, D, ub)
        transpose_rows(ub, 128, uT, 0, ((gcol0, gcol0.r) if b == 0 else (cols[:, CG_PREMIX, :], cols.r)))
        decay_z(S1, (12, 12) if b == 0 else None)
        P.dma("sp", vtm[:], s_v[b], reads=[r_scr["v"]], writes=[vtm.r])
        for c in range(4):
            wb = load_w(w_in_l[c])
            if c < 4:
                pb = nbank()
                for n in range(4):
                    for kt in range(KT):
                        P.op("pe", lambda e, n=n, kt=kt: e.matmul(out=pb[:, n * 128:(n + 1) * 128], lhsT=wb[:, kt, n * 128:(n + 1) * 128], rhs=uT[:, kt, :],
                                                                  start=(kt == 0), stop=(kt == KT - 1)),
                             reads=[wb.rq[kt // 4], uT.r], writes=[pb.r])
                dst = qT if c < 2 else kT
                cc = c % 2
                P.op("act", lambda e: e.activation(out=dst[:, cc * 4:(cc + 1) * 4, :], in_=pb[:].rearrange("p (a b) -> p a b", a=4), func=AF.Copy,
                                                   scale=(1.0 / 16 if c < 2 else 1.0)),
                     reads=[pb.r], writes=[dst.r])
            if c >= 2:
                pb = nbank()
                for kt in range(KT):
                    P.op("pe", lambda e, kt=kt: e.matmul(out=pb[:], lhsT=uT[:, kt, :], rhs=wb[:, kt, :], start=(kt == 0), stop=(kt == KT - 1)),
                         reads=[wb.rq[kt // 4], uT.r], writes=[pb.r])
                if c < 4:
                    P.op("dve", lambda e: e.tensor_copy(out=ktm[:, (c - 2) * 512:(c - 1) * 512], in_=pb[:]), reads=[pb.r], writes=[ktm.r])
                else:
                    P.op("dve", lambda e: e.tensor_copy(out=vtm[:, (c - 4) * 512:(c - 3) * 512], in_=pb[:]), reads=[pb.r], writes=[vtm.r])
        decay_tot(S1)
        if b >= 1:
            P.op("dve", lambda e: e.tensor_copy(out=Dball[:, b, :], in_=Dcols[:, 8:16]), reads=[Dcols.r], writes=[Dball.r])
        pa, pb_ = nbank(), nbank()
        for kt in range(8):
            tgt = pa if kt < 4 else pb_
            P.op("pe", lambda e, kt=kt, tgt=tgt: e.matmul(out=tgt[:, (kt % 4) * 128:(kt % 4 + 1) * 128], lhsT=spl[0][:, kt * 128:(kt + 1) * 128],
                                                          rhs=masks[:, M_TRII_S, :], start=True, stop=True),
                 reads=[spl[0].r, masks.r], writes=[tgt.r])
        for half, tgt in enumerate((pa, pb_)):
            P.op("act", lambda e, half=half, tgt=tgt: e.activation(out=Eb[0][:, half * 4:(half + 1) * 4, :], in_=tgt[:].rearrange("p (a b) -> p a b", a=4), func=AF.Exp),
                 reads=[tgt.r], writes=[Eb[0].r])
            P.op("act", lambda e, half=half, tgt=tgt: e.activation(out=Eb[1][:, half * 4:(half + 1) * 4, :], in_=tgt[:].rearrange("p (a b) -> p a b", a=4), func=AF.Exp, scale=-1.0),
                 reads=[tgt.r], writes=[Eb[1].r])
        P.op("dve", lambda e: e.tensor_tensor(out=qdf[:], in0=qT[:], in1=Eb[0][:], op=ALU.mult), reads=[qT.r, Eb[0].r], writes=[qdf.r])
        P.op("dve", lambda e: e.tensor_tensor(out=kdf[:], in0=kT[:], in1=Eb[1][:], op=ALU.mult), reads=[kT.r, Eb[1].r], writes=[kdf.r])
        pa, pb_ = nbank(), nbank()
        for kt in range(8):
            tgt = pa if kt < 4 else pb_
            P.op("pe", lambda e, kt=kt, tgt=tgt: e.matmul(out=tgt[:, (kt % 4) * 128:(kt % 4 + 1) * 128], lhsT=spl[1][:, kt * 128:(kt + 1) * 128],
                                                          rhs=masks[:, M_TRIE_S, :], start=True, stop=True),
                 reads=[spl[1].r, masks.r], writes=[tgt.r])
        for half, tgt in enumerate((pa, pb_)):
            P.op("act", lambda e, half=half, tgt=tgt: e.activation(out=Eb[0][:, half * 4:(half + 1) * 4, :], in_=tgt[:].rearrange("p (a b) -> p a b", a=4), func=AF.Exp, scale=-1.0),
                 reads=[tgt.r], writes=[Eb[0].r])
            P.op("act", lambda e, half=half, tgt=tgt: e.activation(out=Eb[1][:, half * 4:(half + 1) * 4, :], in_=tgt[:].rearrange("p (a b) -> p a b", a=4), func=AF.Exp),
                 reads=[tgt.r], writes=[Eb[1].r])
        P.op("dve", lambda e: e.tensor_tensor(out=qdb[:], in0=qT[:], in1=Eb[0][:], op=ALU.mult), reads=[qT.r, Eb[0].r], writes=[qdb.r])
        P.op("dve", lambda e: e.tensor_tensor(out=kdb[:], in0=kT[:], in1=Eb[1][:], op=ALU.mult), reads=[kT.r, Eb[1].r], writes=[kdb.r])
        tm_decays(S1)
        P.op("dve", lambda e: e.tensor_tensor(out=kendf[:], in0=ktm[:], in1=Gb[0][:], op=ALU.mult), reads=[ktm.r, Gb[0].r], writes=[kendf.r])
        P.op("dve", lambda e: e.tensor_tensor(out=kdbtm[:], in0=ktm[:], in1=Gb[1][:], op=ALU.mult), reads=[ktm.r, Gb[1].r], writes=[kdbtm.r])
        for h in range(4):
            if b >= 1:
                for di, (kd, qd) in enumerate(((kdf, qdf), (kdb, qdb))):
                    for j in range(2):
                        kt = 2 * h + j
                        P.op("pe", lambda e, kt=kt, j=j: e.matmul(out=pmisc[:, di * 128:(di + 1) * 128], lhsT=kd[:, kt, :], rhs=qd[:, kt, :], start=(j == 0), stop=(j == 1)),
                             reads=[kd.r, qd.r], writes=[pmisc.r])
                P.op("dve", lambda e: e.tensor_tensor(out=st1[:], in0=pmisc[:, 0:128], in1=masks[:, M_TRII, :], op=ALU.mult), reads=[pmisc.r, masks.r], writes=[st1.r])
                P.op("dve", lambda e: e.tensor_tensor(out=st2[:], in0=pmisc[:, 128:256], in1=masks[:, M_MASKB, :], op=ALU.mult), reads=[pmisc.r, masks.r], writes=[st2.r])
                P.op("dve", lambda e: e.tensor_tensor(out=ST[:], in0=st1[:], in1=st2[:], op=ALU.add), reads=[st1.r, st2.r], writes=[ST.r])
                po = nbank()
                P.op("pe", lambda e: e.matmul(out=po[:], lhsT=ST[:], rhs=vtm[:, h * 512:(h + 1) * 512], start=True, stop=False),
                     reads=[ST.r, vtm.r], writes=[po.r])
                for j in range(2):
                    kt = 2 * h + j
                    P.op("pe", lambda e, kt=kt, j=j: e.matmul(out=po[:], lhsT=qdf[:, kt, :], rhs=Sfb[:, kt, :], start=False, stop=(j == 1)),
                         reads=[qdf.r, Sfb.rk[kt]], writes=[po.r])
                P.op("act", lambda e: e.activation(out=opart[:, h * 512:(h + 1) * 512], in_=po[:], func=AF.Copy), reads=[po.r], writes=[opart.r])
            for j in range(2):
                kt = 2 * h + j
                pu = nbank()
                P.op("pe", lambda e, kt=kt: e.matmul(out=pu[:], lhsT=kendf[:, kt * 128:(kt + 1) * 128], rhs=vtm[:, h * 512:(h + 1) * 512], start=True, stop=True),
                     reads=[kendf.r, vtm.r], writes=[pu.r])
                P.op("dve", lambda e, kt=kt: e.scalar_tensor_tensor(out=Sf[:, kt, :], in0=Sf[:, kt, :], scalar=Dcols[:, kt:kt + 1], in1=pu[:], op0=ALU.mult, op1=ALU.add),
                     reads=[Sf.rk[kt], Dcols.r, pu.r], writes=[Sf.rk[kt]])
                P.op("act", lambda e, kt=kt: e.activation(out=Sfb[:, kt, :], in_=Sf[:, kt, :], func=AF.Copy), reads=[Sf.rk[kt]], writes=[Sfb.rk[kt]])
        if b >= 1:
            P.dma("sp", s_qdb[b].rearrange("p (a b) -> p a b", a=8), qdb[:], reads=[qdb.r], writes=[r_scr["qdb"]])
            P.dma("sp", s_kdb[b], kdbtm[:], reads=[kdbtm.r], writes=[r_scr["kdb"]])
            P.dma("sp", s_o[b], opart[:], reads=[opart.r], writes=[r_scr["o"]])

    nc.pop_named_scope('sweep1a')
    nc.push_named_scope('sweep1b')
    Sb = Sba
    Sbb = Sfb
    ofull = opart
    on_bf = ub
    onT = uT
    sets1b = [(qdb, kdbtm, vtm, xt),
              (sb1("qdb2", [128, 8, 128], BF16), sb1("kdbtm2", [128, 1024], BF16), sb1("vtm2", [128, D], BF16), sb1("xo2", [128, D]))]

    def load1b(bb, st):
        P.dma("sp", st[0][:], s_qdb[bb].rearrange("p (a b) -> p a b", a=8), reads=[r_scr["qdb"]], writes=[st[0].r])
        P.dma("sp", st[1][:], s_kdb[bb], reads=[r_scr["kdb"]], writes=[st[1].r])
        P.dma("sp", st[2][:], s_v[bb], reads=[r_scr["v"]], writes=[st[2].r])
        P.dma("sp", st[3][:], s_o[bb], reads=[r_scr["o"]], writes=[st[3].r])

    blocks1b = list(range(NB - 1, 0, -1))
    load1b(blocks1b[0], sets1b[0])
    for i1b, b in enumerate(blocks1b):
        qdb_, kdbtm_, vtm_, xo_ = sets1b[i1b % 2]
        if i1b + 1 < len(blocks1b):
            load1b(blocks1b[i1b + 1], sets1b[(i1b + 1) % 2])
        for kt in range(8):
            P.op("dve", lambda e, kt=kt: e.tensor_scalar(out=Sb[:, kt, :], in0=Sb[:, kt, :], scalar1=Dball[:, b, kt:kt + 1], scalar2=None, op0=ALU.mult),
                 reads=[Sb.rk[kt], Dball.r], writes=[Sb.rk[kt]])
            P.op("act", lambda e, kt=kt: e.activation(out=Sbb[:, kt, :], in_=Sb[:, kt, :], func=AF.Copy), reads=[Sb.rk[kt]], writes=[Sbb.rk[kt]])
        for h in range(4):
            po = nbank()
            for j in range(2):
                kt = 2 * h + j
                P.op("pe", lambda e, kt=kt, j=j: e.matmul(out=po[:], lhsT=qdb_[:, kt, :], rhs=Sbb[:, kt, :], start=(j == 0), stop=(j == 1)),
                     reads=[qdb_.r, Sbb.rk[kt]], writes=[po.r])
            P.op("dve", lambda e: e.tensor_tensor(out=ofull[:, h * 512:(h + 1) * 512], in0=po[:], in1=xo_[:, h * 512:(h + 1) * 512], op=ALU.add),
                 reads=[po.r, xo_.r], writes=[ofull.r])
            for j in range(2):
                kt = 2 * h + j
                pu = nbank()
                P.op("pe", lambda e, kt=kt: e.matmul(out=pu[:], lhsT=kdbtm_[:, kt * 128:(kt + 1) * 128], rhs=vtm_[:, h * 512:(h + 1) * 512], start=True, stop=True),
                     reads=[kdbtm_.r, vtm_.r], writes=[pu.r])
                P.op("dve", lambda e, kt=kt: e.tensor_tensor(out=Sb[:, kt, :], in0=Sb[:, kt, :], in1=pu[:], op=ALU.add), reads=[Sb.rk[kt], pu.r], writes=[Sb.rk[kt]])
        rms_rows(ofull, 128, 512, on_bf, nh=4)
        transpose_rows(on_bf, 128, onT, 0, None)
        P.dma("sp", s_onT.rearrange("(vt p) t -> p vt t", p=128)[:, :, b * 128:(b + 1) * 128], onT[:], reads=[onT.r], writes=[r_scr["onT"]])
    P.barrier()
    es.close()
    esS.close()
    nc.pop_named_scope('sweep1b')
    nc.push_named_scope('sweep2')

    TB = 512
    NTB = 4
    W = TB + 32
    alloc_wbuf(sb)
    wdw = sb("wdw", [128, 16, 31])
    P.dma("sp", wdw[:], wdw_d, writes=[wdw.r])
    uW = sb("uW", [128, 16, W], BF16)
    htmp = sb("htmp", [128, TB])
    OFF = dict(A=0, B=16384, B2=32768, C=34816, Dd=67584, Ee=83968, Ff=100352, END=133120)
    arena = nc.alloc_sbuf_tensor("t_arena2", [128, OFF["END"] // 4], F32)
    RR = {k: Res("reg" + k) for k in ("A", "B", "B2", "C", "Dd", "Ee", "Ff", "Ff2")}

    class V:
        def __init__(self, start, nbytes, dt, regs, pattern=None, **kw):
            ap = arena[:, start // 4:(start + nbytes) // 4]
            if dt == BF16:
                ap = ap.bitcast(BF16)
            if pattern is not None:
                ap = ap.rearrange(pattern, **kw)
            self.t = ap
            self.r = [RR[k] for k in regs]

        def __getitem__(self, k):
            return self.t[k]

    gT = V(OFF["A"], 16 * W * 4, F32, ("A", "B", "B2"), "p (k t) -> p k t", k=16)
    yT = V(OFF["C"], 32768, F32, ("C",), "p (k t) -> p k t", k=16)
    yact = V(OFF["A"], 16384, BF16, ("A",), "p (k t) -> p k t", k=16)
    sgb = V(OFF["Ee"], 16384, BF16, ("Ee",), "p (k t) -> p k t", k=16)
    mb = V(OFF["C"], 32768, F32, ("C",), "p (k t) -> p k t", k=16)
    ract = V(OFF["Ff"] + 10752, 16384, BF16, ("Ff2",), "p (k t) -> p k t", k=16)
    ogT = V(OFF["B"], 16384, BF16, ("B",), "p (k t) -> p k t", k=16)
    sga = V(OFF["Dd"], 16384, BF16, ("Dd",), "p (k t) -> p k t", k=16)
    mixT = V(OFF["Ee"], 16384, BF16, ("Ee",), "p (k t) -> p k t", k=16)
    sg = V(OFF["Ff"], W * 4, F32, ("Ff",))
    ysq = V(OFF["Ff"] + 2304, 2048, F32, ("Ff",))
    mu = V(OFF["Ff"] + 4352, 2048, F32, ("Ff",))
    var = V(OFF["Ff"] + 6400, 2048, F32, ("Ff",))
    rs = V(OFF["Ff"] + 8448, 2048, F32, ("Ff",))
    mix2 = V(OFF["A"], 32768, F32, ("A", "B"), "p (k t) -> p k t", k=4)
    h1 = V(OFF["C"], 32768, F32, ("C",), "p (k t) -> p k t", k=4)
    hidT = V(OFF["Dd"], 65536, BF16, ("Dd", "Ee", "Ff", "Ff2"), "p (k t) -> p k t", k=64)
    ftm = mix2
    u2T = uW

    def fm_proj(src_ap_fn, nchunks, rhs, col0, ncols, evac):
        parts = [(0, ncols)] if ncols <= 512 else [(0, ncols // 2), (ncols // 2, ncols - ncols // 2)]
        for c in range(nchunks):
            wb = load_w(src_ap_fn(c))
            for n in range(4):
                for (c0, nn) in parts:
                    pb = nbank()
                    for kt in range(KT):
                        P.op("pe", lambda e, n=n, kt=kt: e.matmul(out=pb[:, 0:nn], lhsT=wb[:, kt, n * 128:(n + 1) * 128], rhs=rhs[:, kt, col0 + c0:col0 + c0 + nn],
                                                                  start=(kt == 0), stop=(kt == KT - 1)),
                             reads=[wb.rq[kt // 4], rhs.r], writes=[pb.r])
                    evac(c * 4 + n, pb, c0, nn)

    yv = yout.rearrange("(s k p) d -> s p k d", k=NTB, p=128)

    def prep_norm(sb_i, m):
        rr = 112 + TB * sb_i
        n = 128 if m < NTB else 32
        P.dma("sp", xt[0:n, :], xin[rr + 128 * m:rr + 128 * m + n, :], writes=[xt.r])
        rms_rows(xt, n, D, ub)

    def prep_tr(m):
        n = 128 if m < NTB else 32
        transpose_rows(ub, n, uW, 128 * m, (cols[:, CG_PREMIX, :], cols.r))

    for sbk in range(NTOK // TB):
        r0 = 112 + TB * sbk
        t0 = r0 + 16
        if sbk == 0:
            for m in range(NTB + 1):
                prep_norm(0, m)
                prep_tr(m)
        def ev_a(t, pb, c0, n):
            P.op("act", lambda e: e.activation(out=gT[:, t, c0:c0 + n], in_=pb[:, 0:n], func=AF.Copy), reads=[pb.r], writes=[gT.r])
        fm_proj(lambda c: w_in_l[12 + c], 4, uW, 0, W, ev_a)

        def ev_gt(t, pb, c0, n):
            P.op("act", lambda e: e.activation(out=sg[:, c0:c0 + n], in_=pb[:, 0:n], func=AF.Sigmoid), reads=[pb.r], writes=[sg.r])
            P.op("dve", lambda e: e.tensor_tensor(out=gT[:, t, c0:c0 + n], in0=gT[:, t, c0:c0 + n], in1=sg[:, c0:c0 + n], op=ALU.mult), reads=[gT.r, sg.r], writes=[gT.r])
        fm_proj(lambda c: w_in_l[16 + c], 4, uW, 0, W, ev_gt)
        yct = [Res("yct") for _ in range(16)]
        for ct in range(16):
            P.op("dve", lambda e, ct=ct: e.tensor_scalar(out=yT[:, ct, :], in0=gT[:, ct, 1:1 + TB], scalar1=wdw[:, ct, 0:1], scalar2=cols[:, C_BDW, ct:ct + 1],
                                                         op0=ALU.mult, op1=ALU.add),
                 reads=[gT.r, wdw.r, cols.r], writes=[yT.r, yct[ct]])
        for tap in range(1, 31):
            for ct in range(16):
                P.op("dve", lambda e, ct=ct, tap=tap: e.scalar_tensor_tensor(out=yT[:, ct, :], in0=gT[:, ct, tap + 1:tap + 1 + TB], scalar=wdw[:, ct, tap:tap + 1],
                                                                             in1=yT[:, ct, :], op0=ALU.mult, op1=ALU.add),
                     reads=[gT.r, wdw.r, yct[ct]], writes=[yT.r, yct[ct]])
        def ev_gb(t, pb, c0, n):
            P.op("act", lambda e: e.activation(out=sgb[:, t, :], in_=pb[:, 0:n], func=AF.Sigmoid), reads=[pb.r], writes=[sgb.r])
        fm_proj(lambda c: w_in_l[24 + c], 4, uW, 16, TB, ev_gb)

        def ev_ga(t, pb, c0, n):
            P.op("act", lambda e: e.activation(out=sga[:, t, :], in_=pb[:, 0:n], func=AF.Sigmoid), reads=[pb.r], writes=[sga.r])
        fm_proj(lambda c: w_in_l[20 + c], 4, uW, 16, TB, ev_ga)

        def ev_r(t, pb, c0, n):
            P.op("act", lambda e: e.activation(out=ract[:, t, :], in_=pb[:, 0:n], func=AF.Silu), reads=[pb.r], writes=[ract.r])
        fm_proj(lambda c: w_in_l[8 + c], 4, uW, 16, TB, ev_r)
        ps1, ps2 = nbank(), nbank()
        for ct in range(16):
            P.op("pe", lambda e, ct=ct: e.matmul(out=ps1[:], lhsT=ones[:], rhs=yT[:, ct, :], start=(ct == 0), stop=(ct == 15)),
                 reads=[ones.r, yT.r], writes=[ps1.r])
        for ct in range(16):
            P.op("act", lambda e, ct=ct: e.activation(out=ysq[:], in_=yT[:, ct, :], func=AF.Square), reads=[yT.r], writes=[ysq.r])
            P.op("pe", lambda e, ct=ct: e.matmul(out=ps2[:], lhsT=ones[:], rhs=ysq[:], start=(ct == 0), stop=(ct == 15)),
                 reads=[ones.r, ysq.r], writes=[ps2.r])
        P.op("dve", lambda e: e.tensor_scalar(out=mu[:], in0=ps1[:], scalar1=1.0 / D, scalar2=None, op0=ALU.mult), reads=[ps1.r], writes=[mu.r])
        P.op("dve", lambda e: e.tensor_tensor(out=var[:], in0=mu[:], in1=mu[:], op=ALU.mult), reads=[mu.r], writes=[var.r])
        P.op("dve", lambda e: e.scalar_tensor_tensor(out=var[:], in0=ps2[:], scalar=1.0 / D, in1=var[:], op0=ALU.mult, op1=ALU.subtract),
             reads=[ps2.r, var.r], writes=[var.r])
        P.op("act", lambda e: e.activation(out=rs[:], in_=var[:], func=AF.Sqrt, scale=1.0, bias=EPS), reads=[var.r], writes=[rs.r])
        P.op("dve", lambda e: e.reciprocal(out=rs[:], in_=rs[:]), reads=[rs.r], writes=[rs.r])
        for ct in range(16):
            P.op("dve", lambda e, ct=ct: e.tensor_tensor(out=yT[:, ct, :], in0=yT[:, ct, :], in1=mu[:], op=ALU.subtract), reads=[yT.r, mu.r], writes=[yT.r])
            P.op("dve", lambda e, ct=ct: e.tensor_tensor(out=yT[:, ct, :], in0=yT[:, ct, :], in1=rs[:], op=ALU.mult), reads=[yT.r, rs.r], writes=[yT.r])
        for ct in range(16):
            P.op("act", lambda e, ct=ct: e.activation(out=yact[:, ct, :], in_=yT[:, ct, :], func=AF.Silu, scale=cols[:, C_LNG, ct:ct + 1], bias=cols[:, C_LNB, ct:ct + 1]),
                 reads=[yT.r, cols.r], writes=[yact.r])
        def ev_yb(t, pb, c0, n):
            P.op("dve", lambda e: e.scalar_tensor_tensor(out=mb[:, t, :], in0=pb[:, 0:n], scalar=cols[:, C_BCO, t:t + 1], in1=sgb[:, t, :], op0=ALU.add, op1=ALU.mult),
                 reads=[pb.r, cols.r, sgb.r], writes=[mb.r])
        fm_proj(lambda c: w_conv_o_l[c], 4, yact, 0, TB, ev_yb)
        P.dma("sp", ogT[:], s_onT.rearrange("(vt p) t -> p vt t", p=128)[:, :, t0:t0 + TB], reads=[r_scr["onT"]], writes=[ogT.r])
        for vt in range(16):
            P.op("dve", lambda e, vt=vt: e.scalar_tensor_tensor(out=ogT[:, vt, :], in0=ogT[:, vt, :], scalar=cols[:, C_GGLA, vt:vt + 1], in1=ract[:, vt, :], op0=ALU.mult, op1=ALU.mult),
                 reads=[ogT.r, cols.r, ract.r], writes=[ogT.r])

        def ev_ya(t, pb, c0, n):
            P.op("dve", lambda e: e.tensor_tensor(out=htmp[:], in0=pb[:, 0:n], in1=sga[:, t, :], op=ALU.mult), reads=[pb.r, sga.r], writes=[htmp.r])
            P.op("dve", lambda e: e.tensor_tensor(out=mixT[:, t, :], in0=htmp[:], in1=mb[:, t, :], op=ALU.add), reads=[htmp.r, mb.r], writes=[mixT.r])
        fm_proj(lambda c: w_gla_o_l[c], 4, ogT, 0, TB, ev_ya)
        for c in range(4):
            wb = load_w(w_out_l[c])
            for tb in range(NTB):
                pb = nbank()
                for kt in range(KT):
                    P.op("pe", lambda e, kt=kt: e.matmul(out=pb[:], lhsT=mixT[:, kt, tb * 128:(tb + 1) * 128], rhs=wb[:, kt, :], start=(kt == 0), stop=(kt == KT - 1)),
                         reads=[wb.rq[kt // 4], mixT.r], writes=[pb.r])
                P.op("act", lambda e: e.activation(out=mix2[:, tb, c * 512:(c + 1) * 512], in_=pb[:], func=AF.Copy), reads=[pb.r], writes=[mix2.r])
        P.dma("sp", h1[:], xin[t0:t0 + TB, :].rearrange("(k p) d -> p k d", p=128), writes=[h1.r])
        P.dma("sp", xt[:], gbc_d[0], writes=[xt.r])
        for tb in range(NTB):
            for h in range(1):
                P.op("act", lambda e: e.activation(out=junk[:], in_=mix2[:, tb, :], func=AF.Square, accum_out=ssq[:, 0:1]), reads=[mix2.r], writes=[junk.r, ssq.r])
            P.op("act", lambda e: e.activation(out=rstd[:, 0:1], in_=ssq[:, 0:1], func=AF.Sqrt, scale=1.0 / D, bias=EPS), reads=[ssq.r], writes=[rstd.r])
            P.op("dve", lambda e: e.reciprocal(out=rstd[:, 0:1], in_=rstd[:, 0:1]), reads=[rstd.r], writes=[rstd.r])
            P.op("dve", lambda e: e.scalar_tensor_tensor(out=mix2[:, tb, :], in0=mix2[:, tb, :], scalar=rstd[:, 0:1], in1=xt[:], op0=ALU.mult, op1=ALU.mult),
                 reads=[mix2.r, rstd.r, xt.r], writes=[mix2.r])
            P.op("dve", lambda e: e.tensor_tensor(out=h1[:, tb, :], in0=h1[:, tb, :], in1=mix2[:, tb, :], op=ALU.add), reads=[h1.r, mix2.r], writes=[h1.r])
        for tb in range(NTB):
            P.op("act", lambda e: e.activation(out=junk[:], in_=h1[:, tb, :], func=AF.Square, accum_out=ssq[:, 0:1]), reads=[h1.r], writes=[junk.r, ssq.r])
            P.op("act", lambda e: e.activation(out=rstd[:, 0:1], in_=ssq[:, 0:1], func=AF.Sqrt, scale=1.0 / D, bias=EPS), reads=[ssq.r], writes=[rstd.r])
            P.op("dve", lambda e: e.reciprocal(out=rstd[:, 0:1], in_=rstd[:, 0:1]), reads=[rstd.r], writes=[rstd.r])
            P.op("dve", lambda e: e.tensor_scalar(out=ub[:], in0=h1[:, tb, :], scalar1=rstd[:, 0:1], scalar2=None, op0=ALU.mult), reads=[h1.r, rstd.r], writes=[ub.r])
            transpose_rows(ub, 128, u2T, 128 * tb, (cols[:, CG_PREMLP, :], cols.r))

        def ev_up(t, pb, c0, n):
            P.op("act", lambda e: e.activation(out=htmp[:], in_=pb[:, 0:n], func=AF.Relu), reads=[pb.r], writes=[htmp.r])
            P.op("dve", lambda e: e.tensor_tensor(out=hidT[:, t, :], in0=htmp[:], in1=htmp[:], op=ALU.mult), reads=[htmp.r], writes=[hidT.r])
        fm_proj(lambda c: w_up_l[c], 16, u2T, 0, TB, ev_up)
        nxt = sbk + 1 < NTOK // TB
        for c in range(4):
            if nxt:
                prep_norm(sbk + 1, c)
            pbs = [nbank() for _ in range(NTB)]
            for kg in range(4):
                wb = load_w(w_down_l[kg * 4 + c])
                for tb in range(NTB):
                    for kt in range(KT):
                        P.op("pe", lambda e, kt=kt: e.matmul(out=pbs[tb][:], lhsT=hidT[:, kg * 16 + kt, tb * 128:(tb + 1) * 128], rhs=wb[:, kt, :],
                                                             start=(kg == 0 and kt == 0), stop=(kg == 3 and kt == KT - 1)),
                             reads=[wb.rq[kt // 4], hidT.r], writes=[pbs[tb].r])
            for tb in range(NTB):
                P.op("act", lambda e: e.activation(out=ftm[:, tb, c * 512:(c + 1) * 512], in_=pbs[tb][:], func=AF.Copy), reads=[pbs[tb].r], writes=[ftm.r])
            if nxt:
                prep_tr(c)
        if nxt:
            prep_norm(sbk + 1, NTB)
            prep_tr(NTB)
        P.dma("sp", xt[:], gbc_d[1], writes=[xt.r])
        for tb in range(NTB):
            P.op("act", lambda e: e.activation(out=junk[:], in_=ftm[:, tb, :], func=AF.Square, accum_out=ssq[:, 0:1]), reads=[ftm.r], writes=[junk.r, ssq.r])
            P.op("act", lambda e: e.activation(out=rstd[:, 0:1], in_=ssq[:, 0:1], func=AF.Sqrt, scale=1.0 / D, bias=EPS), reads=[ssq.r], writes=[rstd.r])
            P.op("dve", lambda e: e.reciprocal(out=rstd[:, 0:1], in_=rstd[:, 0:1]), reads=[rstd.r], writes=[rstd.r])
            P.op("dve", lambda e: e.scalar_tensor_tensor(out=ftm[:, tb, :], in0=ftm[:, tb, :], scalar=rstd[:, 0:1], in1=xt[:], op0=ALU.mult, op1=ALU.mult),
                 reads=[ftm.r, rstd.r, xt.r], writes=[ftm.r])
            P.op("dve", lambda e: e.tensor_tensor(out=h1[:, tb, :], in0=h1[:, tb, :], in1=ftm[:, tb, :], op=ALU.add), reads=[h1.r, ftm.r], writes=[h1.r])
        P.dma("sp", yv[sbk], h1[:], reads=[h1.r], writes=[r_scr["yout"]])

    P.finish("sp", [r_scr["yout"]])
    nc.pop_named_scope('sweep2')
    return nc


_NC_CACHE = {}


def kernel(x_prompt, x_sample, meta_tokens, g_pre_mix, w_in, w_a2_f, b_a_f, w_a2_b, b_a_b,
           g_gla, w_gla_o, w_dw, b_dw, ln_g, ln_b, w_conv_o, b_conv_o, w_out, g_post_mix,
           g_pre_mlp, w_up, w_down, g_post_mlp):
    f32 = np.float32
    A = lambda a: np.ascontiguousarray(np.asarray(a, dtype=f32))
    x_prompt, x_sample, meta = A(x_prompt), A(x_sample), A(meta_tokens)
    col = lambda v: np.ascontiguousarray(np.asarray(v, f32).reshape(16, 128).T)
    cols = np.zeros((128, 8, 16), f32)
    for i, v in enumerate((g_pre_mix, g_pre_mlp, b_dw, ln_g, ln_b, g_gla, b_conv_o)):
        cols[:, i, :] = col(v)
    wdw = np.ascontiguousarray(np.asarray(w_dw, f32)[0].T.reshape(16, 128, 31).transpose(1, 0, 2))
    gbc = np.ascontiguousarray(np.stack([np.broadcast_to(np.asarray(g_post_mix, f32).reshape(1, D), (128, D)),
                                         np.broadcast_to(np.asarray(g_post_mlp, f32).reshape(1, D), (128, D))]))
    ident = np.eye(128).astype(ml_dtypes.bfloat16)
    jj, ii = np.meshgrid(np.arange(128), np.arange(128), indexing="ij")
    s = f32(-1.0 / 16)
    masks = np.zeros((128, 6, 128), f32)
    masks[:, 0, :] = (jj <= ii)
    masks[:, 1, :] = (jj > ii)
    masks[:, 2, :] = (jj <= ii) * s
    masks[:, 3, :] = (jj < ii) * s
    masks[:, 4, :] = (jj > ii) * s
    masks[:, 5, :] = s
    def chunks(Wm, col_starts, width=512):
        return np.ascontiguousarray(np.stack([Wm[:, cs:cs + width].reshape(16, 128, width).transpose(1, 0, 2) for cs in col_starts]))
    Win = A(w_in)[0]
    starts = [512 * i for i in range(8)] + [C_R + 512 * i for i in range(4)] + [C_A + 512 * i for i in range(4)] + \
             [C_GT + 512 * i for i in range(4)] + [C_GA + 512 * i for i in range(4)] + [C_GB + 512 * i for i in range(4)]
    Wd = A(w_down)[0]
    shared = dict(
        w_in_l=chunks(Win, starts), wz_l=chunks(Win, [C_ZF], 32)[0],
        w_a2_f=A(w_a2_f)[0], b_a_f=A(b_a_f), w_a2_b=A(w_a2_b)[0], b_a_b=A(b_a_b),
        w_gla_o_l=chunks(A(w_gla_o)[0], [0, 512, 1024, 1536]), w_conv_o_l=chunks(A(w_conv_o)[0], [0, 512, 1024, 1536]),
        w_out_l=chunks(A(w_out)[0], [0, 512, 1024, 1536]), w_up_l=chunks(A(w_up)[0], [512 * i for i in range(16)]),
        w_down_l=np.ascontiguousarray(np.concatenate([chunks(Wd[kg * 2048:(kg + 1) * 2048], [0, 512, 1024, 1536]) for kg in range(4)])),
        cols=cols, wdw=wdw, gbc=gbc, ident=ident, masks=masks)
    in_maps = []
    xs = x_sample[0]
    for c in range(8):
        xin = np.zeros((ROWS, D), f32)
        fl = np.zeros((128, 32), f32)
        xoth = np.zeros((NOTH, D), f32)
        if c < 4:
            xin[112:128] = meta
            xin[128:2176] = x_prompt[c]
        else:
            q = c - 4
            xin[112:128] = meta if q == 0 else xs[2048 * q - 16:2048 * q]
            xin[128:2176] = xs[2048 * q:2048 * (q + 1)]
            if q < 3:
                xin[2176:2192] = xs[2048 * (q + 1):2048 * (q + 1) + 16]
            xoth[112:128] = meta
            fl[:, 13] = 1.0 if q > 0 else 0.0
            others = [o for o in range(4) if o != q]
            for j, o in enumerate(others):
                xoth[128 + 2048 * j:128 + 2048 * (j + 1)] = xs[2048 * o:2048 * (o + 1)]
                fl[:, 16 + 2 * j] = 1.0 if o < q else 0.0
                fl[:, 17 + 2 * j] = 1.0 if o > q else 0.0
        fl[:, 12] = 1.0 if c <= 4 else 0.0
        d = dict(shared)
        d["xoth"] = xoth
        d["xin"] = xin
        d["flags"] = fl
        in_maps.append(d)
    if "nc" not in _NC_CACHE:
        _NC_CACHE["nc"] = build_program()
    if TRACE:
        res = run_bass_kernel_spmd(_NC_CACHE["nc"], in_maps, core_ids=list(range(8)), trace=True)
        print("EXEC_NS", res.exec_time_ns, res.mean_exec_time_ns, res.max_exec_time_core_id)
        _NC_CACHE["res"] = res
    else:
        res = run_bass_kernel_spmd(_NC_CACHE["nc"], in_maps, core_ids=list(range(8)))
    outs = [np.asarray(r["yout"], dtype=f32) for r in res.results]
    y_prompt = np.stack(outs[:4], axis=0)
    y_sample = np.concatenate(outs[4:], axis=0)[None]
    return (y_prompt, y_sample)
```
